# Optimizing a Trainium2 kernel written in Bass

```python
import jax, jax.numpy as jnp
from jax import lax
import numpy as np


D_MODEL = 1024
BATCH = 16
SEQ = 2048
DEPTH = 4

GRID_W = 64
CTX_LEN = 256
N_MIXERS = 4
EPS = 1e-6
ROPE_THETA = 10000.0
NEG_INF = -1e30
Q_BLOCK = 128
HEAD_DIM = 64
D_FF = -(-(8 * D_MODEL) // (3 * 256)) * 256
SWA_HEADS = D_MODEL // HEAD_DIM
SWA_KV_HEADS = 4
SWA_WINDOW = 128
NA_HEADS = D_MODEL // HEAD_DIM
NA_KH = 8
NA_KW = 16
NA_QCB = 16
NA_SLAB = 2 * NA_KW
POOL_WINDOWS = (2, 4, 8, 16)
POOL_GROUPS = len(POOL_WINDOWS)
POOL_DG = D_MODEL // POOL_GROUPS
MLA_HEADS = 16
MLA_NOPE = 64
MLA_ROPE = 32
MLA_V = 64
MLA_Q_RANK = 256
MLA_KV_RANK = 128

kernel_name = 'hybrid_interleaved_diffusion_trunk'


def n_layers_of(m):
    return len(range(m, DEPTH, N_MIXERS))


def rms_norm(x, g):
    xf = x.astype(jnp.float32)
    y = xf * lax.rsqrt(jnp.mean(xf * xf, axis=-1, keepdims=True) + EPS)
    return (y * g.astype(jnp.float32)).astype(x.dtype)


def modulate(x, g, shift, scale):
    return rms_norm(x, g) * (1 + scale) + shift


def rope_1d(x, pos):
    half = x.shape[-1] // 2
    freqs = ROPE_THETA ** (-jnp.arange(half, dtype=jnp.float32) / half)
    ang = pos.astype(jnp.float32)[:, None] * freqs[None, :]
    cos = jnp.cos(ang)[:, None, :]
    sin = jnp.sin(ang)[:, None, :]
    xf = x.astype(jnp.float32)
    x1, x2 = xf[..., :half], xf[..., half:]
    return jnp.concatenate([x1 * cos - x2 * sin, x1 * sin + x2 * cos], -1).astype(x.dtype)


def rope_2d_axial(x, length):
    t = jnp.arange(length)
    d = x.shape[-1] // 2
    return jnp.concatenate([rope_1d(x[..., :d], t // GRID_W), rope_1d(x[..., d:], t % GRID_W)], -1)


def softmax_f32(s, sink=None):
    if sink is None:
        return jax.nn.softmax(s, axis=-1)
    sink = jnp.broadcast_to(sink, s.shape[:-1] + (1,))
    return jax.nn.softmax(jnp.concatenate([s, sink], -1), axis=-1)[..., :-1]


def swiglu(h, w_gu, w_down):
    gu = h @ w_gu
    return (jax.nn.silu(gu[..., :D_FF]) * gu[..., D_FF:]) @ w_down


def swa_mixer(a_lat, a_ctx, w_qkv, g_q, g_k, sink, w_o, need_ctx):
    B, L, _ = a_lat.shape
    Lc = a_ctx.shape[1]
    Hq, Hk, dh = SWA_HEADS, SWA_KV_HEADS, HEAD_DIM
    G = Hq // Hk
    nq, nk = Hq * dh, Hk * dh
    scale = dh ** -0.5
    qkv = a_lat @ w_qkv
    q = rope_2d_axial(rms_norm(qkv[..., :nq].reshape(B, L, Hq, dh), g_q), L)
    k = rope_2d_axial(rms_norm(qkv[..., nq:nq + nk].reshape(B, L, Hk, dh), g_k), L)
    v = qkv[..., nq + nk:].reshape(B, L, Hk, dh)
    kv_c = a_ctx @ w_qkv[:, nq:]
    k_c = rms_norm(kv_c[..., :nk].reshape(B, Lc, Hk, dh), g_k)
    v_c = kv_c[..., nk:].reshape(B, Lc, Hk, dh)
    sink_g = sink.astype(jnp.float32).reshape(Hk, G)[None, :, :, None, None]
    nb = L // Q_BLOCK
    span = Q_BLOCK + 2 * SWA_WINDOW
    pad = ((0, 0), (SWA_WINDOW, SWA_WINDOW), (0, 0), (0, 0))
    k_p = jnp.pad(k, pad)
    v_p = jnp.pad(v, pad)
    q_blocks = jnp.moveaxis(q.reshape(B, nb, Q_BLOCK, Hk, G, dh), 1, 0)
    ctx_valid = jnp.ones((Q_BLOCK, Lc), bool)

    def block(args):
        b, q_b = args
        start = b * Q_BLOCK
        k_b = jnp.concatenate([lax.dynamic_slice_in_dim(k_p, start, span, 1), k_c], 1)
        v_b = jnp.concatenate([lax.dynamic_slice_in_dim(v_p, start, span, 1), v_c], 1)
        q_pos = start + jnp.arange(Q_BLOCK)
        k_pos = start - SWA_WINDOW + jnp.arange(span)
        valid = (jnp.abs(q_pos[:, None] - k_pos[None, :]) <= SWA_WINDOW) & (k_pos >= 0) & (k_pos < L)
        valid = jnp.concatenate([valid, ctx_valid], 1)
        s = jnp.einsum('bqhgd,bkhd->bhgqk', q_b, k_b, preferred_element_type=jnp.float32) * scale
        p = softmax_f32(jnp.where(valid, s, NEG_INF), sink_g).astype(v_b.dtype)
        return jnp.einsum('bhgqk,bkhd->bqhgd', p, v_b)

    o = lax.map(block, (jnp.arange(nb), q_blocks))
    y_lat = jnp.moveaxis(o, 0, 1).reshape(B, L, nq) @ w_o
    y_ctx = None
    if need_ctx:
        q_c = rms_norm((a_ctx @ w_qkv[:, :nq]).reshape(B, Lc, Hk, G, dh), g_q)
        s = jnp.einsum('bqhgd,bkhd->bhgqk', q_c, k_c, preferred_element_type=jnp.float32) * scale
        p = softmax_f32(s, sink_g).astype(v_c.dtype)
        y_ctx = jnp.einsum('bhgqk,bkhd->bqhgd', p, v_c).reshape(B, Lc, nq) @ w_o
    return y_lat, y_ctx


def na_mixer(a_lat, a_ctx, w_qkv, g_q, g_k, rpb, w_o, need_ctx):
    B, L, _ = a_lat.shape
    Lc = a_ctx.shape[1]
    H, dh = NA_HEADS, HEAD_DIM
    n = H * dh
    scale = dh ** -0.5
    rows = L // GRID_W
    kh = min(NA_KH, rows)
    qkv = a_lat @ w_qkv
    q = rms_norm(qkv[..., :n].reshape(B, rows, GRID_W, H, dh), g_q)
    k = rms_norm(qkv[..., n:2 * n].reshape(B, rows, GRID_W, H, dh), g_k)
    v = qkv[..., 2 * n:].reshape(B, rows, GRID_W, H, dh)
    kv_c = a_ctx @ w_qkv[:, n:]
    k_c = rms_norm(kv_c[..., :n].reshape(B, Lc, H, dh), g_k)
    v_c = kv_c[..., n:].reshape(B, Lc, H, dh)
    ncb = GRID_W // NA_QCB
    q_col = np.arange(GRID_W).reshape(ncb, NA_QCB)
    slab0 = [int(s) for s in np.clip(np.arange(ncb) * NA_QCB - NA_KW // 2, 0, GRID_W - NA_SLAB)]
    col_start = np.clip(q_col - NA_KW // 2, 0, GRID_W - NA_KW)
    key_col = np.asarray(slab0)[:, None] + np.arange(NA_SLAB)
    kc3 = key_col[:, None, :]
    col_valid = (kc3 >= col_start[..., None]) & (kc3 < col_start[..., None] + NA_KW)
    dc_idx = (np.clip(kc3 - q_col[..., None], -(NA_KW - 1), NA_KW - 1) + NA_KW - 1)[:, :, None, :]
    nkey = kh * NA_SLAB
    valid_lat = np.broadcast_to(col_valid[:, :, None, :], (ncb, NA_QCB, kh, NA_SLAB)).reshape(ncb, NA_QCB, nkey)
    valid = jnp.asarray(np.concatenate([valid_lat, np.ones((ncb, NA_QCB, Lc), bool)], -1))
    rpb_f = rpb.astype(jnp.float32)

    def slabs(t):
        return jnp.stack([t[:, :, s:s + NA_SLAB] for s in slab0], 1).reshape(B, ncb, nkey, H, dh)

    def row_step(args):
        r, q_r = args
        r0 = jnp.clip(r - kh // 2, 0, rows - kh)
        k_s = slabs(lax.dynamic_slice_in_dim(k, r0, kh, 1))
        v_s = slabs(lax.dynamic_slice_in_dim(v, r0, kh, 1))
        dr_idx = (r0 + jnp.arange(kh) - r + NA_KH - 1)[None, None, :, None]
        bias = rpb_f[:, dr_idx, dc_idx].reshape(H, ncb, NA_QCB, nkey)
        q_b = q_r.reshape(B, ncb, NA_QCB, H, dh)
        s_lat = jnp.einsum('bjqhd,bjkhd->bhjqk', q_b, k_s, preferred_element_type=jnp.float32) * scale + bias
        s_ctx = jnp.einsum('bjqhd,bkhd->bhjqk', q_b, k_c, preferred_element_type=jnp.float32) * scale
        p = softmax_f32(jnp.where(valid, jnp.concatenate([s_lat, s_ctx], -1), NEG_INF)).astype(v.dtype)
        o = (jnp.einsum('bhjqk,bjkhd->bjqhd', p[..., :nkey], v_s)
             + jnp.einsum('bhjqk,bkhd->bjqhd', p[..., nkey:], v_c))
        return o.reshape(B, GRID_W, H, dh)

    o = lax.map(row_step, (jnp.arange(rows), jnp.moveaxis(q, 1, 0)))
    y_lat = jnp.moveaxis(o, 0, 1).reshape(B, L, n) @ w_o
    y_ctx = None
    if need_ctx:
        q_c = rms_norm((a_ctx @ w_qkv[:, :n]).reshape(B, Lc, H, dh), g_q)
        s = jnp.einsum('bqhd,bkhd->bhqk', q_c, k_c, preferred_element_type=jnp.float32) * scale
        p = softmax_f32(s).astype(v_c.dtype)
        y_ctx = jnp.einsum('bhqk,bkhd->bqhd', p, v_c).reshape(B, Lc, n) @ w_o
    return y_lat, y_ctx


def multi_pool(h):
    B, L, _ = h.shape
    hf = h.astype(jnp.float32)
    cs = jnp.pad(lax.cumsum(hf, axis=1), ((0, 0), (1, 0), (0, 0)))
    t = jnp.arange(L)
    outs = []
    for g, w in enumerate(POOL_WINDOWS):
        lo = jnp.clip(t - w // 2, 0, L)
        hi = jnp.clip(t - w // 2 + w, 0, L)
        cg = cs[..., g * POOL_DG:(g + 1) * POOL_DG]
        outs.append((cg[:, hi] - cg[:, lo]) / (hi - lo).astype(jnp.float32)[None, :, None])
    return (jnp.concatenate(outs, -1) - hf).astype(h.dtype)


def pool_mixer(a_lat, a_ctx, w, b, scale, need_ctx):
    def mix(h):
        B, L, _ = h.shape
        p = multi_pool(h).reshape(B, L, POOL_GROUPS, POOL_DG)
        y = jnp.einsum('blgc,gcd->blgd', p, w) + b
        return y.reshape(B, L, D_MODEL) * scale
    return mix(a_lat), (mix(a_ctx) if need_ctx else None)


def mla_mixer(a_lat, a_ctx, w_a, g_cq, g_ckv, w_uq, w_ukv, g_q, g_k, w_o, need_ctx):
    H, dn, dr, dv = MLA_HEADS, MLA_NOPE, MLA_ROPE, MLA_V
    dqk = dn + dr
    qr = MLA_Q_RANK
    scale = dqk ** -0.5

    def queries(cq_raw):
        B_, L_, _ = cq_raw.shape
        c_q = rms_norm(cq_raw, g_cq)
        return rms_norm((c_q @ w_uq).reshape(B_, L_, H, dqk), g_q)

    def keys_values(kv_raw):
        B_, L_, _ = kv_raw.shape
        c_kv = rms_norm(kv_raw[..., :MLA_KV_RANK], g_ckv)
        k_r = jnp.broadcast_to(kv_raw[..., MLA_KV_RANK:][:, :, None, :], (B_, L_, H, dr))
        kv = (c_kv @ w_ukv).reshape(B_, L_, H, dn + dv)
        k = rms_norm(jnp.concatenate([kv[..., :dn], k_r], -1), g_k)
        return k, kv[..., dn:]

    def rope_part(t):
        return jnp.concatenate([t[..., :dn], rope_2d_axial(t[..., dn:], t.shape[1])], -1)

    B, L, _ = a_lat.shape
    Lc = a_ctx.shape[1]
    proj = a_lat @ w_a
    q = rope_part(queries(proj[..., :qr]))
    k, v = keys_values(proj[..., qr:])
    k = rope_part(k)
    if need_ctx:
        proj_c = a_ctx @ w_a
        q_c = queries(proj_c[..., :qr])
        kv_raw_c = proj_c[..., qr:]
    else:
        kv_raw_c = a_ctx @ w_a[:, qr:]
    k_c, v_c = keys_values(kv_raw_c)
    k_all = jnp.concatenate([k, k_c], 1)
    v_all = jnp.concatenate([v, v_c], 1)
    nb = L // Q_BLOCK
    q_blocks = jnp.moveaxis(q.reshape(B, nb, Q_BLOCK, H, dqk), 1, 0)

    def block(q_b):
        s = jnp.einsum('bqhd,bkhd->bhqk', q_b, k_all, preferred_element_type=jnp.float32) * scale
        p = softmax_f32(s).astype(v_all.dtype)
        return jnp.einsum('bhqk,bkhd->bqhd', p, v_all)

    o = lax.map(block, q_blocks)
    y_lat = jnp.moveaxis(o, 0, 1).reshape(B, L, H * dv) @ w_o
    y_ctx = None
    if need_ctx:
        s = jnp.einsum('bqhd,bkhd->bhqk', q_c, k_c, preferred_element_type=jnp.float32) * scale
        p = softmax_f32(s).astype(v_c.dtype)
        y_ctx = jnp.einsum('bhqk,bkhd->bqhd', p, v_c).reshape(B, Lc, H * dv) @ w_o
    return y_lat, y_ctx


def setup_inputs(seed: int = 0) -> dict:
    key = jax.random.key(seed)
    ks = iter(jax.random.split(key, 48))
    f32 = jnp.float32

    def nrm(shape, fan_in, gain=1.0):
        return jax.random.normal(next(ks), shape, f32) * (gain * fan_in ** -0.5)

    def gains(shape):
        return 1.0 + 0.05 * jax.random.normal(next(ks), shape, f32)

    def small(shape, s=0.01):
        return s * jax.random.normal(next(ks), shape, f32)

    D = D_MODEL
    nA, nB, nC, nD = (n_layers_of(m) for m in range(N_MIXERS))
    dqk = MLA_NOPE + MLA_ROPE
    return {
        'x': jax.random.normal(next(ks), (BATCH, SEQ, D), f32),
        'c': jax.random.normal(next(ks), (BATCH, D), f32),
        'ctx': jax.random.normal(next(ks), (BATCH, CTX_LEN, D), f32),
        'c_ctx': jax.random.normal(next(ks), (D,), f32),
        'w_ada': nrm((DEPTH, D, 6 * D), D, 0.5),
        'b_ada': small((DEPTH, 6 * D)),
        'g_mix': gains((DEPTH, D)),
        'g_ffn': gains((DEPTH, D)),
        'w_gate_up': nrm((DEPTH, D, 2 * D_FF), D),
        'w_down': nrm((DEPTH, D_FF, D), D_FF),
        'swa_w_qkv': nrm((nA, D, (SWA_HEADS + 2 * SWA_KV_HEADS) * HEAD_DIM), D),
        'swa_g_q': gains((nA, HEAD_DIM)),
        'swa_g_k': gains((nA, HEAD_DIM)),
        'swa_sink': small((nA, SWA_HEADS), 0.5),
        'swa_w_o': nrm((nA, SWA_HEADS * HEAD_DIM, D), SWA_HEADS * HEAD_DIM),
        'na_w_qkv': nrm((nB, D, 3 * NA_HEADS * HEAD_DIM), D),
        'na_g_q': gains((nB, HEAD_DIM)),
        'na_g_k': gains((nB, HEAD_DIM)),
        'na_rpb': small((nB, NA_HEADS, 2 * NA_KH - 1, 2 * NA_KW - 1), 0.1),
        'na_w_o': nrm((nB, NA_HEADS * HEAD_DIM, D), NA_HEADS * HEAD_DIM),
        'pool_w': nrm((nC, POOL_GROUPS, POOL_DG, POOL_DG), POOL_DG),
        'pool_b': small((nC, POOL_GROUPS, POOL_DG)),
        'pool_scale': gains((nC, D)),
        'mla_w_a': nrm((nD, D, MLA_Q_RANK + MLA_KV_RANK + MLA_ROPE), D),
        'mla_g_cq': gains((nD, MLA_Q_RANK)),
        'mla_g_ckv': gains((nD, MLA_KV_RANK)),
        'mla_w_uq': nrm((nD, MLA_Q_RANK, MLA_HEADS * dqk), MLA_Q_RANK),
        'mla_w_ukv': nrm((nD, MLA_KV_RANK, MLA_HEADS * (MLA_NOPE + MLA_V)), MLA_KV_RANK),
        'mla_g_q': gains((nD, dqk)),
        'mla_g_k': gains((nD, dqk)),
        'mla_w_o': nrm((nD, MLA_HEADS * MLA_V, D), MLA_HEADS * MLA_V),
    }


def reference(x, c, ctx, c_ctx, w_ada, b_ada, g_mix, g_ffn, w_gate_up, w_down,
              swa_w_qkv, swa_g_q, swa_g_k, swa_sink, swa_w_o,
              na_w_qkv, na_g_q, na_g_k, na_rpb, na_w_o,
              pool_w, pool_b, pool_scale,
              mla_w_a, mla_g_cq, mla_g_ckv, mla_w_uq, mla_w_ukv, mla_g_q, mla_g_k, mla_w_o):
    cond = jax.nn.silu(c)
    cond_c = jax.nn.silu(c_ctx)
    h, hc = x, ctx
    for i in range(DEPTH):
        m, j = i % N_MIXERS, i // N_MIXERS
        need_ctx = i < DEPTH - 1
        mod = (cond @ w_ada[i] + b_ada[i])[:, None, :]
        mod_c = cond_c @ w_ada[i] + b_ada[i]
        sh1, sc1, gt1, sh2, sc2, gt2 = jnp.split(mod, 6, axis=-1)
        sh1c, sc1c, gt1c, sh2c, sc2c, gt2c = jnp.split(mod_c, 6, axis=-1)
        a = modulate(h, g_mix[i], sh1, sc1)
        ac = modulate(hc, g_mix[i], sh1c, sc1c)
        if m == 0:
            y, yc = swa_mixer(a, ac, swa_w_qkv[j], swa_g_q[j], swa_g_k[j], swa_sink[j], swa_w_o[j], need_ctx)
        elif m == 1:
            y, yc = na_mixer(a, ac, na_w_qkv[j], na_g_q[j], na_g_k[j], na_rpb[j], na_w_o[j], need_ctx)
        elif m == 2:
            y, yc = pool_mixer(a, ac, pool_w[j], pool_b[j], pool_scale[j], need_ctx)
        else:
            y, yc = mla_mixer(a, ac, mla_w_a[j], mla_g_cq[j], mla_g_ckv[j], mla_w_uq[j], mla_w_ukv[j],
                              mla_g_q[j], mla_g_k[j], mla_w_o[j], need_ctx)
        h = h + gt1 * y
        h = h + gt2 * swiglu(modulate(h, g_ffn[i], sh2, sc2), w_gate_up[i], w_down[i])
        if need_ctx:
            hc = hc + gt1c * yc
            hc = hc + gt2c * swiglu(modulate(hc, g_ffn[i], sh2c, sc2c), w_gate_up[i], w_down[i])
    return h
```

```python
import numpy as np
from contextlib import ExitStack
import concourse.bass as bass
import concourse.mybir as mybir
from concourse.bass_utils import run_bass_kernel_spmd

F32 = mybir.dt.float32
BF16 = mybir.dt.bfloat16
AF = mybir.ActivationFunctionType
ALU = mybir.AluOpType

D = 1024
L = 2048
LC = 256
NT = L + LC
DFF = 2816
NFC = DFF // 128
EPS = 1e-6
TL = [(0, 512), (512, 512), (1024, 512), (1536, 512), (2048, 256)]
NCORES = 8
NBPC = 2
WARM_SWA = 256
WARM_NA = 256
WARM_MLA = 320


class Buf:
    __slots__ = ("w", "r", "name")

    def __init__(self, name):
        self.name = name
        self.w = None
        self.r = {}


class Sched:
    ENGS = ("pe", "act", "dve", "pool", "sp")

    def __init__(self, nc, es):
        self.nc = nc
        self.es = es
        self.prog = {e: [] for e in self.ENGS}
        self.sem = {e: es.enter_context(nc.semaphore("s_" + e)) for e in self.ENGS}
        self.cnt = {e: 0 for e in self.ENGS}
        self.seen = {e: {} for e in self.ENGS}
        self.dsem = {}
        self.dcnt = {}
        self.bufs = {}
        self.ninst = 0

    def B(self, *key):
        b = self.bufs.get(key)
        if b is None:
            b = self.bufs[key] = Buf(key)
        return b

    def _semof(self, key):
        return self.sem[key] if key in self.sem else self.dsem[key]

    def _wait(self, eng, key, count, same_ok=False):
        if key == eng and same_ok:
            return
        if self.seen[eng].get(key, 0) >= count:
            return
        self.seen[eng][key] = count
        s = self._semof(key)
        self.prog[eng].append(lambda e, s=s, c=count: e.wait_ge(s, c))

    def op(self, eng, fn, reads=(), writes=(), inc=True):
        for b in reads:
            if b.w is not None:
                self._wait(eng, b.w[0], b.w[1], same_ok=(eng == "pe"))
        for b in writes:
            if b.w is not None:
                self._wait(eng, b.w[0], b.w[1], same_ok=(eng == "pe"))
            for k, c in b.r.items():
                self._wait(eng, k, c, same_ok=(eng == "pe"))
        c = self.cnt[eng] + 1
        if inc:
            self.cnt[eng] = c
            s = self.sem[eng]
            self.prog[eng].append(lambda e, fn=fn, s=s: fn(e).then_inc(s, 1))
        else:
            self.prog[eng].append(lambda e, fn=fn: fn(e))
        for b in reads:
            if b.r.get(eng, 0) < c:
                b.r[eng] = c
        for b in writes:
            b.w = (eng, c)
            b.r = {}
        self.ninst += 1

    def dma(self, q, out, in_, reads=(), writes=(), key=None):
        if key not in self.dsem:
            self.dsem[key] = self.es.enter_context(self.nc.semaphore("d_%d" % len(self.dsem)))
            self.dcnt[key] = 0
        for b in reads:
            if b.w is not None:
                self._wait(q, b.w[0], b.w[1])
        for b in writes:
            if b.w is not None:
                self._wait(q, b.w[0], b.w[1])
            for k, c in b.r.items():
                self._wait(q, k, c)
        self.dcnt[key] += 16
        c = self.dcnt[key]
        s = self.dsem[key]
        self.prog[q].append(lambda e, s=s, out=out, in_=in_: e.dma_start(out=out, in_=in_).then_inc(s, 16))
        for b in reads:
            if b.r.get(key, 0) < c:
                b.r[key] = c
        for b in writes:
            b.w = (key, c)
            b.r = {}
        self.ninst += 1

    def barrier(self):
        snap = dict(self.cnt)
        dsnap = dict(self.dcnt)
        for e in self.ENGS:
            for k, c in snap.items():
                if k != e and c > 0:
                    self._wait(e, k, c)
            for k, c in dsnap.items():
                if c > 0:
                    self._wait(e, k, c)
        for b in self.bufs.values():
            b.w = None
            b.r = {}

    def finish(self):
        self.barrier()
        nc = self.nc
        with nc.Block() as block:
            @block.tensor
            def _(e):
                for f in self.prog["pe"]:
                    f(e)

            @block.scalar
            def _(e):
                for f in self.prog["act"]:
                    f(e)

            @block.vector
            def _(e):
                for f in self.prog["dve"]:
                    f(e)

            @block.gpsimd
            def _(e):
                for f in self.prog["pool"]:
                    f(e)

            @block.sync
            def _(e):
                for f in self.prog["sp"]:
                    f(e)


def _rope_tables(npart, blocks):
    t = np.arange(L)
    cos = np.ones((npart, L), np.float64)
    ssin = np.zeros((npart, L), np.float64)
    perm = np.zeros((npart, npart), np.float32)
    for (s0, nd, kind) in blocks:
        half = nd // 2
        pos = (t // 64) if kind == "row" else (t % 64)
        freqs = 10000.0 ** (-np.arange(half, dtype=np.float32).astype(np.float64) / half)
        ang = pos[None, :].astype(np.float64) * freqs[:, None]
        ang = (pos[None, :].astype(np.float32) * freqs.astype(np.float32)[:, None]).astype(np.float64)
        for i in range(half):
            d1, d2 = s0 + i, s0 + half + i
            cos[d1] = np.cos(ang[i]); cos[d2] = np.cos(ang[i])
            ssin[d1] = -np.sin(ang[i]); ssin[d2] = np.sin(ang[i])
            perm[d2, d1] = 1.0
            perm[d1, d2] = 1.0
    return cos.astype(np.float32), ssin.astype(np.float32), perm


def _vec_layout():
    off = {}
    n = 0
    def add(name, w):
        nonlocal n
        off[name] = n
        n += w
    add("b_ada", 4 * 48 * 3)
    add("g_mix", 32)
    add("g_ffn", 32)
    add("pool_scale", 8)
    add("pool_b", 8)
    add("swa_gq", 1); add("swa_gk", 1); add("swa_sink", 16)
    add("na_gq", 1); add("na_gk", 1)
    add("mla_gcq", 2); add("mla_gckv", 1); add("mla_gq", 1); add("mla_gk", 1)
    add("eps", 1); add("zero", 1)
    add("cT", 8 * 3)
    return off, n


VOFF, NV = _vec_layout()
SWA_QHEADS = [(8 * (c // 4) + (c % 4), 8 * (c // 4) + 4 + (c % 4)) for c in range(8)]


def _fm(v):
    return np.ascontiguousarray(v.reshape(-1, 128).T)


def host_shared(inp):
    f = np.float32
    sh = {}
    vecs = np.zeros((128, NV), f)
    def put(name, arr):
        arr = np.asarray(arr, f)
        vecs[:, VOFF[name]:VOFF[name] + arr.shape[1]] = arr
    put("b_ada", np.repeat(np.concatenate([_fm(inp["b_ada"][i]) for i in range(4)], 1), 3, axis=1))
    put("g_mix", np.concatenate([_fm(inp["g_mix"][i]) for i in range(4)], 1))
    put("g_ffn", np.concatenate([_fm(inp["g_ffn"][i]) for i in range(4)], 1))
    put("pool_scale", _fm(inp["pool_scale"][0]))
    put("pool_b", _fm(inp["pool_b"][0].reshape(-1)))
    put("swa_gq", np.tile(inp["swa_g_q"][0], 2)[:, None])
    put("swa_gk", np.tile(inp["swa_g_k"][0], 2)[:, None])
    put("swa_sink", np.broadcast_to(inp["swa_sink"][0][None, :], (128, 16)))
    put("na_gq", np.tile(inp["na_g_q"][0], 2)[:, None])
    put("na_gk", np.tile(inp["na_g_k"][0], 2)[:, None])
    put("mla_gcq", _fm(inp["mla_g_cq"][0]))
    put("mla_gckv", inp["mla_g_ckv"][0][:, None])
    gq = np.zeros((128, 1), f); gq[:96, 0] = inp["mla_g_q"][0]
    gk = np.zeros((128, 1), f); gk[:96, 0] = inp["mla_g_k"][0]
    put("mla_gq", gq); put("mla_gk", gk)
    vecs[:, VOFF["eps"]] = EPS
    sh["vecs_shared"] = vecs
    for k in ("w_ada", "w_gate_up", "w_down"):
        sh[k] = np.ascontiguousarray(inp[k], f)
    wqkv = inp["swa_w_qkv"][0]
    qcols = np.concatenate([np.arange(h * 64, h * 64 + 64) for pr in SWA_QHEADS for h in pr])
    sh["swa_wq"] = np.ascontiguousarray(wqkv[:, qcols])
    sh["swa_wk"] = np.ascontiguousarray(wqkv[:, 1024:1280])
    sh["swa_wv"] = np.ascontiguousarray(wqkv[:, 1280:1536])
    sh["swa_wo"] = np.ascontiguousarray(inp["swa_w_o"][0][qcols, :])
    c64, s64, p64 = _rope_tables(64, [(0, 32, "row"), (32, 32, "col")])
    sh["rope64_cos"] = np.concatenate([c64, c64], 0)
    sh["rope64_sin"] = np.concatenate([s64, s64], 0)
    pp = np.zeros((128, 128), f); pp[:64, :64] = p64; pp[64:, 64:] = p64
    sh["rope64_perm"] = pp
    sh["na_wqkv"] = np.ascontiguousarray(inp["na_w_qkv"][0])
    sh["na_wo"] = np.ascontiguousarray(inp["na_w_o"][0])
    sh["na_bias"] = _na_bias_tables(inp["na_rpb"][0])
    sh["pool_w"] = np.ascontiguousarray(inp["pool_w"][0])
    sh["mla_wa"] = np.ascontiguousarray(inp["mla_w_a"][0])
    sh["mla_wuq"] = np.ascontiguousarray(inp["mla_w_uq"][0])
    wukv = inp["mla_w_ukv"][0].reshape(128, 16, 128)
    wk = np.zeros((128, 16, 96), f); wk[:, :, :64] = wukv[:, :, :64]
    sh["mla_wuk"] = wk.reshape(128, 16 * 96)
    sh["mla_wuv"] = np.ascontiguousarray(wukv[:, :, 64:]).reshape(128, 1024)
    sh["mla_wo"] = np.ascontiguousarray(inp["mla_w_o"][0])
    c96, s96, p96 = _rope_tables(96, [(64, 16, "row"), (80, 16, "col")])
    sh["rope96_cos"] = c96; sh["rope96_sin"] = s96; sh["rope96_perm"] = p96
    iext = np.zeros((32, 96), f); iext[np.arange(32), 64 + np.arange(32)] = 1.0
    sh["mla_iext"] = iext
    consts = np.zeros((128, 512), f)
    consts[:, 0:128] = 1.0
    consts[:64, 128:192] = 1.0; consts[64:, 192:256] = 1.0
    consts[:96, 256:352] = 1.0
    sh["consts"] = consts
    kk = np.arange(128)[:, None]; qq = np.arange(128)[None, :]
    m_prev = ((kk + 128 - qq) <= 128).astype(f)
    m_same = np.ones((128, 128), f)
    m_next = ((qq + 128 - kk) <= 128).astype(f)
    sh["swa_mask"] = np.concatenate([m_prev, m_same, m_next], 1)
    pe = np.zeros((128, 4, 2, 2, 16), f)
    for gi, w in enumerate((2, 4, 8, 16)):
        for li, Lx in enumerate((L, LC)):
            t = np.arange(Lx)
            lo = np.clip(t - w // 2, 0, Lx); hi = np.clip(t - w // 2 + w, 0, Lx)
            inv = (1.0 / (hi - lo).astype(f)).astype(f)
            pe[:, gi, li, 0, :w // 2] = inv[:w // 2]
            if w // 2 - 1 > 0:
                pe[:, gi, li, 1, :w // 2 - 1] = inv[Lx - (w // 2 - 1):]
    sh["pool_edge"] = pe.reshape(128, -1)
    return sh


NA_NEG = -30000.0
NA_EI = 24
NA_EF = 16
NA_ET = NA_EI + NA_EF


def _na_bias_tables(rpb):
    H = 16
    tab = np.full((128, H, NA_ET, 64), NA_NEG, np.float32)
    kc = np.arange(64)[:, None]
    qc = np.arange(64)[None, :]
    cs = np.clip(qc - 8, 0, 48)
    colvalid = (kc >= cs) & (kc < cs + 16)
    dcidx = np.clip(kc - qc, -15, 15) + 15
    for half in range(2):
        for sec, (n_e, e0, lo, hi) in enumerate(((NA_EI, -4, -4, 3), (NA_EF, 0, -7, 7))):
            base = 0 if sec == 0 else NA_EI
            for idx in range(n_e):
                e = idx + e0 - half
                dr = 7 - e
                if dr < lo or dr > hi:
                    continue
                vals = rpb[:, dr + 7, :][:, dcidx]
                vals = np.where(colvalid[None], vals, NA_NEG)
                tab[half * 64:(half + 1) * 64, :, base + idx, :] = vals.transpose(1, 0, 2)
    return tab.reshape(128, H * NA_ET * 64)


def host_core(inp, core):
    b0 = core * NBPC
    d = {}
    d["xT"] = np.ascontiguousarray(inp["x"][b0:b0 + NBPC].transpose(0, 2, 1))
    d["ctxT"] = np.ascontiguousarray(inp["ctx"][b0:b0 + NBPC].transpose(0, 2, 1))
    cc = np.stack([inp["c"][b0], inp["c"][b0 + 1], inp["c_ctx"]], 1)
    d["cT"] = np.ascontiguousarray(cc.reshape(8, 128, 3).transpose(1, 0, 2)).reshape(128, 24).astype(np.float32)
    return d


def build(layers=(0, 1, 2, 3), nb=NBPC, last_layer_idx=3, parts="ABCD", dbg=None):
    nc = bass.Bass("TRN2", target_bir_lowering=False)
    dr = {}
    def din(name, shape):
        dr[name] = nc.dram_tensor(name, list(shape), F32, kind="ExternalInput").ap()
        return dr[name]
    xT = din("xT", [NBPC, D, L]); ctxT = din("ctxT", [NBPC, D, LC]); cT_d = din("cT", [128, 24])
    vecs_d = din("vecs_shared", [128, NV])
    w_ada = din("w_ada", [4, D, 6 * D]); w_gu = din("w_gate_up", [4, D, 2 * DFF]); w_dn = din("w_down", [4, DFF, D])
    swa_wq = din("swa_wq", [D, 1024]); swa_wk = din("swa_wk", [D, 256]); swa_wv = din("swa_wv", [D, 256]); swa_wo = din("swa_wo", [D, D])
    r64c = din("rope64_cos", [128, L]); r64s = din("rope64_sin", [128, L]); r64p = din("rope64_perm", [128, 128])
    na_wqkv = din("na_wqkv", [D, 3072]); na_wo = din("na_wo", [D, D]); na_bias = din("na_bias", [128, 16 * NA_ET * 64])
    pool_w = din("pool_w", [4, 256, 256])
    mla_wa = din("mla_wa", [D, 416]); mla_wuq = din("mla_wuq", [256, 1536]); mla_wuk = din("mla_wuk", [128, 16 * 96])
    mla_wuv = din("mla_wuv", [128, 1024]); mla_wo = din("mla_wo", [D, D])
    r96c = din("rope96_cos", [96, L]); r96s = din("rope96_sin", [96, L]); r96p = din("rope96_perm", [96, 96])
    iext_d = din("mla_iext", [32, 96]); consts_d = din("consts", [128, 512]); swa_mask_d = din("swa_mask", [128, 384])
    pool_edge_d = din("pool_edge", [128, 4 * 2 * 2 * 16])
    outT = nc.dram_tensor("outT", [NBPC, D, L], F32, kind="ExternalOutput").ap()
    dbgT = nc.dram_tensor("dbgT", [128, 8, NT], F32, kind="ExternalOutput").ap() if dbg else None

    es = ExitStack()
    with es:
        S = Sched(nc, es)
        B = S.B

        uid = [0]

        def sb(name, shape, dt, stack=es):
            uid[0] += 1
            return stack.enter_context(nc.sbuf_tensor("%s_%d" % (name, uid[0]), list(shape), dt))

        hT = sb("hT", [128, 8, NT], F32)
        vec = sb("vec", [128, NV], F32)
        cst = sb("cst", [128, 512], BF16)
        modT = sb("modT", [128, 4 * 48 * 3], F32)
        gsT = sb("gsT", [128, 4 * 2 * 8 * 3], F32)
        sinkE = sb("sinkE", [128, 16], F32)
        psb = [es.enter_context(nc.psum_tensor("ps%d" % i, [128, 512], F32)) for i in range(8)]
        PSB = [B("ps", i) for i in range(8)]
        rr = {"A": 0, "C": 0, "Bk": 0}

        rr["nA"] = 4

        def psA():
            i = rr["A"] % rr["nA"]; rr["A"] = (i + 1) % rr["nA"]
            return psb[i], PSB[i]

        def psP():
            return psb[3], PSB[3]

        def set_nA(n_):
            rr["nA"] = n_; rr["A"] = 0

        def psAcc():
            i = 4 + rr["Bk"]; rr["Bk"] = (rr["Bk"] + 1) % 2
            return psb[i], PSB[i]

        rr["nC"] = 2

        def psC():
            i = 6 + rr["C"] % rr["nC"]; rr["C"] = (rr["C"] + 1) % rr["nC"]
            return psb[i], PSB[i]

        def set_warm(ncols):
            rr["warm"] = ncols
            rr["nC"] = 1 if ncols else 2
            rr["C"] = 0
        rr["warm"] = 0

        ones128 = cst[:, 0:128]
        bd64 = cst[:, 128:256]
        ones96 = cst[0:96, 256:352]

        def V(name, j=0, p0=0, p1=128):
            o = VOFF[name] + j
            return vec[p0:p1, o:o + 1]

        def modcol(i, ch, col):
            o = (i * 48 + ch) * 3 + col
            return modT[:, o:o + 1]

        def gscol(i, which, k, col):
            o = ((i * 2 + which) * 8 + k) * 3 + col
            return gsT[:, o:o + 1]

        def act(out, in_, func, reads, writes, bias=None, scale=None):
            kw = {}
            if bias is not None:
                kw["bias"] = bias
            if scale is not None:
                kw["scale"] = scale
            S.op("act", lambda e: e.activation(out=out, in_=in_, func=func, **kw), reads, writes)

        def mm(out, lhsT, rhs, start, stop, reads, writes):
            S.op("pe", lambda e: e.matmul(out, lhsT, rhs, start=start, stop=stop), reads, writes, inc=True)

        def tt(eng, out, in0, in1, op, reads, writes):
            S.op(eng, lambda e: e.tensor_tensor(out=out, in0=in0, in1=in1, op=op), reads, writes)

        def ts(eng, out, in0, s1, s2, op0, op1, reads, writes):
            if op1 is None:
                S.op(eng, lambda e: e.tensor_scalar(out=out, in0=in0, scalar1=s1, scalar2=None, op0=op0), reads, writes)
            else:
                S.op(eng, lambda e: e.tensor_scalar(out=out, in0=in0, scalar1=s1, scalar2=s2, op0=op0, op1=op1), reads, writes)

        def stt(eng, out, in0, scalar, in1, op0, op1, reads, writes):
            S.op(eng, lambda e: e.scalar_tensor_tensor(out=out, in0=in0, scalar=scalar, in1=in1, op0=op0, op1=op1), reads, writes)

        def cp(eng, out, in_, reads, writes):
            if eng == "act":
                S.op(eng, lambda e: e.activation(out=out, in_=in_, func=AF.Identity), reads, writes)
            else:
                S.op(eng, lambda e: e.tensor_copy(out=out, in_=in_), reads, writes)

        def wload(dst, src, buf, key):
            S.dma("pool", dst, src, writes=[buf], key=key)

        Bvec = B("vec"); Bcst = B("cst")
        Bmod = lambda i_: B("mod", i_)
        Bgs = lambda i_: B("gs", i_)
        S.dma("sp", vec[:, 0:VOFF["cT"]], vecs_d[:, 0:VOFF["cT"]], writes=[Bvec], key="vec")
        S.dma("sp", vec[:, VOFF["cT"]:NV], cT_d, writes=[Bvec], key="vec")
        wload(cst[:], consts_d, Bcst, "cst")
        condT = sb("condT", [128, 24], BF16)
        Bcond = B("cond")
        act(condT[:], vec[:, VOFF["cT"]:VOFF["cT"] + 24], AF.Silu, [Bvec], [Bcond])
        cond3 = condT[:].rearrange("p (k c) -> p k c", c=3)
        act(sinkE[:], vec[:, VOFF["swa_sink"]:VOFF["swa_sink"] + 16], AF.Exp, [Bvec], [B("sinkE")])

        def mod_gen(i, wab):
            for cb in range(12):
                w = wab[cb % 2]; Bw = B("wab", cb % 2)
                wload(w[:], w_ada[i, :, cb * 512:(cb + 1) * 512].rearrange("(k p) n -> p k n", p=128), Bw, "wab%d" % (cb % 2))
                yield
                pt, Bp = psA()
                for oc in range(4):
                    for k in range(8):
                        mm(pt[:, oc * 3:oc * 3 + 3], w[:, k, oc * 128:(oc + 1) * 128], cond3[:, k, :], k == 0, k == 7,
                           [Bw, Bcond], [Bp])
                o = (i * 48 + cb * 4) * 3
                bb = vec[:, VOFF["b_ada"] + o: VOFF["b_ada"] + o + 12]
                tt("dve", modT[:, o:o + 12], pt[:, 0:12], bb, ALU.add, [Bp, Bvec], [Bmod(i)])
            for which, (gname, scj) in enumerate((("g_mix", 1), ("g_ffn", 4))):
                for col in range(3):
                    o_in = (i * 48 + scj * 8) * 3 + col
                    sc_view = modT[:, o_in:o_in + 22:3]
                    o_out = ((i * 2 + which) * 8) * 3 + col
                    gs_view = gsT[:, o_out:o_out + 22:3]
                    g_view = vec[:, VOFF[gname] + i * 8: VOFF[gname] + i * 8 + 8]
                    stt("dve", gs_view, sc_view, 1.0, g_view, ALU.add, ALU.mult, [Bmod(i), Bvec], [Bgs(i)])
            yield

        with ExitStack() as ps_:
            wab0 = [sb("wab%d" % j, [128, 8, 512], BF16, ps_) for j in range(2)]
            if len(layers) > 0:
                for _ in mod_gen(layers[0], wab0):
                    pass
            S.barrier()

        Bh = lambda k, t: B("h", k, t)

        def modulate(i, which, bcol, aT, Ba, scr, tiles):
            shj = 0 if which == 0 else 3
            for ti in tiles:
                t0, n = TL[ti]
                col = 2 if ti == 4 else bcol
                pt, Bp = psC()
                for k in range(8):
                    sq = scr["sq"][k % 2]; Bsq = B("sq", k % 2)
                    tt("pool", sq[:, :n], hT[:, k, t0:t0 + n], hT[:, k, t0:t0 + n], ALU.mult, [Bh(k, ti)], [Bsq])
                    mm(pt[:, :n], ones128, sq[:, :n], k == 0, k == 7, [Bsq, Bcst], [Bp])
                rstd = scr["rstd"]; Br = B("rstd")
                act(rstd[:, :n], pt[:, :n], AF.Ln, [Bp, Bvec], [Br], bias=V("eps"), scale=1.0 / D)
                act(rstd[:, :n], rstd[:, :n], AF.Exp, [Br], [Br], scale=-0.5)
                for k in range(8):
                    tmp = scr["tmp"][k % 2]; Bt = B("mtmp", k % 2)
                    tt("dve", tmp[:, :n], hT[:, k, t0:t0 + n], rstd[:, :n], ALU.mult, [Bh(k, ti), Br], [Bt])
                    act(aT[:, k, t0:t0 + n], tmp[:, :n], AF.Identity, [Bt, Bgs(i), Bmod(i)], [Ba(k, ti)],
                        bias=modcol(i, shj * 8 + k, col), scale=gscol(i, which, k, col))

        def ffn(i, bcol, aT, Ba, st, tiles, hook=None):
            G = 2
            wg = [sb("ffn_wg%d" % j, [128, 8, 2 * G * 128], BF16, st) for j in range(2)]
            wd = [sb("ffn_wd%d" % j, [128, G, D], BF16, st) for j in range(2)]
            actb = [sb("ffn_act%d" % j, [128, G, 512], BF16, st) for j in range(2)]
            sgb = [sb("ffn_sg%d" % j, [128, 512], BF16, st) for j in range(2)]
            ngroups = NFC // G
            na = 0
            for g in range(ngroups):
                s = g % 2
                Bwg = B("ffn_wg", s); Bwd = B("ffn_wd", s)
                f0 = g * G * 128
                wload(wg[s][:, :, 0:G * 128], w_gu[i, :, f0:f0 + G * 128].rearrange("(k p) n -> p k n", p=128), Bwg, "ffn_wg%d" % s)
                wload(wg[s][:, :, G * 128:2 * G * 128], w_gu[i, :, DFF + f0:DFF + f0 + G * 128].rearrange("(k p) n -> p k n", p=128), Bwg, "ffn_wg%d" % s)
                wload(wd[s][:], w_dn[i, f0:f0 + G * 128, :].rearrange("(g p) n -> p g n", p=128), Bwd, "ffn_wd%d" % s)
                if hook is not None:
                    hook()
                    if g % 4 == 3:
                        hook()
                for ti in tiles:
                    t0, n = TL[ti]
                    col = 2 if ti == 4 else bcol
                    ab = actb[na % 2]; Bab = B("ffn_act", na % 2); na += 1
                    for fi in range(G):
                        pg, Bpg = psA()
                        for k in range(8):
                            mm(pg[:, :n], wg[s][:, k, fi * 128:(fi + 1) * 128], aT[:, k, t0:t0 + n], k == 0, k == 7, [Bwg, Ba(k, ti)], [Bpg])
                        pu, Bpu = psA()
                        for k in range(8):
                            mm(pu[:, :n], wg[s][:, k, (G + fi) * 128:(G + fi + 1) * 128], aT[:, k, t0:t0 + n], k == 0, k == 7, [Bwg, Ba(k, ti)], [Bpu])
                        sg = sgb[fi % 2]; Bsg = B("ffn_sg", fi % 2)
                        act(sg[:, :n], pg[:, :n], AF.Silu, [Bpg], [Bsg])
                        tt("dve", ab[:, fi, :n], pu[:, :n], sg[:, :n], ALU.mult, [Bpu, Bsg], [Bab])
                    for o in range(8):
                        po, Bpo = psAcc() if o % 2 == 0 else psC()
                        for fi in range(G):
                            mm(po[:, :n], wd[s][:, fi, o * 128:(o + 1) * 128], ab[:, fi, :n], fi == 0, fi == G - 1, [Bwd, Bab], [Bpo])
                        stt("dve", hT[:, o, t0:t0 + n], po[:, :n], modcol(i, 5 * 8 + o, col), hT[:, o, t0:t0 + n], ALU.mult, ALU.add,
                            [Bpo, Bmod(i), Bh(o, ti)], [Bh(o, ti)])

        def out_proj(i, bcol, oT, Bo, w_dram, st, tiles, name):
            wo = sb(name + "_wo", [128, 8, D], BF16, st)
            Bwo = B(name + "_wo")
            for half in range(2):
                wload(wo[:, half * 4:(half + 1) * 4, :], w_dram[half * 512:(half + 1) * 512, :].rearrange("(k p) n -> p k n", p=128), Bwo, name + "_wo")
            for ti in tiles:
                t0, n = TL[ti]
                col = 2 if ti == 4 else bcol
                for o in range(8):
                    po, Bpo = psA()
                    for k in range(8):
                        mm(po[:, :n], wo[:, k, o * 128:(o + 1) * 128], oT[:, k, t0:t0 + n], k == 0, k == 7, [Bwo] + Bo(k, ti), [Bpo])
                    stt("dve", hT[:, o, t0:t0 + n], po[:, :n], modcol(i, 2 * 8 + o, col), hT[:, o, t0:t0 + n], ALU.mult, ALU.add,
                        [Bpo, Bmod(i), Bh(o, ti)], [Bh(o, ti)])

        def headnorm_gen(out_ap, Bout, praw, Bpraw, M, n, ones_lhsT, invD, gain, scr, rope=None):
            sq = scr["hsq"]; Bsq = B("hsq")
            if "hraw" in scr:
                hr = scr["hraw"]; Bhr = B("hraw")
                cp("dve", hr[0:M, :n], praw, [Bpraw], [Bhr])
                tt("pool", sq[0:M, :n], hr[0:M, :n], hr[0:M, :n], ALU.mult, [Bhr], [Bsq])
            else:
                act(sq[0:M, :n], praw, AF.Square, [Bpraw], [Bsq])
            yield
            ps2, Bps2 = psC()
            mm(ps2[0:M, :n], ones_lhsT, sq[0:M, :n], True, True, [Bsq, Bcst], [Bps2])
            rs = scr["hrs"]; Brs = B("hrs")
            act(rs[0:M, :n], ps2[0:M, :n], AF.Ln, [Bps2, Bvec], [Brs], bias=V("eps", 0, 0, M), scale=invD)
            act(rs[0:M, :n], rs[0:M, :n], AF.Exp, [Brs], [Brs], scale=-0.5)
            if rope is None:
                stt("dve", out_ap, praw, gain, rs[0:M, :n], ALU.mult, ALU.mult, [Bpraw, Bvec, Brs], Bout)
                return
            perm, cos_ap, sin_ap, Brope = rope
            qn = scr["hqn"]; Bqn = B("hqn")
            stt("dve", qn[0:M, :n], praw, gain, rs[0:M, :n], ALU.mult, ALU.mult, [Bpraw, Bvec, Brs], [Bqn])
            yield
            ps3, Bps3 = psC()
            mm(ps3[0:M, :n], perm, qn[0:M, :n], True, True, [Bqn] + Brope, [Bps3])
            t1 = scr["ht1"]; Bt1 = B("ht1")
            tt("pool", t1[0:M, :n], qn[0:M, :n], cos_ap, ALU.mult, [Bqn] + Brope, [Bt1])
            t2 = scr["ht2"]; Bt2 = B("ht2")
            tt("dve", t2[0:M, :n], ps3[0:M, :n], sin_ap, ALU.mult, [Bps3] + Brope, [Bt2])
            tt("pool", out_ap, t1[0:M, :n], t2[0:M, :n], ALU.add, [Bt1, Bt2], Bout)

        def headnorm(*a_, **k_):
            for _ in headnorm_gen(*a_, **k_):
                pass

        class HNPipe:
            def __init__(self):
                self.g = []

            def push(self, gen):
                self.step()
                next(gen)
                self.g.append(gen)

            def step(self):
                for g in list(self.g):
                    try:
                        next(g)
                    except StopIteration:
                        self.g.remove(g)

            def drain(self):
                while self.g:
                    self.step()

        def head_scratch(st, rope=True):
            d = {"hsq": sb("hsq", [128, 512], BF16, st), "hrs": sb("hrs", [128, 512], F32, st)}
            if rope:
                d.update({"hqn": sb("hqn", [128, 512], BF16, st), "ht1": sb("ht1", [128, 512], F32, st),
                          "ht2": sb("ht2", [128, 512], F32, st)})
            return d

        pend = []
        LA = 2

        def attn_flush(keep=0):
            while len(pend) > keep:
                pend.pop(0)()

        def attend(qT_ap_fn, kT_ap_fn, Kdim, vaug_fn, units, scale, out_ap, reads_q, reads_k, reads_v, Bout, scr,
                   nq, sink_ap=None, hook=None, la=2):
            po, Bpo = psAcc()
            nu = len(units)
            for ui, (kc, c0, c1, mspec) in enumerate(units):
                pS, BpS = psA()
                mm(pS[:, 0:c1 - c0], kT_ap_fn(kc), qT_ap_fn(c0, c1), True, True, reads_q + reads_k, [BpS])
                if rr["warm"]:
                    wn = rr["warm"]
                    S.op("pe", lambda en, wn=wn: en.matmul(psb[7][:, 0:wn], ones128, cst[:, 0:wn], start=True, stop=True), [], [], inc=False)
                pi = scr["pi"]; scr["pi"] = (pi + 1) % len(scr["P"])
                Pt = scr["P"][pi]; BP = B("P", pi)
                act(Pt[:, 0:c1 - c0], pS[:, 0:c1 - c0], AF.Exp, [BpS], [BP], scale=scale)
                if mspec:
                    for (cc0, cc1, m_ap, mb) in mspec:
                        tt("dve", Pt[:, cc0 - c0:cc1 - c0], Pt[:, cc0 - c0:cc1 - c0], m_ap, ALU.mult, [BP] + mb, [BP])
                pend.append(lambda kc=kc, c0=c0, c1=c1, Pt=Pt, BP=BP, ui=ui:
                            mm(po[:, c0:c1], vaug_fn(kc), Pt[:, 0:c1 - c0], ui == 0, ui == nu - 1, [BP] + reads_v, [Bpo]))
                attn_flush(la)
                if hook is not None and ui % 2 == 1:
                    hook()

            def epilogue():
                ri = scr["ri"]; scr["ri"] = (ri + 1) % len(scr["rc"])
                rc = scr["rc"][ri]; Brc = B("rc", ri)
                if sink_ap is not None:
                    act(rc[64:128, :nq], po[64:128, :nq], AF.Ln, [Bpo, B("sinkE")], [Brc], bias=sink_ap)
                else:
                    act(rc[64:128, :nq], po[64:128, :nq], AF.Ln, [Bpo], [Brc])
                act(rc[64:128, :nq], rc[64:128, :nq], AF.Exp, [Brc], [Brc], scale=-1.0)
                tt("dve", out_ap, po[0:64, :nq], rc[64:128, :nq], ALU.mult, [Bpo, Brc], [Bout])
            pend.append(epilogue)

        def attn_scratch(st, nP=4):
            return {"P": [sb("Pt%d" % j, [128, 512], BF16, st) for j in range(nP)], "pi": 0,
                    "rc": [sb("rc%d" % j, [128, 512], F32, st) for j in range(1)], "ri": 0}

        for b in range(nb):
            for k in range(8):
                S.dma("sp", hT[:, k, 0:L], xT[b, k * 128:(k + 1) * 128, :], writes=[Bh(k, t) for t in range(4)], key="hx%d" % k)
                S.dma("sp", hT[:, k, L:NT], ctxT[b, k * 128:(k + 1) * 128, :], writes=[Bh(k, 4)], key="hc%d" % k)
            for i in layers:
                need_ctx = i < last_layer_idx
                tiles = [0, 1, 2, 3, 4] if need_ctx else [0, 1, 2, 3]
                Ba = lambda k, t: B("a", k, t)

                def mk_mscr(stk):
                    return {"sq": [sb("msq%d" % j, [128, 512], BF16, stk) for j in range(2)],
                            "rstd": sb("mrstd", [128, 512], F32, stk),
                            "tmp": [sb("mtmp%d" % j, [128, 512], F32, stk) for j in range(2)]}

                def dbg_dump(tag, src, q):
                    if dbg == tag and b == 0:
                        S.barrier()
                        S.dma(q, dbgT, src, writes=[B("dbg")], key="dbg")
                        S.barrier()

                with ExitStack() as st2:
                    if "B" in parts:
                        ENV = dict(locals())
                        ENV.update(dict(S=S, nc=nc, sb=sb, st=st2))
                        if i == 2:
                            pool_layer(ENV)
                        elif i == 0:
                            swa_layer(ENV)
                        elif i == 1:
                            na_layer(ENV)
                        else:
                            mla_layer(ENV)
                    S.barrier()
                dbg_dump("h1", hT[:], "sp")
                with ExitStack() as st3:
                    aT = sb("aT", [128, 8, NT], BF16, st3)
                    mscr = mk_mscr(st3)
                    if "C" in parts:
                        modulate(i, 1, b, aT, Ba, mscr, tiles)
                    dbg_dump("a2", aT[:], "pool")
                    if "D" in parts:
                        li = list(layers).index(i)
                        mg = None
                        if b == 0 and li + 1 < len(layers):
                            wab1 = [sb("wabf%d" % j, [128, 8, 512], BF16, st3) for j in range(2)]
                            mg = mod_gen(layers[li + 1], wab1)

                        def mhook(mg=mg):
                            if mg is not None:
                                try:
                                    next(mg)
                                except StopIteration:
                                    pass
                        ffn(i, b, aT, Ba, st3, tiles, hook=mhook if mg is not None else None)
                        if mg is not None:
                            for _ in mg:
                                pass
                    S.barrier()
            for k in range(8):
                S.dma("sp", outT[b, k * 128:(k + 1) * 128, :], hT[:, k, 0:L], reads=[Bh(k, t) for t in range(4)],
                      writes=[B("out", b, k)], key="ost%d" % k)
        S.finish()
        build.ninst = S.ninst
    return nc


class NS:
    def __init__(self, d):
        self.__dict__.update(d)


def pool_layer(E):
    S, nc, sb, st = E["S"], E["nc"], E["sb"], E["st"]
    B = S.B
    i, b, Ba, hT = E["i"], E["b"], E["Ba"], E["hT"]
    vec, V, Bvec, Bmod, modcol = E["vec"], E["V"], E["Bvec"], E["Bmod"], E["modcol"]
    act, mm, tt, ts, stt, cp, wload, psA = E["act"], E["mm"], E["tt"], E["ts"], E["stt"], E["cp"], E["wload"], E["psA"]
    need_ctx, pool_w, pool_edge_d, Bh = E["need_ctx"], E["pool_w"], E["pool_edge_d"], E["Bh"]
    aT = sb("aT", [128, 8, NT], BF16, st)
    mscr = E["mk_mscr"](st)
    if "A" in E["parts"]:
        E["modulate"](i, 0, b, aT, Ba, mscr, [0, 1, 2, 3, 4])
    E["dbg_dump"]("a1", aT[:], "pool")
    pT = sb("poolP", [128, 8, NT], BF16, st)
    Bp = lambda k, t: B("poolP", k, t)
    wp = sb("poolW", [128, 4, 2, 256], BF16, st)
    Bwp = B("poolW")
    wload(wp[:].rearrange("p g k n -> p (g k) n"), pool_w.rearrange("g (k p) n -> p (g k) n", p=128), Bwp, "poolW")
    edge = sb("poolE", [128, 4 * 2 * 2 * 16], F32, st)
    Bedge = B("poolE")
    S.dma("sp", edge[:], pool_edge_d, writes=[Bedge], key="poolE")
    PADL = 8
    WLEN = PADL + L + 16
    bufs = {}
    for en_ in ("dve", "pool"):
        bufs[en_] = (sb("poolA_" + en_, [128, WLEN], F32, st), sb("poolB_" + en_, [128, WLEN], F32, st),
                     B("poolA", en_), B("poolB", en_))
    sT = sb("poolS", [128, 8 * 3], F32, st)
    Bs = B("poolS")
    for col in range(3):
        o_in = (i * 48 + 2 * 8) * 3 + col
        tt("dve", sT[:, col:col + 22:3], E["modT"][:, o_in:o_in + 22:3], vec[:, VOFF["pool_scale"]:VOFF["pool_scale"] + 8], ALU.mult,
           [Bmod(i), Bvec], [Bs])
    segs = [(0, L, 0, [0, 1, 2, 3])] + ([(L, LC, 1, [4])] if True else [])
    for k in range(8):
        gi = k // 2
        w = (2, 4, 8, 16)[gi]
        EN = "pool" if k in (1, 3, 5) else "dve"
        bufA, bufB, BA, BB = bufs[EN]
        for (s0, Ln, li, tls) in segs:
            allB = [Ba(k, t) for t in tls]
            S.op(EN, lambda e, bufA=bufA: e.memset(bufA[:, 0:PADL], 0.0), [], [BA])
            S.op(EN, lambda e, Ln=Ln, bufA=bufA: e.memset(bufA[:, PADL + Ln:PADL + Ln + 16], 0.0), [], [BA])
            cp(EN, bufA[:, PADL:PADL + Ln], aT[:, k, s0:s0 + Ln], allB, [BA])
            src, Bsrc, dst, Bdst = bufA, BA, bufB, BB
            step = 1
            n_valid = PADL + Ln + 16
            while step < w:
                n_valid -= step
                tt(EN, dst[:, 0:n_valid], src[:, 0:n_valid], src[:, step:step + n_valid], ALU.add, [Bsrc], [Bdst])
                src, Bsrc, dst, Bdst = dst, Bdst, src, Bsrc
                step *= 2
            o0 = PADL - w // 2
            stt("dve", pT[:, k, s0:s0 + Ln], src[:, o0:o0 + Ln], 1.0 / w, aT[:, k, s0:s0 + Ln], ALU.mult, ALU.subtract, [Bsrc] + allB,
                [Bp(k, t) for t in tls])
            eo = ((gi * 2 + li) * 2) * 16
            ne = w // 2
            tt(EN, dst[:, 0:ne], src[:, o0:o0 + ne], edge[:, eo:eo + ne], ALU.mult, [Bsrc, Bedge], [Bdst])
            tt(EN, pT[:, k, s0:s0 + ne], dst[:, 0:ne], aT[:, k, s0:s0 + ne], ALU.subtract, [Bdst, Ba(k, tls[0])], [Bp(k, tls[0])])
            ne2 = w // 2 - 1
            if ne2 > 0:
                tt(EN, dst[:, 16:16 + ne2], src[:, o0 + Ln - ne2:o0 + Ln], edge[:, eo + 16:eo + 16 + ne2], ALU.mult, [Bsrc, Bedge], [Bdst])
                tt(EN, pT[:, k, s0 + Ln - ne2:s0 + Ln], dst[:, 16:16 + ne2], aT[:, k, s0 + Ln - ne2:s0 + Ln], ALU.subtract,
                   [Bdst, Ba(k, tls[-1])], [Bp(k, tls[-1])])
    tmpb = [sb("poolT%d" % j, [128, 512], F32, st) for j in range(2)]
    nn = 0
    for ti in ([0, 1, 2, 3, 4] if need_ctx else [0, 1, 2, 3]):
        t0, n = TL[ti]
        col = 2 if ti == 4 else b
        for o in range(8):
            g = o // 2
            po, Bpo = psA()
            for kk in range(2):
                mm(po[:, :n], wp[:, g, kk, (o % 2) * 128:(o % 2 + 1) * 128], pT[:, 2 * g + kk, t0:t0 + n], kk == 0, kk == 1,
                   [Bwp, Bp(2 * g + kk, ti)], [Bpo])
            tb = tmpb[nn % 2]; Btb = B("poolT", nn % 2); nn += 1
            act(tb[:, :n], po[:, :n], AF.Identity, [Bpo, Bvec], [Btb], bias=V("pool_b", o))
            stt("dve", hT[:, o, t0:t0 + n], tb[:, :n], sT[:, o * 3 + col:o * 3 + col + 1], hT[:, o, t0:t0 + n], ALU.mult, ALU.add,
                [Btb, Bs, Bh(o, ti)], [Bh(o, ti)])


def swa_layer(E):
    e = NS(E)
    S, nc, sb, st = e.S, e.nc, e.sb, e.st
    B = S.B
    i, b = e.i, e.b
    act, mm, tt, stt, cp, wload, psA = e.act, e.mm, e.tt, e.stt, e.cp, e.wload, e.psA
    qT = sb("swa_q", [128, 8, NT], BF16, st)
    kT = sb("swa_k", [128, 2, NT], BF16, st)
    vA = sb("swa_v", [128, 18, 4, 128], BF16, st)
    Bq = lambda c, hh, t: B("swa_q", c, hh, t)
    Bk = lambda m: B("swa_k", m)
    Bv = B("swa_v")
    Ba = e.Ba
    with ExitStack() as s1:
        aT = sb("aT", [128, 8, NT], BF16, s1)
        with ExitStack() as s0:
            mscr = e.mk_mscr(s0)
            e.modulate(i, 0, b, aT, Ba, mscr, [0, 1, 2, 3, 4])
            S.barrier()
        e.dbg_dump("a1", aT[:], "pool")
        hs = e.head_scratch(s1)
        wst = [sb("swa_ws%d" % j, [128, 8, 128], BF16, s1) for j in range(2)]
        ropeT = sb("swa_rope", [128, 2, L], BF16, s1)
        perm = sb("swa_perm", [128, 128], BF16, s1)
        Bperm = B("swa_perm")
        wload(perm[:], e.r64p, Bperm, "swa_perm")
        wload(ropeT[:, 0, :], e.r64c, Bperm, "swa_perm")
        wload(ropeT[:, 1, :], e.r64s, Bperm, "swa_perm")
        S.op("dve", lambda en: en.memset(vA[:, :, :, 64:128], 1.0), [], [Bv])
        nw = 0
        pipe = e.HNPipe()
        plist = [("k", 0), ("k", 1)] + [("q", c) for c in range(8)]
        for (kind, c) in plist:
            w = wst[nw % 2]; Bw = B("swa_ws", nw % 2)
            src = e.swa_wk if kind == "k" else e.swa_wq
            wload(w[:], src[:, c * 128:(c + 1) * 128].rearrange("(k p) n -> p k n", p=128), Bw, "swa_ws%d" % (nw % 2))
            nw += 1
            for ti in range(5):
                t0, n = TL[ti]
                pr, Bpr = psA()
                for k in range(8):
                    mm(pr[:, :n], w[:, k, :], aT[:, k, t0:t0 + n], k == 0, k == 7, [Bw, Ba(k, ti)], [Bpr])
                rope = None
                if ti < 4:
                    rope = (perm[:], ropeT[:, 0, t0:t0 + n], ropeT[:, 1, t0:t0 + n], [Bperm])
                if kind == "k":
                    out_ap = kT[:, c, t0:t0 + n]; Bo = [Bk(c)]; gain = e.V("swa_gk")
                else:
                    out_ap = qT[:, c, t0:t0 + n]; Bo = [Bq(c, 0, ti), Bq(c, 1, ti)]; gain = e.V("swa_gq")
                pipe.push(e.headnorm_gen(out_ap, Bo, pr[:, :n], Bpr, 128, n, e.bd64, 1.0 / 64, gain, hs, rope))
        pipe.drain()
        for half in range(2):
            w = wst[nw % 2]; Bw = B("swa_ws", nw % 2)
            wload(w[:], e.swa_wv[:, half * 128:(half + 1) * 128].rearrange("(k p) n -> p k n", p=128), Bw, "swa_ws%d" % (nw % 2))
            nw += 1
            for blk in range(18):
                ti = min(blk // 4, 4)
                pv, Bpv = psA()
                for k in range(8):
                    mm(pv[:, 0:128], aT[:, k, blk * 128:(blk + 1) * 128], w[:, k, :], k == 0, k == 7, [Bw, Ba(k, ti)], [Bpv])
                cp("act" if blk % 2 else "dve", vA[:, blk, 2 * half:2 * half + 2, 0:64], pv[:, 0:128].rearrange("p (g d) -> p g d", d=64),
                   [Bpv], [Bv])
        S.barrier()
    with ExitStack() as s2:
        ascr = e.attn_scratch(s2, 4)
        mask = sb("swa_mask", [128, 384], BF16, s2)
        Bmask = B("swa_mask")
        wload(mask[:], e.swa_mask_d, Bmask, "swa_mask")
        e.set_warm(WARM_SWA)
        for c in range(8):
            m = c // 4
            for hh in range(2):
                head = SWA_QHEADS[c][hh]
                g = 2 * m + hh
                p0, p1 = hh * 64, hh * 64 + 64
                for qt in range(5):
                    t0, nq = TL[qt]
                    units = [(16, 0, nq, None), (17, 0, nq, None)]
                    if qt < 4:
                        for rel in range(-1, 5):
                            kc = qt * 4 + rel
                            if kc < 0 or kc > 15:
                                continue
                            qlo, qhi = max(rel - 1, 0), min(rel + 1, 3)
                            c0, c1 = qlo * 128, (qhi + 1) * 128
                            mc0 = (qlo - (rel - 1)) * 128
                            units.append((kc, c0, c1, [(c0, c1, mask[:, mc0:mc0 + (c1 - c0)], [Bmask])]))
                    e.attend(lambda a0, a1, t0=t0, p0=p0, p1=p1, c=c: qT[p0:p1, c, t0 + a0:t0 + a1],
                             lambda kc, p0=p0, p1=p1, m=m: kT[p0:p1, m, kc * 128:(kc + 1) * 128], 64,
                             lambda kc, g=g: vA[:, kc, g, :], units, 0.125, qT[p0:p1, c, t0:t0 + nq],
                             [Bq(c, hh, qt)], [Bk(m)], [Bv], Bq(c, hh, qt), ascr, nq,
                             sink_ap=e.sinkE[64:128, head:head + 1], la=3)
        e.attn_flush()
        e.set_warm(0)
        S.barrier()
    e.dbg_dump("o", qT[:], "pool")
    with ExitStack() as s3:
        e.out_proj(i, b, qT, lambda k, ti: [Bq(k, 0, ti), Bq(k, 1, ti)], e.swa_wo, s3, [0, 1, 2, 3, 4], "swa")
        S.barrier()


def _na_units(m, ET, hh, BET):
    if m == 0:
        segs = [(0, 4, "F", range(0, 4)), (4, 8, "I", range(0, 6))]
    elif m == 3:
        segs = [(24, 29, "I", range(10, 16)), (29, 32, "F", range(12, 16))]
    else:
        segs = [(8 * m, 8 * m + 8, "I", range(4 * m - 2, 4 * m + 6))]
    units = []
    allk = sorted(set(k for sg in segs for k in sg[3]))
    for kc in allk:
        kr0 = 2 * kc
        ms = []
        for (ra, rb, kind, chunks) in segs:
            if kc not in chunks:
                continue
            if kind == "I":
                rlo, rhi = max(ra, kr0 - 3), min(rb - 1, kr0 + 5)
                if rlo > rhi:
                    continue
                idx0 = (rlo - kr0 + 7) + 4
                assert 0 <= idx0 and idx0 + (rhi - rlo + 1) <= NA_EI
            else:
                rlo, rhi = ra, rb - 1
                idx0 = NA_EI + (rlo - kr0 + 7)
                assert NA_EI + 1 <= idx0 and idx0 + (rhi - rlo + 1) <= NA_ET
            nr = rhi - rlo + 1
            cc0, cc1 = (rlo - 8 * m) * 64, (rhi + 1 - 8 * m) * 64
            tab = ET[:, hh, idx0:idx0 + nr, :].rearrange("p e c -> p (e c)")
            ms.append((cc0, cc1, tab, [BET]))
        if not ms:
            continue
        ms.sort(key=lambda x: x[0])
        for a_, b_ in zip(ms[:-1], ms[1:]):
            assert a_[1] == b_[0], (m, kc, ms)
        units.append((kc, ms[0][0], ms[-1][1], ms))
    return units


def na_layer(E):
    e = NS(E)
    S, nc, sb, st = e.S, e.nc, e.sb, e.st
    B = S.B
    i, b = e.i, e.b
    act, mm, tt, stt, cp, wload, psA = e.act, e.mm, e.tt, e.stt, e.cp, e.wload, e.psA
    Ba = e.Ba
    oT = sb("na_o", [128, 8, NT], BF16, st)
    Bo = lambda c, hh, t: B("na_o", c, hh, t)
    with ExitStack() as s1:
        aT = sb("aT", [128, 8, NT], BF16, s1)
        with ExitStack() as s0:
            mscr = e.mk_mscr(s0)
            e.modulate(i, 0, b, aT, Ba, mscr, [0, 1, 2, 3, 4])
            S.barrier()
        hs = e.head_scratch(s1, rope=False)
        wst = [sb("na_ws%d" % j, [128, 8, 128], BF16, s1) for j in range(3)]
        qp = sb("na_q", [128, NT], BF16, s1)
        kp = sb("na_k", [128, NT], BF16, s1)
        vA = sb("na_v", [128, 18, 2, 128], BF16, s1)
        ET = sb("na_et", [128, 2, NA_ET, 64], BF16, s1)
        ascr = e.attn_scratch(s1, 4)
        Bqp = lambda t: B("na_q", t)
        Bkp, Bv, BET = B("na_k"), B("na_v"), B("na_et")
        S.op("dve", lambda en: en.memset(vA[:, :, :, 64:128], 1.0), [], [Bv])
        for c in range(8):
            Bw = [B("na_ws", j) for j in range(3)]
            for j in range(3):
                wload(wst[j][:], e.na_wqkv[:, j * 1024 + c * 128:j * 1024 + (c + 1) * 128].rearrange("(k p) n -> p k n", p=128),
                      Bw[j], "na_ws%d" % j)
            wload(ET[:].rearrange("p a e c -> p (a e c)"), e.na_bias[:, (2 * c) * NA_ET * 64:(2 * c + 2) * NA_ET * 64], BET, "na_et")
            ETf = ET[:].rearrange("p a e c -> p (a e c)")
            for hx in range(2):
                act(ETf[:, hx * NA_ET * 64:(hx + 1) * NA_ET * 64], ETf[:, hx * NA_ET * 64:(hx + 1) * NA_ET * 64], AF.Exp, [BET], [BET])
            pipe = e.HNPipe()
            for ti in range(5):
                t0, n = TL[ti]
                for j, (dst, Bd, gname) in enumerate(((qp, [Bqp(ti)], "na_gq"), (kp, [Bkp], "na_gk"))):
                    pr, Bpr = psA()
                    for k in range(8):
                        mm(pr[:, :n], wst[j][:, k, :], aT[:, k, t0:t0 + n], k == 0, k == 7, [Bw[j], Ba(k, ti)], [Bpr])
                    pipe.push(e.headnorm_gen(dst[:, t0:t0 + n], Bd, pr[:, :n], Bpr, 128, n, e.bd64, 1.0 / 64, e.V(gname), hs, None))
            pipe.drain()
            for blk in range(18):
                ti = min(blk // 4, 4)
                pv, Bpv = psA()
                for k in range(8):
                    mm(pv[:, 0:128], aT[:, k, blk * 128:(blk + 1) * 128], wst[2][:, k, :], k == 0, k == 7, [Bw[2], Ba(k, ti)], [Bpv])
                cp("act" if blk % 2 else "dve", vA[:, blk, :, 0:64], pv[:, 0:128].rearrange("p (g d) -> p g d", d=64), [Bpv], [Bv])
            e.set_warm(WARM_NA)
            for hh in range(2):
                p0, p1 = hh * 64, hh * 64 + 64
                for qt in range(5):
                    t0, nq = TL[qt]
                    units = [(16, 0, nq, None), (17, 0, nq, None)]
                    if qt < 4:
                        units += _na_units(qt, ET, hh, BET)
                    e.attend(lambda a0, a1, t0=t0, p0=p0, p1=p1: qp[p0:p1, t0 + a0:t0 + a1],
                             lambda kc, p0=p0, p1=p1: kp[p0:p1, kc * 128:(kc + 1) * 128], 64,
                             lambda kc, hh=hh: vA[:, kc, hh, :], units, 0.125, oT[p0:p1, c, t0:t0 + nq],
                             [Bqp(qt)], [Bkp], [Bv], Bo(c, hh, qt), ascr, nq, la=3)
            e.attn_flush()
            e.set_warm(0)
        S.barrier()
    e.dbg_dump("o", oT[:], "pool")
    with ExitStack() as s3:
        e.out_proj(i, b, oT, lambda k, ti: [Bo(k, 0, ti), Bo(k, 1, ti)], e.na_wo, s3, [0, 1, 2, 3, 4], "na")
        S.barrier()


def mla_layer(E):
    e = NS(E)
    S, nc, sb, st = e.S, e.nc, e.sb, e.st
    B = S.B
    i, b = e.i, e.b
    act, mm, tt, stt, cp, wload, psA, psC = e.act, e.mm, e.tt, e.stt, e.cp, e.wload, e.psA, e.psC
    Ba = e.Ba
    aT = sb("aT", [128, 8, NT], BF16, st)
    with ExitStack() as s0:
        mscr = e.mk_mscr(s0)
        e.modulate(i, 0, b, aT, Ba, mscr, [0, 1, 2, 3, 4])
        S.barrier()
    Bo = lambda c, hh, t: B("mla_o", c, hh, t)
    with ExitStack() as s1:
        cqT = sb("mla_cq", [128, 2, L], BF16, s1)
        ckvT = sb("mla_ckv", [128, NT], BF16, s1)
        krT = sb("mla_kr", [32, NT], BF16, s1)
        hs = e.head_scratch(s1, rope=True)
        hs["hraw"] = sb("hraw", [128, 512], BF16, s1)
        Bcq = lambda t: B("mla_cq", t)
        Bckv, Bkr = B("mla_ckv"), B("mla_kr")
        with ExitStack() as sA:
            wa = sb("mla_wa", [128, 8, 416], BF16, sA)
            Bwa = B("mla_wa")
            wload(wa[:], e.mla_wa.rearrange("(k p) n -> p k n", p=128), Bwa, "mla_wa")
            for ti in range(5):
                t0, n = TL[ti]
                rd = [Bwa] + [Ba(k, ti) for k in range(8)]
                if ti < 4:
                    pq = [psA(), psA()]
                    for j in range(2):
                        for k in range(8):
                            mm(pq[j][0][:, :n], wa[:, k, j * 128:(j + 1) * 128], aT[:, k, t0:t0 + n], k == 0, k == 7, rd, [pq[j][1]])
                    ps2, Bps2 = psC()
                    sq = hs["hsq"]; Bsq = B("hsq")
                    for j in range(2):
                        act(sq[:, :n], pq[j][0][:, :n], AF.Square, [pq[j][1]], [Bsq])
                        mm(ps2[:, :n], e.ones128, sq[:, :n], j == 0, j == 1, [Bsq, e.Bcst], [Bps2])
                    rs = hs["hrs"]; Brs = B("hrs")
                    act(rs[:, :n], ps2[:, :n], AF.Ln, [Bps2, e.Bvec], [Brs], bias=e.V("eps"), scale=1.0 / 256)
                    act(rs[:, :n], rs[:, :n], AF.Exp, [Brs], [Brs], scale=-0.5)
                    for j in range(2):
                        stt("dve", cqT[:, j, t0:t0 + n], pq[j][0][:, :n], e.V("mla_gcq", j), rs[:, :n], ALU.mult, ALU.mult,
                            [pq[j][1], e.Bvec, Brs], [Bcq(ti)])
                pk, Bpk = psA()
                for k in range(8):
                    mm(pk[:, :n], wa[:, k, 256:384], aT[:, k, t0:t0 + n], k == 0, k == 7, rd, [Bpk])
                e.headnorm(ckvT[:, t0:t0 + n], [Bckv], pk[:, :n], Bpk, 128, n, e.ones128, 1.0 / 128, e.V("mla_gckv"), hs, None)
                pr, Bpr = psA()
                for k in range(8):
                    mm(pr[0:32, :n], wa[:, k, 384:416], aT[:, k, t0:t0 + n], k == 0, k == 7, rd, [Bpr])
                cp("act", krT[0:32, t0:t0 + n], pr[0:32, :n], [Bpr], [Bkr])
            S.barrier()
        with ExitStack() as sB:
            wuq = sb("mla_wuq", [128, 2, 1536], BF16, sB)
            wuk = sb("mla_wuk", [128, 16 * 96], BF16, sB)
            wuv = sb("mla_wuv", [128, 1024], BF16, sB)
            iext = sb("mla_iext", [32, 96], BF16, sB)
            perm = sb("mla_perm", [96, 96], BF16, sB)
            rcos = sb("mla_cos", [96, L], BF16, sB)
            rsin = sb("mla_sin", [96, L], BF16, sB)
            Bw = B("mla_w")
            wload(wuq[:], e.mla_wuq.rearrange("(k p) n -> p k n", p=128), Bw, "mla_w")
            wload(wuk[:], e.mla_wuk, Bw, "mla_w")
            wload(wuv[:], e.mla_wuv, Bw, "mla_w")
            wload(iext[:], e.iext_d, Bw, "mla_w")
            wload(perm[:], e.r96p, Bw, "mla_w")
            wload(rcos[:], e.r96c, Bw, "mla_w")
            wload(rsin[:], e.r96s, Bw, "mla_w")
            qh = [sb("mla_qh%d" % j, [96, L], BF16, sB) for j in range(2)]
            kh = [sb("mla_kh%d" % j, [96, NT], BF16, sB) for j in range(2)]
            vA = [sb("mla_vh%d" % j, [128, 18, 128], BF16, sB) for j in range(2)]
            ascr = e.attn_scratch(sB, 4)
            for j in range(2):
                S.op("dve", lambda en, j=j: en.memset(vA[j][:, :, 64:128], 1.0), [], [B("mla_vh", j)])
            scale = 96.0 ** -0.5
            e.set_nA(3)
            psP = e.psP

            def proj_gen(h):
                s_ = h % 2
                Bkh, Bvh = B("mla_kh", s_), B("mla_vh", s_)
                for ti in range(4):
                    t0, n = TL[ti]
                    pq, Bpq = psP()
                    for j in range(2):
                        mm(pq[0:96, :n], wuq[:, j, h * 96:(h + 1) * 96], cqT[:, j, t0:t0 + n], j == 0, j == 1, [Bw, Bcq(ti)], [Bpq])
                    for _ in e.headnorm_gen(qh[s_][:, t0:t0 + n], [B("mla_qh", s_, ti)], pq[0:96, :n], Bpq, 96, n, e.ones96, 1.0 / 96,
                                            e.V("mla_gq", 0, 0, 96), hs, (perm[:], rcos[:, t0:t0 + n], rsin[:, t0:t0 + n], [Bw])):
                        yield
                    yield
                for ti in range(5):
                    t0, n = TL[ti]
                    pk, Bpk = psP()
                    mm(pk[0:96, :n], wuk[:, h * 96:(h + 1) * 96], ckvT[:, t0:t0 + n], True, False, [Bw, Bckv], [Bpk])
                    mm(pk[0:96, :n], iext[:, :], krT[0:32, t0:t0 + n], False, True, [Bw, Bkr], [Bpk])
                    rope = (perm[:], rcos[:, t0:t0 + n], rsin[:, t0:t0 + n], [Bw]) if ti < 4 else None
                    for _ in e.headnorm_gen(kh[s_][:, t0:t0 + n], [Bkh], pk[0:96, :n], Bpk, 96, n, e.ones96, 1.0 / 96,
                                            e.V("mla_gk", 0, 0, 96), hs, rope):
                        yield
                    yield
                for b0 in range(0, 18, 8):
                    nb_ = min(8, 18 - b0)
                    pv, Bpv = psP()
                    for bi in range(nb_):
                        blk = b0 + bi
                        mm(pv[:, bi * 64:(bi + 1) * 64], ckvT[:, blk * 128:(blk + 1) * 128], wuv[:, h * 64:(h + 1) * 64], True, True,
                           [Bw, Bckv], [Bpv])
                    cp("act", vA[s_][:, b0:b0 + nb_, 0:64], pv[:, 0:nb_ * 64].rearrange("p (g d) -> p g d", d=64), [Bpv], [Bvh])
                    yield

            def exhaust(g):
                if g is not None:
                    for _ in g:
                        pass

            exhaust(proj_gen(0))
            e.set_warm(WARM_MLA)
            for h in range(16):
                s_ = h % 2
                Bkh, Bvh = B("mla_kh", s_), B("mla_vh", s_)
                nxt = proj_gen(h + 1) if h + 1 < 16 else None

                def hook(nxt=nxt):
                    if nxt is not None:
                        try:
                            next(nxt)
                        except StopIteration:
                            pass
                hh = h % 2
                c = h // 2
                for qt in range(4):
                    t0, nq = TL[qt]
                    units = [(kc, 0, nq, None) for kc in range(18)]
                    e.attend(lambda a0, a1, t0=t0, s_=s_: qh[s_][0:96, t0 + a0:t0 + a1],
                             lambda kc, s_=s_: kh[s_][0:96, kc * 128:(kc + 1) * 128], 96,
                             lambda kc, s_=s_: vA[s_][:, kc, :], units, scale, aT[hh * 64:hh * 64 + 64, c, t0:t0 + nq],
                             [B("mla_qh", s_, qt)], [Bkh], [Bvh], Bo(c, hh, qt), ascr, nq, hook=hook)
                exhaust(nxt)
            e.attn_flush()
            e.set_warm(0)
            e.set_nA(4)
            S.barrier()
    e.dbg_dump("o", aT[:], "pool")
    with ExitStack() as s3:
        e.out_proj(i, b, aT, lambda k, ti: [Bo(k, 0, ti), Bo(k, 1, ti)], e.mla_wo, s3, [0, 1, 2, 3], "mla")
        S.barrier()


_CACHE = {}


def run_layers(inputs, layers, ncores=NCORES, last_layer_idx=3):
    inp = {k: np.asarray(v) for k, v in inputs.items()}
    shared = host_shared(inp)
    key = (tuple(layers), last_layer_idx)
    if key not in _CACHE:
        _CACHE[key] = build(layers=layers, last_layer_idx=last_layer_idx)
    nc = _CACHE[key]
    in_maps = []
    for c in range(ncores):
        m = dict(shared)
        m.update(host_core(inp, c))
        in_maps.append(m)
    res = run_bass_kernel_spmd(nc, in_maps, core_ids=list(range(ncores)))
    run_layers.dbg = np.asarray(res.results[0]["dbgT"]) if "dbgT" in res.results[0] else None
    outs = [np.asarray(r["outT"]).transpose(0, 2, 1) for r in res.results]
    return np.ascontiguousarray(np.concatenate(outs, 0)).astype(np.float32)


def kernel(**inputs):
    return run_layers(inputs, (0, 1, 2, 3))
```

```python
import numpy as np
from contextlib import ExitStack
import concourse.bass as bass
import concourse.mybir as mybir
from concourse.bass_utils import run_bass_kernel_spmd

F32 = mybir.dt.float32
BF16 = mybir.dt.bfloat16
AF = mybir.ActivationFunctionType
ALU = mybir.AluOpType

D = 1024
L = 2048
LC = 256
NT = L + LC
DFF = 2816
NFC = DFF // 128
EPS = 1e-6
TL = [(0, 512), (512, 512), (1024, 512), (1536, 512), (2048, 256)]
NCORES = 8
NBPC = 2


class Buf:
    __slots__ = ("w", "r", "name")

    def __init__(self, name):
        self.name = name
        self.w = None
        self.r = {}


class Sched:
    ENGS = ("pe", "act", "dve", "pool", "sp")

    def __init__(self, nc, es):
        self.nc = nc
        self.es = es
        self.prog = {e: [] for e in self.ENGS}
        self.sem = {e: es.enter_context(nc.semaphore("s_" + e)) for e in self.ENGS}
        self.cnt = {e: 0 for e in self.ENGS}
        self.seen = {e: {} for e in self.ENGS}
        self.dsem = {}
        self.dcnt = {}
        self.bufs = {}
        self.ninst = 0

    def B(self, *key):
        b = self.bufs.get(key)
        if b is None:
            b = self.bufs[key] = Buf(key)
        return b

    def _semof(self, key):
        return self.sem[key] if key in self.sem else self.dsem[key]

    def _wait(self, eng, key, count, same_ok=False):
        if key == eng and same_ok:
            return
        if self.seen[eng].get(key, 0) >= count:
            return
        self.seen[eng][key] = count
        s = self._semof(key)
        self.prog[eng].append(lambda e, s=s, c=count: e.wait_ge(s, c))

    def op(self, eng, fn, reads=(), writes=(), inc=True):
        for b in reads:
            if b.w is not None:
                self._wait(eng, b.w[0], b.w[1], same_ok=(eng == "pe"))
        for b in writes:
            if b.w is not None:
                self._wait(eng, b.w[0], b.w[1], same_ok=(eng == "pe"))
            for k, c in b.r.items():
                self._wait(eng, k, c, same_ok=True)
        c = self.cnt[eng] + 1
        if inc:
            self.cnt[eng] = c
            s = self.sem[eng]
            self.prog[eng].append(lambda e, fn=fn, s=s: fn(e).then_inc(s, 1))
        else:
            self.prog[eng].append(lambda e, fn=fn: fn(e))
        for b in reads:
            if b.r.get(eng, 0) < c:
                b.r[eng] = c
        for b in writes:
            b.w = (eng, c)
            b.r = {}
        self.ninst += 1

    def dma(self, q, out, in_, reads=(), writes=(), key=None):
        if key not in self.dsem:
            self.dsem[key] = self.es.enter_context(self.nc.semaphore("d_%d" % len(self.dsem)))
            self.dcnt[key] = 0
        for b in reads:
            if b.w is not None:
                self._wait(q, b.w[0], b.w[1])
        for b in writes:
            if b.w is not None:
                self._wait(q, b.w[0], b.w[1])
            for k, c in b.r.items():
                self._wait(q, k, c)
        self.dcnt[key] += 16
        c = self.dcnt[key]
        s = self.dsem[key]
        self.prog[q].append(lambda e, s=s, out=out, in_=in_: e.dma_start(out=out, in_=in_).then_inc(s, 16))
        for b in reads:
            if b.r.get(key, 0) < c:
                b.r[key] = c
        for b in writes:
            b.w = (key, c)
            b.r = {}
        self.ninst += 1

    def barrier(self):
        snap = dict(self.cnt)
        dsnap = dict(self.dcnt)
        for e in self.ENGS:
            for k, c in snap.items():
                if k != e and c > 0:
                    self._wait(e, k, c)
            for k, c in dsnap.items():
                if c > 0:
                    self._wait(e, k, c)
        for b in self.bufs.values():
            b.w = None
            b.r = {}

    def finish(self):
        self.barrier()
        nc = self.nc
        with nc.Block() as block:
            @block.tensor
            def _(e):
                for f in self.prog["pe"]:
                    f(e)

            @block.scalar
            def _(e):
                for f in self.prog["act"]:
                    f(e)

            @block.vector
            def _(e):
                for f in self.prog["dve"]:
                    f(e)

            @block.gpsimd
            def _(e):
                for f in self.prog["pool"]:
                    f(e)

            @block.sync
            def _(e):
                for f in self.prog["sp"]:
                    f(e)


def _rope_tables(npart, blocks):
    t = np.arange(L)
    cos = np.ones((npart, L), np.float64)
    ssin = np.zeros((npart, L), np.float64)
    perm = np.zeros((npart, npart), np.float32)
    for (s0, nd, kind) in blocks:
        half = nd // 2
        pos = (t // 64) if kind == "row" else (t % 64)
        freqs = 10000.0 ** (-np.arange(half, dtype=np.float32).astype(np.float64) / half)
        ang = pos[None, :].astype(np.float64) * freqs[:, None]
        ang = (pos[None, :].astype(np.float32) * freqs.astype(np.float32)[:, None]).astype(np.float64)
        for i in range(half):
            d1, d2 = s0 + i, s0 + half + i
            cos[d1] = np.cos(ang[i]); cos[d2] = np.cos(ang[i])
            ssin[d1] = -np.sin(ang[i]); ssin[d2] = np.sin(ang[i])
            perm[d2, d1] = 1.0
            perm[d1, d2] = 1.0
    return cos.astype(np.float32), ssin.astype(np.float32), perm


def _vec_layout():
    off = {}
    n = 0
    def add(name, w):
        nonlocal n
        off[name] = n
        n += w
    add("b_ada", 4 * 48 * 3)
    add("g_mix", 32)
    add("g_ffn", 32)
    add("pool_scale", 8)
    add("pool_b", 8)
    add("swa_gq", 1); add("swa_gk", 1); add("swa_sink", 16)
    add("na_gq", 1); add("na_gk", 1)
    add("mla_gcq", 2); add("mla_gckv", 1); add("mla_gq", 1); add("mla_gk", 1)
    add("eps", 1); add("zero", 1)
    add("cT", 8 * 3)
    return off, n


VOFF, NV = _vec_layout()
SWA_QHEADS = [(8 * (c // 4) + (c % 4), 8 * (c // 4) + 4 + (c % 4)) for c in range(8)]


def _fm(v):
    return np.ascontiguousarray(v.reshape(-1, 128).T)


def host_shared(inp):
    f = np.float32
    sh = {}
    vecs = np.zeros((128, NV), f)
    def put(name, arr):
        arr = np.asarray(arr, f)
        vecs[:, VOFF[name]:VOFF[name] + arr.shape[1]] = arr
    put("b_ada", np.repeat(np.concatenate([_fm(inp["b_ada"][i]) for i in range(4)], 1), 3, axis=1))
    put("g_mix", np.concatenate([_fm(inp["g_mix"][i]) for i in range(4)], 1))
    put("g_ffn", np.concatenate([_fm(inp["g_ffn"][i]) for i in range(4)], 1))
    put("pool_scale", _fm(inp["pool_scale"][0]))
    put("pool_b", _fm(inp["pool_b"][0].reshape(-1)))
    put("swa_gq", np.tile(inp["swa_g_q"][0], 2)[:, None])
    put("swa_gk", np.tile(inp["swa_g_k"][0], 2)[:, None])
    put("swa_sink", np.broadcast_to(inp["swa_sink"][0][None, :], (128, 16)))
    put("na_gq", np.tile(inp["na_g_q"][0], 2)[:, None])
    put("na_gk", np.tile(inp["na_g_k"][0], 2)[:, None])
    put("mla_gcq", _fm(inp["mla_g_cq"][0]))
    put("mla_gckv", inp["mla_g_ckv"][0][:, None])
    gq = np.zeros((128, 1), f); gq[:96, 0] = inp["mla_g_q"][0]
    gk = np.zeros((128, 1), f); gk[:96, 0] = inp["mla_g_k"][0]
    put("mla_gq", gq); put("mla_gk", gk)
    vecs[:, VOFF["eps"]] = EPS
    sh["vecs_shared"] = vecs
    for k in ("w_ada", "w_gate_up", "w_down"):
        sh[k] = np.ascontiguousarray(inp[k], f)
    wqkv = inp["swa_w_qkv"][0]
    qcols = np.concatenate([np.arange(h * 64, h * 64 + 64) for pr in SWA_QHEADS for h in pr])
    sh["swa_wq"] = np.ascontiguousarray(wqkv[:, qcols])
    sh["swa_wk"] = np.ascontiguousarray(wqkv[:, 1024:1280])
    sh["swa_wv"] = np.ascontiguousarray(wqkv[:, 1280:1536])
    sh["swa_wo"] = np.ascontiguousarray(inp["swa_w_o"][0][qcols, :])
    c64, s64, p64 = _rope_tables(64, [(0, 32, "row"), (32, 32, "col")])
    sh["rope64_cos"] = np.concatenate([c64, c64], 0)
    sh["rope64_sin"] = np.concatenate([s64, s64], 0)
    pp = np.zeros((128, 128), f); pp[:64, :64] = p64; pp[64:, 64:] = p64
    sh["rope64_perm"] = pp
    sh["na_wqkv"] = np.ascontiguousarray(inp["na_w_qkv"][0])
    sh["na_wo"] = np.ascontiguousarray(inp["na_w_o"][0])
    sh["na_bias"] = _na_bias_tables(inp["na_rpb"][0])
    sh["pool_w"] = np.ascontiguousarray(inp["pool_w"][0])
    sh["mla_wa"] = np.ascontiguousarray(inp["mla_w_a"][0])
    sh["mla_wuq"] = np.ascontiguousarray(inp["mla_w_uq"][0])
    wukv = inp["mla_w_ukv"][0].reshape(128, 16, 128)
    wk = np.zeros((128, 16, 96), f); wk[:, :, :64] = wukv[:, :, :64]
    sh["mla_wuk"] = wk.reshape(128, 16 * 96)
    sh["mla_wuv"] = np.ascontiguousarray(wukv[:, :, 64:]).reshape(128, 1024)
    sh["mla_wo"] = np.ascontiguousarray(inp["mla_w_o"][0])
    c96, s96, p96 = _rope_tables(96, [(64, 16, "row"), (80, 16, "col")])
    sh["rope96_cos"] = c96; sh["rope96_sin"] = s96; sh["rope96_perm"] = p96
    iext = np.zeros((32, 96), f); iext[np.arange(32), 64 + np.arange(32)] = 1.0
    sh["mla_iext"] = iext
    consts = np.zeros((128, 512), f)
    consts[:, 0:128] = 1.0
    consts[:64, 128:192] = 1.0; consts[64:, 192:256] = 1.0
    consts[:96, 256:352] = 1.0
    sh["consts"] = consts
    kk = np.arange(128)[:, None]; qq = np.arange(128)[None, :]
    m_prev = ((kk + 128 - qq) <= 128).astype(f)
    m_same = np.ones((128, 128), f)
    m_next = ((qq + 128 - kk) <= 128).astype(f)
    sh["swa_mask"] = np.concatenate([m_prev, m_same, m_next], 1)
    pe = np.zeros((128, 4, 2, 2, 16), f)
    for gi, w in enumerate((2, 4, 8, 16)):
        for li, Lx in enumerate((L, LC)):
            t = np.arange(Lx)
            lo = np.clip(t - w // 2, 0, Lx); hi = np.clip(t - w // 2 + w, 0, Lx)
            inv = (1.0 / (hi - lo).astype(f)).astype(f)
            pe[:, gi, li, 0, :w // 2] = inv[:w // 2]
            if w // 2 - 1 > 0:
                pe[:, gi, li, 1, :w // 2 - 1] = inv[Lx - (w // 2 - 1):]
    sh["pool_edge"] = pe.reshape(128, -1)
    return sh


NA_NEG = -30000.0
NA_EI = 24
NA_EF = 16
NA_ET = NA_EI + NA_EF


def _na_bias_tables(rpb):
    H = 16
    tab = np.full((128, H, NA_ET, 64), NA_NEG, np.float32)
    kc = np.arange(64)[:, None]
    qc = np.arange(64)[None, :]
    cs = np.clip(qc - 8, 0, 48)
    colvalid = (kc >= cs) & (kc < cs + 16)
    dcidx = np.clip(kc - qc, -15, 15) + 15
    for half in range(2):
        for sec, (n_e, e0, lo, hi) in enumerate(((NA_EI, -4, -4, 3), (NA_EF, 0, -7, 7))):
            base = 0 if sec == 0 else NA_EI
            for idx in range(n_e):
                e = idx + e0 - half
                dr = 7 - e
                if dr < lo or dr > hi:
                    continue
                vals = rpb[:, dr + 7, :][:, dcidx]
                vals = np.where(colvalid[None], vals, NA_NEG)
                tab[half * 64:(half + 1) * 64, :, base + idx, :] = vals.transpose(1, 0, 2)
    return tab.reshape(128, H * NA_ET * 64)


def host_core(inp, core):
    b0 = core * NBPC
    d = {}
    d["xT"] = np.ascontiguousarray(inp["x"][b0:b0 + NBPC].transpose(0, 2, 1))
    d["ctxT"] = np.ascontiguousarray(inp["ctx"][b0:b0 + NBPC].transpose(0, 2, 1))
    cc = np.stack([inp["c"][b0], inp["c"][b0 + 1], inp["c_ctx"]], 1)
    d["cT"] = np.ascontiguousarray(cc.reshape(8, 128, 3).transpose(1, 0, 2)).reshape(128, 24).astype(np.float32)
    return d


def build(layers=(0, 1, 2, 3), nb=NBPC, last_layer_idx=3, parts="ABCD", dbg=None):
    nc = bass.Bass("TRN2", target_bir_lowering=False)
    dr = {}
    def din(name, shape):
        dr[name] = nc.dram_tensor(name, list(shape), F32, kind="ExternalInput").ap()
        return dr[name]
    xT = din("xT", [NBPC, D, L]); ctxT = din("ctxT", [NBPC, D, LC]); cT_d = din("cT", [128, 24])
    vecs_d = din("vecs_shared", [128, NV])
    w_ada = din("w_ada", [4, D, 6 * D]); w_gu = din("w_gate_up", [4, D, 2 * DFF]); w_dn = din("w_down", [4, DFF, D])
    swa_wq = din("swa_wq", [D, 1024]); swa_wk = din("swa_wk", [D, 256]); swa_wv = din("swa_wv", [D, 256]); swa_wo = din("swa_wo", [D, D])
    r64c = din("rope64_cos", [128, L]); r64s = din("rope64_sin", [128, L]); r64p = din("rope64_perm", [128, 128])
    na_wqkv = din("na_wqkv", [D, 3072]); na_wo = din("na_wo", [D, D]); na_bias = din("na_bias", [128, 16 * NA_ET * 64])
    pool_w = din("pool_w", [4, 256, 256])
    mla_wa = din("mla_wa", [D, 416]); mla_wuq = din("mla_wuq", [256, 1536]); mla_wuk = din("mla_wuk", [128, 16 * 96])
    mla_wuv = din("mla_wuv", [128, 1024]); mla_wo = din("mla_wo", [D, D])
    r96c = din("rope96_cos", [96, L]); r96s = din("rope96_sin", [96, L]); r96p = din("rope96_perm", [96, 96])
    iext_d = din("mla_iext", [32, 96]); consts_d = din("consts", [128, 512]); swa_mask_d = din("swa_mask", [128, 384])
    pool_edge_d = din("pool_edge", [128, 4 * 2 * 2 * 16])
    outT = nc.dram_tensor("outT", [NBPC, D, L], F32, kind="ExternalOutput").ap()
    dbgT = nc.dram_tensor("dbgT", [128, 8, NT], F32, kind="ExternalOutput").ap() if dbg else None

    es = ExitStack()
    with es:
        S = Sched(nc, es)
        B = S.B

        uid = [0]

        def sb(name, shape, dt, stack=es):
            uid[0] += 1
            return stack.enter_context(nc.sbuf_tensor("%s_%d" % (name, uid[0]), list(shape), dt))

        hT = sb("hT", [128, 8, NT], F32)
        vec = sb("vec", [128, NV], F32)
        cst = sb("cst", [128, 512], BF16)
        modT = sb("modT", [128, 4 * 48 * 3], F32)
        gsT = sb("gsT", [128, 4 * 2 * 8 * 3], F32)
        sinkE = sb("sinkE", [128, 16], F32)
        psb = [es.enter_context(nc.psum_tensor("ps%d" % i, [128, 512], F32)) for i in range(8)]
        PSB = [B("ps", i) for i in range(8)]
        rr = {"A": 0, "C": 0, "Bk": 0}

        rr["nA"] = 4

        def psA():
            i = rr["A"] % rr["nA"]; rr["A"] = (i + 1) % rr["nA"]
            return psb[i], PSB[i]

        def psP():
            return psb[3], PSB[3]

        def set_nA(n_):
            rr["nA"] = n_; rr["A"] = 0

        def psAcc():
            i = 4 + rr["Bk"]; rr["Bk"] = (rr["Bk"] + 1) % 2
            return psb[i], PSB[i]

        def psC():
            i = 6 + rr["C"]; rr["C"] = (rr["C"] + 1) % 2
            return psb[i], PSB[i]

        ones128 = cst[:, 0:128]
        bd64 = cst[:, 128:256]
        ones96 = cst[0:96, 256:352]

        def V(name, j=0, p0=0, p1=128):
            o = VOFF[name] + j
            return vec[p0:p1, o:o + 1]

        def modcol(i, ch, col):
            o = (i * 48 + ch) * 3 + col
            return modT[:, o:o + 1]

        def gscol(i, which, k, col):
            o = ((i * 2 + which) * 8 + k) * 3 + col
            return gsT[:, o:o + 1]

        def act(out, in_, func, reads, writes, bias=None, scale=None):
            kw = {}
            if bias is not None:
                kw["bias"] = bias
            if scale is not None:
                kw["scale"] = scale
            S.op("act", lambda e: e.activation(out=out, in_=in_, func=func, **kw), reads, writes)

        def mm(out, lhsT, rhs, start, stop, reads, writes):
            S.op("pe", lambda e: e.matmul(out, lhsT, rhs, start=start, stop=stop), reads, writes, inc=True)

        def tt(eng, out, in0, in1, op, reads, writes):
            S.op(eng, lambda e: e.tensor_tensor(out=out, in0=in0, in1=in1, op=op), reads, writes)

        def ts(eng, out, in0, s1, s2, op0, op1, reads, writes):
            if op1 is None:
                S.op(eng, lambda e: e.tensor_scalar(out=out, in0=in0, scalar1=s1, scalar2=None, op0=op0), reads, writes)
            else:
                S.op(eng, lambda e: e.tensor_scalar(out=out, in0=in0, scalar1=s1, scalar2=s2, op0=op0, op1=op1), reads, writes)

        def stt(eng, out, in0, scalar, in1, op0, op1, reads, writes):
            S.op(eng, lambda e: e.scalar_tensor_tensor(out=out, in0=in0, scalar=scalar, in1=in1, op0=op0, op1=op1), reads, writes)

        def cp(eng, out, in_, reads, writes):
            if eng == "act":
                S.op(eng, lambda e: e.activation(out=out, in_=in_, func=AF.Identity), reads, writes)
            else:
                S.op(eng, lambda e: e.tensor_copy(out=out, in_=in_), reads, writes)

        def wload(dst, src, buf, key):
            S.dma("pool", dst, src, writes=[buf], key=key)

        Bvec = B("vec"); Bcst = B("cst"); Bmod = B("mod"); Bgs = B("gs")
        S.dma("sp", vec[:, 0:VOFF["cT"]], vecs_d[:, 0:VOFF["cT"]], writes=[Bvec], key="vec")
        S.dma("sp", vec[:, VOFF["cT"]:NV], cT_d, writes=[Bvec], key="vec")
        wload(cst[:], consts_d, Bcst, "cst")
        with ExitStack() as ps_:
            condT = sb("condT", [128, 24], BF16, ps_)
            wab = [sb("wab%d" % i, [128, 8, 512], BF16, ps_) for i in range(2)]
            Bcond = B("cond")
            act(condT[:], vec[:, VOFF["cT"]:VOFF["cT"] + 24], AF.Silu, [Bvec], [Bcond])
            cond3 = condT[:].rearrange("p (k c) -> p k c", c=3)
            n = 0
            for i in range(4):
                for cb in range(12):
                    w = wab[n % 2]; Bw = B("wab", n % 2)
                    wload(w[:], w_ada[i, :, cb * 512:(cb + 1) * 512].rearrange("(k p) n -> p k n", p=128), Bw, "wab%d" % (n % 2))
                    pt, Bp = psA()
                    for oc in range(4):
                        for k in range(8):
                            mm(pt[:, oc * 3:oc * 3 + 3], w[:, k, oc * 128:(oc + 1) * 128], cond3[:, k, :], k == 0, k == 7,
                               [Bw, Bcond], [Bp])
                    ch0 = cb * 4
                    o = (i * 48 + ch0) * 3
                    bb = vec[:, VOFF["b_ada"] + o: VOFF["b_ada"] + o + 12]
                    tt("dve", modT[:, o:o + 12], pt[:, 0:12], bb, ALU.add, [Bp, Bvec], [Bmod])
                    n += 1
            for i in range(4):
                for which, (gname, scj) in enumerate((("g_mix", 1), ("g_ffn", 4))):
                    for col in range(3):
                        o_in = (i * 48 + scj * 8) * 3 + col
                        sc_view = modT[:, o_in:o_in + 22:3]
                        o_out = ((i * 2 + which) * 8) * 3 + col
                        gs_view = gsT[:, o_out:o_out + 22:3]
                        g_view = vec[:, VOFF[gname] + i * 8: VOFF[gname] + i * 8 + 8]
                        stt("dve", gs_view, sc_view, 1.0, g_view, ALU.add, ALU.mult, [Bmod, Bvec], [Bgs])
            act(sinkE[:], vec[:, VOFF["swa_sink"]:VOFF["swa_sink"] + 16], AF.Exp, [Bvec], [B("sinkE")])
            S.barrier()

        Bh = lambda k, t: B("h", k, t)

        def modulate(i, which, bcol, aT, Ba, scr, tiles):
            shj = 0 if which == 0 else 3
            for ti in tiles:
                t0, n = TL[ti]
                col = 2 if ti == 4 else bcol
                pt, Bp = psC()
                for k in range(8):
                    sq = scr["sq"][k % 2]; Bsq = B("sq", k % 2)
                    tt("pool", sq[:, :n], hT[:, k, t0:t0 + n], hT[:, k, t0:t0 + n], ALU.mult, [Bh(k, ti)], [Bsq])
                    mm(pt[:, :n], ones128, sq[:, :n], k == 0, k == 7, [Bsq, Bcst], [Bp])
                rstd = scr["rstd"]; Br = B("rstd")
                act(rstd[:, :n], pt[:, :n], AF.Ln, [Bp, Bvec], [Br], bias=V("eps"), scale=1.0 / D)
                act(rstd[:, :n], rstd[:, :n], AF.Exp, [Br], [Br], scale=-0.5)
                for k in range(8):
                    tmp = scr["tmp"][k % 2]; Bt = B("mtmp", k % 2)
                    tt("dve", tmp[:, :n], hT[:, k, t0:t0 + n], rstd[:, :n], ALU.mult, [Bh(k, ti), Br], [Bt])
                    act(aT[:, k, t0:t0 + n], tmp[:, :n], AF.Identity, [Bt, Bgs, Bmod], [Ba(k, ti)],
                        bias=modcol(i, shj * 8 + k, col), scale=gscol(i, which, k, col))

        def ffn(i, bcol, aT, Ba, st, tiles):
            G = 2
            wg = [sb("ffn_wg%d" % j, [128, 8, 2 * G * 128], BF16, st) for j in range(2)]
            wd = [sb("ffn_wd%d" % j, [128, G, D], BF16, st) for j in range(2)]
            actb = [sb("ffn_act%d" % j, [128, G, 512], BF16, st) for j in range(2)]
            sgb = [sb("ffn_sg%d" % j, [128, 512], BF16, st) for j in range(2)]
            ngroups = NFC // G
            na = 0
            for g in range(ngroups):
                s = g % 2
                Bwg = B("ffn_wg", s); Bwd = B("ffn_wd", s)
                f0 = g * G * 128
                wload(wg[s][:, :, 0:G * 128], w_gu[i, :, f0:f0 + G * 128].rearrange("(k p) n -> p k n", p=128), Bwg, "ffn_wg%d" % s)
                wload(wg[s][:, :, G * 128:2 * G * 128], w_gu[i, :, DFF + f0:DFF + f0 + G * 128].rearrange("(k p) n -> p k n", p=128), Bwg, "ffn_wg%d" % s)
                wload(wd[s][:], w_dn[i, f0:f0 + G * 128, :].rearrange("(g p) n -> p g n", p=128), Bwd, "ffn_wd%d" % s)
                for ti in tiles:
                    t0, n = TL[ti]
                    col = 2 if ti == 4 else bcol
                    ab = actb[na % 2]; Bab = B("ffn_act", na % 2); na += 1
                    for fi in range(G):
                        pg, Bpg = psA()
                        for k in range(8):
                            mm(pg[:, :n], wg[s][:, k, fi * 128:(fi + 1) * 128], aT[:, k, t0:t0 + n], k == 0, k == 7, [Bwg, Ba(k, ti)], [Bpg])
                        pu, Bpu = psA()
                        for k in range(8):
                            mm(pu[:, :n], wg[s][:, k, (G + fi) * 128:(G + fi + 1) * 128], aT[:, k, t0:t0 + n], k == 0, k == 7, [Bwg, Ba(k, ti)], [Bpu])
                        sg = sgb[fi % 2]; Bsg = B("ffn_sg", fi % 2)
                        act(sg[:, :n], pg[:, :n], AF.Silu, [Bpg], [Bsg])
                        tt("dve", ab[:, fi, :n], pu[:, :n], sg[:, :n], ALU.mult, [Bpu, Bsg], [Bab])
                    for o in range(8):
                        po, Bpo = psAcc() if o % 2 == 0 else psC()
                        for fi in range(G):
                            mm(po[:, :n], wd[s][:, fi, o * 128:(o + 1) * 128], ab[:, fi, :n], fi == 0, fi == G - 1, [Bwd, Bab], [Bpo])
                        stt("dve", hT[:, o, t0:t0 + n], po[:, :n], modcol(i, 5 * 8 + o, col), hT[:, o, t0:t0 + n], ALU.mult, ALU.add,
                            [Bpo, Bmod, Bh(o, ti)], [Bh(o, ti)])

        def out_proj(i, bcol, oT, Bo, w_dram, st, tiles, name):
            wo = sb(name + "_wo", [128, 8, D], BF16, st)
            Bwo = B(name + "_wo")
            for half in range(2):
                wload(wo[:, half * 4:(half + 1) * 4, :], w_dram[half * 512:(half + 1) * 512, :].rearrange("(k p) n -> p k n", p=128), Bwo, name + "_wo")
            for ti in tiles:
                t0, n = TL[ti]
                col = 2 if ti == 4 else bcol
                for o in range(8):
                    po, Bpo = psA()
                    for k in range(8):
                        mm(po[:, :n], wo[:, k, o * 128:(o + 1) * 128], oT[:, k, t0:t0 + n], k == 0, k == 7, [Bwo] + Bo(k, ti), [Bpo])
                    stt("dve", hT[:, o, t0:t0 + n], po[:, :n], modcol(i, 2 * 8 + o, col), hT[:, o, t0:t0 + n], ALU.mult, ALU.add,
                        [Bpo, Bmod, Bh(o, ti)], [Bh(o, ti)])

        def headnorm_gen(out_ap, Bout, praw, Bpraw, M, n, ones_lhsT, invD, gain, scr, rope=None, fine=False):
            sq = scr["hsq"]; Bsq = B("hsq")
            if "hraw" in scr:
                hr = scr["hraw"]; Bhr = B("hraw")
                cp("dve", hr[0:M, :n], praw, [Bpraw], [Bhr])
                if fine:
                    yield
                tt("pool", sq[0:M, :n], hr[0:M, :n], hr[0:M, :n], ALU.mult, [Bhr], [Bsq])
            else:
                act(sq[0:M, :n], praw, AF.Square, [Bpraw], [Bsq])
            yield
            ps2, Bps2 = psC()
            mm(ps2[0:M, :n], ones_lhsT, sq[0:M, :n], True, True, [Bsq, Bcst], [Bps2])
            if fine:
                yield
            rs = scr["hrs"]; Brs = B("hrs")
            act(rs[0:M, :n], ps2[0:M, :n], AF.Ln, [Bps2, Bvec], [Brs], bias=V("eps", 0, 0, M), scale=invD)
            act(rs[0:M, :n], rs[0:M, :n], AF.Exp, [Brs], [Brs], scale=-0.5)
            if fine:
                yield
            if rope is None:
                stt("dve", out_ap, praw, gain, rs[0:M, :n], ALU.mult, ALU.mult, [Bpraw, Bvec, Brs], Bout)
                return
            perm, cos_ap, sin_ap, Brope = rope
            qn = scr["hqn"]; Bqn = B("hqn")
            stt("dve", qn[0:M, :n], praw, gain, rs[0:M, :n], ALU.mult, ALU.mult, [Bpraw, Bvec, Brs], [Bqn])
            yield
            ps3, Bps3 = psC()
            mm(ps3[0:M, :n], perm, qn[0:M, :n], True, True, [Bqn] + Brope, [Bps3])
            t1 = scr["ht1"]; Bt1 = B("ht1")
            tt("pool", t1[0:M, :n], qn[0:M, :n], cos_ap, ALU.mult, [Bqn] + Brope, [Bt1])
            if fine:
                yield
            t2 = scr["ht2"]; Bt2 = B("ht2")
            tt("dve", t2[0:M, :n], ps3[0:M, :n], sin_ap, ALU.mult, [Bps3] + Brope, [Bt2])
            if fine:
                yield
            tt("pool", out_ap, t1[0:M, :n], t2[0:M, :n], ALU.add, [Bt1, Bt2], Bout)

        def headnorm(*a_, **k_):
            for _ in headnorm_gen(*a_, **k_):
                pass

        class HNPipe:
            def __init__(self):
                self.g = []

            def push(self, gen):
                self.step()
                next(gen)
                self.g.append(gen)

            def step(self):
                for g in list(self.g):
                    try:
                        next(g)
                    except StopIteration:
                        self.g.remove(g)

            def drain(self):
                while self.g:
                    self.step()

        def head_scratch(st, rope=True):
            d = {"hsq": sb("hsq", [128, 512], BF16, st), "hrs": sb("hrs", [128, 512], F32, st)}
            if rope:
                d.update({"hqn": sb("hqn", [128, 512], BF16, st), "ht1": sb("ht1", [128, 512], F32, st),
                          "ht2": sb("ht2", [128, 512], F32, st)})
            return d

        pend = []
        LA = 2

        def attn_flush(keep=0):
            while len(pend) > keep:
                pend.pop(0)()

        def attend(qT_ap_fn, kT_ap_fn, Kdim, vaug_fn, units, scale, out_ap, reads_q, reads_k, reads_v, Bout, scr,
                   nq, sink_ap=None, hook=None, la=2):
            po, Bpo = psAcc()
            nu = len(units)
            for ui, (kc, c0, c1, mspec) in enumerate(units):
                pS, BpS = psA()
                mm(pS[:, 0:c1 - c0], kT_ap_fn(kc), qT_ap_fn(c0, c1), True, True, reads_q + reads_k, [BpS])
                pi = scr["pi"]; scr["pi"] = (pi + 1) % len(scr["P"])
                Pt = scr["P"][pi]; BP = B("P", pi)
                act(Pt[:, 0:c1 - c0], pS[:, 0:c1 - c0], AF.Exp, [BpS], [BP], scale=scale)
                if mspec:
                    for (cc0, cc1, m_ap, mb) in mspec:
                        tt("dve", Pt[:, cc0 - c0:cc1 - c0], Pt[:, cc0 - c0:cc1 - c0], m_ap, ALU.mult, [BP] + mb, [BP])
                pend.append(lambda kc=kc, c0=c0, c1=c1, Pt=Pt, BP=BP, ui=ui:
                            mm(po[:, c0:c1], vaug_fn(kc), Pt[:, 0:c1 - c0], ui == 0, ui == nu - 1, [BP] + reads_v, [Bpo]))
                attn_flush(la)
                if hook is not None:
                    hook()

            def epilogue():
                ri = scr["ri"]; scr["ri"] = (ri + 1) % len(scr["rc"])
                rc = scr["rc"][ri]; Brc = B("rc", ri)
                if sink_ap is not None:
                    act(rc[64:128, :nq], po[64:128, :nq], AF.Ln, [Bpo, B("sinkE")], [Brc], bias=sink_ap)
                else:
                    act(rc[64:128, :nq], po[64:128, :nq], AF.Ln, [Bpo], [Brc])
                act(rc[64:128, :nq], rc[64:128, :nq], AF.Exp, [Brc], [Brc], scale=-1.0)
                tt("dve", out_ap, po[0:64, :nq], rc[64:128, :nq], ALU.mult, [Bpo, Brc], [Bout])
            pend.append(epilogue)

        def attn_scratch(st, nP=4):
            return {"P": [sb("Pt%d" % j, [128, 512], BF16, st) for j in range(nP)], "pi": 0,
                    "rc": [sb("rc%d" % j, [128, 512], F32, st) for j in range(1)], "ri": 0}

        for b in range(nb):
            for k in range(8):
                S.dma("sp", hT[:, k, 0:L], xT[b, k * 128:(k + 1) * 128, :], writes=[Bh(k, t) for t in range(4)], key="hx%d" % k)
                S.dma("sp", hT[:, k, L:NT], ctxT[b, k * 128:(k + 1) * 128, :], writes=[Bh(k, 4)], key="hc%d" % k)
            for i in layers:
                need_ctx = i < last_layer_idx
                tiles = [0, 1, 2, 3, 4] if need_ctx else [0, 1, 2, 3]
                Ba = lambda k, t: B("a", k, t)

                def mk_mscr(stk):
                    return {"sq": [sb("msq%d" % j, [128, 512], BF16, stk) for j in range(2)],
                            "rstd": sb("mrstd", [128, 512], F32, stk),
                            "tmp": [sb("mtmp%d" % j, [128, 512], F32, stk) for j in range(2)]}

                def dbg_dump(tag, src, q):
                    if dbg == tag and b == 0:
                        S.barrier()
                        S.dma(q, dbgT, src, writes=[B("dbg")], key="dbg")
                        S.barrier()

                with ExitStack() as st2:
                    if "B" in parts:
                        ENV = dict(locals())
                        ENV.update(dict(S=S, nc=nc, sb=sb, st=st2))
                        if i == 2:
                            pool_layer(ENV)
                        elif i == 0:
                            swa_layer(ENV)
                        elif i == 1:
                            na_layer(ENV)
                        else:
                            mla_layer(ENV)
                    S.barrier()
                dbg_dump("h1", hT[:], "sp")
                with ExitStack() as st3:
                    aT = sb("aT", [128, 8, NT], BF16, st3)
                    mscr = mk_mscr(st3)
                    if "C" in parts:
                        modulate(i, 1, b, aT, Ba, mscr, tiles)
                    dbg_dump("a2", aT[:], "pool")
                    if "D" in parts:
                        ffn(i, b, aT, Ba, st3, tiles)
                    S.barrier()
            for k in range(8):
                S.dma("sp", outT[b, k * 128:(k + 1) * 128, :], hT[:, k, 0:L], reads=[Bh(k, t) for t in range(4)],
                      writes=[B("out", b, k)], key="ost%d" % k)
        S.finish()
        build.ninst = S.ninst
    return nc


class NS:
    def __init__(self, d):
        self.__dict__.update(d)


def pool_layer(E):
    S, nc, sb, st = E["S"], E["nc"], E["sb"], E["st"]
    B = S.B
    i, b, Ba, hT = E["i"], E["b"], E["Ba"], E["hT"]
    vec, V, Bvec, Bmod, modcol = E["vec"], E["V"], E["Bvec"], E["Bmod"], E["modcol"]
    act, mm, tt, ts, stt, cp, wload, psA = E["act"], E["mm"], E["tt"], E["ts"], E["stt"], E["cp"], E["wload"], E["psA"]
    need_ctx, pool_w, pool_edge_d, Bh = E["need_ctx"], E["pool_w"], E["pool_edge_d"], E["Bh"]
    aT = sb("aT", [128, 8, NT], BF16, st)
    mscr = E["mk_mscr"](st)
    if "A" in E["parts"]:
        E["modulate"](i, 0, b, aT, Ba, mscr, [0, 1, 2, 3, 4])
    E["dbg_dump"]("a1", aT[:], "pool")
    pT = sb("poolP", [128, 8, NT], BF16, st)
    Bp = lambda k, t: B("poolP", k, t)
    wp = sb("poolW", [128, 4, 2, 256], BF16, st)
    Bwp = B("poolW")
    wload(wp[:].rearrange("p g k n -> p (g k) n"), pool_w.rearrange("g (k p) n -> p (g k) n", p=128), Bwp, "poolW")
    edge = sb("poolE", [128, 4 * 2 * 2 * 16], F32, st)
    Bedge = B("poolE")
    S.dma("sp", edge[:], pool_edge_d, writes=[Bedge], key="poolE")
    PADL = 8
    WLEN = PADL + L + 16
    bufA = sb("poolA", [128, WLEN], F32, st)
    bufB = sb("poolB", [128, WLEN], F32, st)
    BA, BB = B("poolA"), B("poolB")
    sT = sb("poolS", [128, 8 * 3], F32, st)
    Bs = B("poolS")
    for col in range(3):
        o_in = (i * 48 + 2 * 8) * 3 + col
        tt("dve", sT[:, col:col + 22:3], E["modT"][:, o_in:o_in + 22:3], vec[:, VOFF["pool_scale"]:VOFF["pool_scale"] + 8], ALU.mult,
           [Bmod, Bvec], [Bs])
    segs = [(0, L, 0, [0, 1, 2, 3])] + ([(L, LC, 1, [4])] if True else [])
    for k in range(8):
        gi = k // 2
        w = (2, 4, 8, 16)[gi]
        for (s0, Ln, li, tls) in segs:
            allB = [Ba(k, t) for t in tls]
            S.op("dve", lambda e: e.memset(bufA[:, 0:PADL], 0.0), [], [BA])
            S.op("dve", lambda e, Ln=Ln: e.memset(bufA[:, PADL + Ln:PADL + Ln + 16], 0.0), [], [BA])
            cp("dve", bufA[:, PADL:PADL + Ln], aT[:, k, s0:s0 + Ln], allB, [BA])
            src, Bsrc, dst, Bdst = bufA, BA, bufB, BB
            step = 1
            n_valid = PADL + Ln + 16
            while step < w:
                n_valid -= step
                tt("dve", dst[:, 0:n_valid], src[:, 0:n_valid], src[:, step:step + n_valid], ALU.add, [Bsrc], [Bdst])
                src, Bsrc, dst, Bdst = dst, Bdst, src, Bsrc
                step *= 2
            o0 = PADL - w // 2
            stt("dve", pT[:, k, s0:s0 + Ln], src[:, o0:o0 + Ln], 1.0 / w, aT[:, k, s0:s0 + Ln], ALU.mult, ALU.subtract, [Bsrc] + allB,
                [Bp(k, t) for t in tls])
            eo = ((gi * 2 + li) * 2) * 16
            ne = w // 2
            tt("dve", dst[:, 0:ne], src[:, o0:o0 + ne], edge[:, eo:eo + ne], ALU.mult, [Bsrc, Bedge], [Bdst])
            tt("dve", pT[:, k, s0:s0 + ne], dst[:, 0:ne], aT[:, k, s0:s0 + ne], ALU.subtract, [Bdst, Ba(k, tls[0])], [Bp(k, tls[0])])
            ne2 = w // 2 - 1
            if ne2 > 0:
                tt("dve", dst[:, 16:16 + ne2], src[:, o0 + Ln - ne2:o0 + Ln], edge[:, eo + 16:eo + 16 + ne2], ALU.mult, [Bsrc, Bedge], [Bdst])
                tt("dve", pT[:, k, s0 + Ln - ne2:s0 + Ln], dst[:, 16:16 + ne2], aT[:, k, s0 + Ln - ne2:s0 + Ln], ALU.subtract,
                   [Bdst, Ba(k, tls[-1])], [Bp(k, tls[-1])])
    tmpb = [sb("poolT%d" % j, [128, 512], F32, st) for j in range(2)]
    nn = 0
    for ti in ([0, 1, 2, 3, 4] if need_ctx else [0, 1, 2, 3]):
        t0, n = TL[ti]
        col = 2 if ti == 4 else b
        for o in range(8):
            g = o // 2
            po, Bpo = psA()
            for kk in range(2):
                mm(po[:, :n], wp[:, g, kk, (o % 2) * 128:(o % 2 + 1) * 128], pT[:, 2 * g + kk, t0:t0 + n], kk == 0, kk == 1,
                   [Bwp, Bp(2 * g + kk, ti)], [Bpo])
            tb = tmpb[nn % 2]; Btb = B("poolT", nn % 2); nn += 1
            act(tb[:, :n], po[:, :n], AF.Identity, [Bpo, Bvec], [Btb], bias=V("pool_b", o))
            stt("dve", hT[:, o, t0:t0 + n], tb[:, :n], sT[:, o * 3 + col:o * 3 + col + 1], hT[:, o, t0:t0 + n], ALU.mult, ALU.add,
                [Btb, Bs, Bh(o, ti)], [Bh(o, ti)])


def swa_layer(E):
    e = NS(E)
    S, nc, sb, st = e.S, e.nc, e.sb, e.st
    B = S.B
    i, b = e.i, e.b
    act, mm, tt, stt, cp, wload, psA = e.act, e.mm, e.tt, e.stt, e.cp, e.wload, e.psA
    qT = sb("swa_q", [128, 8, NT], BF16, st)
    kT = sb("swa_k", [128, 2, NT], BF16, st)
    vA = sb("swa_v", [128, 18, 4, 128], BF16, st)
    Bq = lambda c, hh, t: B("swa_q", c, hh, t)
    Bk = lambda m: B("swa_k", m)
    Bv = B("swa_v")
    Ba = e.Ba
    with ExitStack() as s1:
        aT = sb("aT", [128, 8, NT], BF16, s1)
        with ExitStack() as s0:
            mscr = e.mk_mscr(s0)
            e.modulate(i, 0, b, aT, Ba, mscr, [0, 1, 2, 3, 4])
            S.barrier()
        e.dbg_dump("a1", aT[:], "pool")
        hs = e.head_scratch(s1)
        wst = [sb("swa_ws%d" % j, [128, 8, 128], BF16, s1) for j in range(2)]
        ropeT = sb("swa_rope", [128, 2, L], BF16, s1)
        perm = sb("swa_perm", [128, 128], BF16, s1)
        Bperm = B("swa_perm")
        wload(perm[:], e.r64p, Bperm, "swa_perm")
        wload(ropeT[:, 0, :], e.r64c, Bperm, "swa_perm")
        wload(ropeT[:, 1, :], e.r64s, Bperm, "swa_perm")
        S.op("dve", lambda en: en.memset(vA[:, :, :, 64:128], 1.0), [], [Bv])
        nw = 0
        pipe = e.HNPipe()
        plist = [("k", 0), ("k", 1)] + [("q", c) for c in range(8)]
        for (kind, c) in plist:
            w = wst[nw % 2]; Bw = B("swa_ws", nw % 2)
            src = e.swa_wk if kind == "k" else e.swa_wq
            wload(w[:], src[:, c * 128:(c + 1) * 128].rearrange("(k p) n -> p k n", p=128), Bw, "swa_ws%d" % (nw % 2))
            nw += 1
            for ti in range(5):
                t0, n = TL[ti]
                pr, Bpr = psA()
                for k in range(8):
                    mm(pr[:, :n], w[:, k, :], aT[:, k, t0:t0 + n], k == 0, k == 7, [Bw, Ba(k, ti)], [Bpr])
                rope = None
                if ti < 4:
                    rope = (perm[:], ropeT[:, 0, t0:t0 + n], ropeT[:, 1, t0:t0 + n], [Bperm])
                if kind == "k":
                    out_ap = kT[:, c, t0:t0 + n]; Bo = [Bk(c)]; gain = e.V("swa_gk")
                else:
                    out_ap = qT[:, c, t0:t0 + n]; Bo = [Bq(c, 0, ti), Bq(c, 1, ti)]; gain = e.V("swa_gq")
                pipe.push(e.headnorm_gen(out_ap, Bo, pr[:, :n], Bpr, 128, n, e.bd64, 1.0 / 64, gain, hs, rope))
        pipe.drain()
        for half in range(2):
            w = wst[nw % 2]; Bw = B("swa_ws", nw % 2)
            wload(w[:], e.swa_wv[:, half * 128:(half + 1) * 128].rearrange("(k p) n -> p k n", p=128), Bw, "swa_ws%d" % (nw % 2))
            nw += 1
            for blk in range(18):
                ti = min(blk // 4, 4)
                pv, Bpv = psA()
                for k in range(8):
                    mm(pv[:, 0:128], aT[:, k, blk * 128:(blk + 1) * 128], w[:, k, :], k == 0, k == 7, [Bw, Ba(k, ti)], [Bpv])
                cp("act" if blk % 2 else "dve", vA[:, blk, 2 * half:2 * half + 2, 0:64], pv[:, 0:128].rearrange("p (g d) -> p g d", d=64),
                   [Bpv], [Bv])
        S.barrier()
    with ExitStack() as s2:
        ascr = e.attn_scratch(s2, 4)
        mask = sb("swa_mask", [128, 384], BF16, s2)
        Bmask = B("swa_mask")
        wload(mask[:], e.swa_mask_d, Bmask, "swa_mask")
        for c in range(8):
            m = c // 4
            for hh in range(2):
                head = SWA_QHEADS[c][hh]
                g = 2 * m + hh
                p0, p1 = hh * 64, hh * 64 + 64
                for qt in range(5):
                    t0, nq = TL[qt]
                    units = [(16, 0, nq, None), (17, 0, nq, None)]
                    if qt < 4:
                        for rel in range(-1, 5):
                            kc = qt * 4 + rel
                            if kc < 0 or kc > 15:
                                continue
                            qlo, qhi = max(rel - 1, 0), min(rel + 1, 3)
                            c0, c1 = qlo * 128, (qhi + 1) * 128
                            mc0 = (qlo - (rel - 1)) * 128
                            units.append((kc, c0, c1, [(c0, c1, mask[:, mc0:mc0 + (c1 - c0)], [Bmask])]))
                    e.attend(lambda a0, a1, t0=t0, p0=p0, p1=p1, c=c: qT[p0:p1, c, t0 + a0:t0 + a1],
                             lambda kc, p0=p0, p1=p1, m=m: kT[p0:p1, m, kc * 128:(kc + 1) * 128], 64,
                             lambda kc, g=g: vA[:, kc, g, :], units, 0.125, qT[p0:p1, c, t0:t0 + nq],
                             [Bq(c, hh, qt)], [Bk(m)], [Bv], Bq(c, hh, qt), ascr, nq,
                             sink_ap=e.sinkE[64:128, head:head + 1], la=3)
        e.attn_flush()
        S.barrier()
    e.dbg_dump("o", qT[:], "pool")
    with ExitStack() as s3:
        e.out_proj(i, b, qT, lambda k, ti: [Bq(k, 0, ti), Bq(k, 1, ti)], e.swa_wo, s3, [0, 1, 2, 3, 4], "swa")
        S.barrier()


def _na_units(m, ET, hh, BET):
    if m == 0:
        segs = [(0, 4, "F", range(0, 4)), (4, 8, "I", range(0, 6))]
    elif m == 3:
        segs = [(24, 29, "I", range(10, 16)), (29, 32, "F", range(12, 16))]
    else:
        segs = [(8 * m, 8 * m + 8, "I", range(4 * m - 2, 4 * m + 6))]
    units = []
    allk = sorted(set(k for sg in segs for k in sg[3]))
    for kc in allk:
        kr0 = 2 * kc
        ms = []
        for (ra, rb, kind, chunks) in segs:
            if kc not in chunks:
                continue
            if kind == "I":
                rlo, rhi = max(ra, kr0 - 3), min(rb - 1, kr0 + 5)
                if rlo > rhi:
                    continue
                idx0 = (rlo - kr0 + 7) + 4
                assert 0 <= idx0 and idx0 + (rhi - rlo + 1) <= NA_EI
            else:
                rlo, rhi = ra, rb - 1
                idx0 = NA_EI + (rlo - kr0 + 7)
                assert NA_EI + 1 <= idx0 and idx0 + (rhi - rlo + 1) <= NA_ET
            nr = rhi - rlo + 1
            cc0, cc1 = (rlo - 8 * m) * 64, (rhi + 1 - 8 * m) * 64
            tab = ET[:, hh, idx0:idx0 + nr, :].rearrange("p e c -> p (e c)")
            ms.append((cc0, cc1, tab, [BET]))
        if not ms:
            continue
        ms.sort(key=lambda x: x[0])
        for a_, b_ in zip(ms[:-1], ms[1:]):
            assert a_[1] == b_[0], (m, kc, ms)
        units.append((kc, ms[0][0], ms[-1][1], ms))
    return units


def na_layer(E):
    e = NS(E)
    S, nc, sb, st = e.S, e.nc, e.sb, e.st
    B = S.B
    i, b = e.i, e.b
    act, mm, tt, stt, cp, wload, psA = e.act, e.mm, e.tt, e.stt, e.cp, e.wload, e.psA
    Ba = e.Ba
    oT = sb("na_o", [128, 8, NT], BF16, st)
    Bo = lambda c, hh, t: B("na_o", c, hh, t)
    with ExitStack() as s1:
        aT = sb("aT", [128, 8, NT], BF16, s1)
        with ExitStack() as s0:
            mscr = e.mk_mscr(s0)
            e.modulate(i, 0, b, aT, Ba, mscr, [0, 1, 2, 3, 4])
            S.barrier()
        hs = e.head_scratch(s1, rope=False)
        wst = [sb("na_ws%d" % j, [128, 8, 128], BF16, s1) for j in range(3)]
        qp = sb("na_q", [128, NT], BF16, s1)
        kp = sb("na_k", [128, NT], BF16, s1)
        vA = sb("na_v", [128, 18, 2, 128], BF16, s1)
        ET = sb("na_et", [128, 2, NA_ET, 64], BF16, s1)
        ascr = e.attn_scratch(s1, 4)
        Bqp = lambda t: B("na_q", t)
        Bkp, Bv, BET = B("na_k"), B("na_v"), B("na_et")
        S.op("dve", lambda en: en.memset(vA[:, :, :, 64:128], 1.0), [], [Bv])
        for c in range(8):
            Bw = [B("na_ws", j) for j in range(3)]
            for j in range(3):
                wload(wst[j][:], e.na_wqkv[:, j * 1024 + c * 128:j * 1024 + (c + 1) * 128].rearrange("(k p) n -> p k n", p=128),
                      Bw[j], "na_ws%d" % j)
            wload(ET[:].rearrange("p a e c -> p (a e c)"), e.na_bias[:, (2 * c) * NA_ET * 64:(2 * c + 2) * NA_ET * 64], BET, "na_et")
            ETf = ET[:].rearrange("p a e c -> p (a e c)")
            for hx in range(2):
                act(ETf[:, hx * NA_ET * 64:(hx + 1) * NA_ET * 64], ETf[:, hx * NA_ET * 64:(hx + 1) * NA_ET * 64], AF.Exp, [BET], [BET])
            pipe = e.HNPipe()
            for ti in range(5):
                t0, n = TL[ti]
                for j, (dst, Bd, gname) in enumerate(((qp, [Bqp(ti)], "na_gq"), (kp, [Bkp], "na_gk"))):
                    pr, Bpr = psA()
                    for k in range(8):
                        mm(pr[:, :n], wst[j][:, k, :], aT[:, k, t0:t0 + n], k == 0, k == 7, [Bw[j], Ba(k, ti)], [Bpr])
                    pipe.push(e.headnorm_gen(dst[:, t0:t0 + n], Bd, pr[:, :n], Bpr, 128, n, e.bd64, 1.0 / 64, e.V(gname), hs, None))
            pipe.drain()
            for blk in range(18):
                ti = min(blk // 4, 4)
                pv, Bpv = psA()
                for k in range(8):
                    mm(pv[:, 0:128], aT[:, k, blk * 128:(blk + 1) * 128], wst[2][:, k, :], k == 0, k == 7, [Bw[2], Ba(k, ti)], [Bpv])
                cp("act" if blk % 2 else "dve", vA[:, blk, :, 0:64], pv[:, 0:128].rearrange("p (g d) -> p g d", d=64), [Bpv], [Bv])
            for hh in range(2):
                p0, p1 = hh * 64, hh * 64 + 64
                for qt in range(5):
                    t0, nq = TL[qt]
                    units = [(16, 0, nq, None), (17, 0, nq, None)]
                    if qt < 4:
                        units += _na_units(qt, ET, hh, BET)
                    e.attend(lambda a0, a1, t0=t0, p0=p0, p1=p1: qp[p0:p1, t0 + a0:t0 + a1],
                             lambda kc, p0=p0, p1=p1: kp[p0:p1, kc * 128:(kc + 1) * 128], 64,
                             lambda kc, hh=hh: vA[:, kc, hh, :], units, 0.125, oT[p0:p1, c, t0:t0 + nq],
                             [Bqp(qt)], [Bkp], [Bv], Bo(c, hh, qt), ascr, nq, la=3)
            e.attn_flush()
        S.barrier()
    e.dbg_dump("o", oT[:], "pool")
    with ExitStack() as s3:
        e.out_proj(i, b, oT, lambda k, ti: [Bo(k, 0, ti), Bo(k, 1, ti)], e.na_wo, s3, [0, 1, 2, 3, 4], "na")
        S.barrier()


def mla_layer(E):
    e = NS(E)
    S, nc, sb, st = e.S, e.nc, e.sb, e.st
    B = S.B
    i, b = e.i, e.b
    act, mm, tt, stt, cp, wload, psA, psC = e.act, e.mm, e.tt, e.stt, e.cp, e.wload, e.psA, e.psC
    Ba = e.Ba
    aT = sb("aT", [128, 8, NT], BF16, st)
    with ExitStack() as s0:
        mscr = e.mk_mscr(s0)
        e.modulate(i, 0, b, aT, Ba, mscr, [0, 1, 2, 3, 4])
        S.barrier()
    Bo = lambda c, hh, t: B("mla_o", c, hh, t)
    with ExitStack() as s1:
        cqT = sb("mla_cq", [128, 2, L], BF16, s1)
        ckvT = sb("mla_ckv", [128, NT], BF16, s1)
        krT = sb("mla_kr", [32, NT], BF16, s1)
        hs = e.head_scratch(s1, rope=True)
        hs["hraw"] = sb("hraw", [128, 512], BF16, s1)
        Bcq = lambda t: B("mla_cq", t)
        Bckv, Bkr = B("mla_ckv"), B("mla_kr")
        with ExitStack() as sA:
            wa = sb("mla_wa", [128, 8, 416], BF16, sA)
            Bwa = B("mla_wa")
            wload(wa[:], e.mla_wa.rearrange("(k p) n -> p k n", p=128), Bwa, "mla_wa")
            for ti in range(5):
                t0, n = TL[ti]
                rd = [Bwa] + [Ba(k, ti) for k in range(8)]
                if ti < 4:
                    pq = [psA(), psA()]
                    for j in range(2):
                        for k in range(8):
                            mm(pq[j][0][:, :n], wa[:, k, j * 128:(j + 1) * 128], aT[:, k, t0:t0 + n], k == 0, k == 7, rd, [pq[j][1]])
                    ps2, Bps2 = psC()
                    sq = hs["hsq"]; Bsq = B("hsq")
                    for j in range(2):
                        act(sq[:, :n], pq[j][0][:, :n], AF.Square, [pq[j][1]], [Bsq])
                        mm(ps2[:, :n], e.ones128, sq[:, :n], j == 0, j == 1, [Bsq, e.Bcst], [Bps2])
                    rs = hs["hrs"]; Brs = B("hrs")
                    act(rs[:, :n], ps2[:, :n], AF.Ln, [Bps2, e.Bvec], [Brs], bias=e.V("eps"), scale=1.0 / 256)
                    act(rs[:, :n], rs[:, :n], AF.Exp, [Brs], [Brs], scale=-0.5)
                    for j in range(2):
                        stt("dve", cqT[:, j, t0:t0 + n], pq[j][0][:, :n], e.V("mla_gcq", j), rs[:, :n], ALU.mult, ALU.mult,
                            [pq[j][1], e.Bvec, Brs], [Bcq(ti)])
                pk, Bpk = psA()
                for k in range(8):
                    mm(pk[:, :n], wa[:, k, 256:384], aT[:, k, t0:t0 + n], k == 0, k == 7, rd, [Bpk])
                e.headnorm(ckvT[:, t0:t0 + n], [Bckv], pk[:, :n], Bpk, 128, n, e.ones128, 1.0 / 128, e.V("mla_gckv"), hs, None)
                pr, Bpr = psA()
                for k in range(8):
                    mm(pr[0:32, :n], wa[:, k, 384:416], aT[:, k, t0:t0 + n], k == 0, k == 7, rd, [Bpr])
                cp("act", krT[0:32, t0:t0 + n], pr[0:32, :n], [Bpr], [Bkr])
            S.barrier()
        with ExitStack() as sB:
            wuq = sb("mla_wuq", [128, 2, 1536], BF16, sB)
            wuk = sb("mla_wuk", [128, 16 * 96], BF16, sB)
            wuv = sb("mla_wuv", [128, 1024], BF16, sB)
            iext = sb("mla_iext", [32, 96], BF16, sB)
            perm = sb("mla_perm", [96, 96], BF16, sB)
            rcos = sb("mla_cos", [96, L], BF16, sB)
            rsin = sb("mla_sin", [96, L], BF16, sB)
            Bw = B("mla_w")
            wload(wuq[:], e.mla_wuq.rearrange("(k p) n -> p k n", p=128), Bw, "mla_w")
            wload(wuk[:], e.mla_wuk, Bw, "mla_w")
            wload(wuv[:], e.mla_wuv, Bw, "mla_w")
            wload(iext[:], e.iext_d, Bw, "mla_w")
            wload(perm[:], e.r96p, Bw, "mla_w")
            wload(rcos[:], e.r96c, Bw, "mla_w")
            wload(rsin[:], e.r96s, Bw, "mla_w")
            qh = [sb("mla_qh%d" % j, [96, L], BF16, sB) for j in range(2)]
            kh = [sb("mla_kh%d" % j, [96, NT], BF16, sB) for j in range(2)]
            vA = [sb("mla_vh%d" % j, [128, 18, 128], BF16, sB) for j in range(2)]
            ascr = e.attn_scratch(sB, 4)
            for j in range(2):
                S.op("dve", lambda en, j=j: en.memset(vA[j][:, :, 64:128], 1.0), [], [B("mla_vh", j)])
            scale = 96.0 ** -0.5
            e.set_nA(3)
            psP = e.psP

            def proj_gen(h):
                s_ = h % 2
                Bkh, Bvh = B("mla_kh", s_), B("mla_vh", s_)
                for ti in range(4):
                    t0, n = TL[ti]
                    pq, Bpq = psP()
                    for j in range(2):
                        mm(pq[0:96, :n], wuq[:, j, h * 96:(h + 1) * 96], cqT[:, j, t0:t0 + n], j == 0, j == 1, [Bw, Bcq(ti)], [Bpq])
                    for _ in e.headnorm_gen(qh[s_][:, t0:t0 + n], [B("mla_qh", s_, ti)], pq[0:96, :n], Bpq, 96, n, e.ones96, 1.0 / 96,
                                            e.V("mla_gq", 0, 0, 96), hs, (perm[:], rcos[:, t0:t0 + n], rsin[:, t0:t0 + n], [Bw]), fine=True):
                        yield
                    yield
                for ti in range(5):
                    t0, n = TL[ti]
                    pk, Bpk = psP()
                    mm(pk[0:96, :n], wuk[:, h * 96:(h + 1) * 96], ckvT[:, t0:t0 + n], True, False, [Bw, Bckv], [Bpk])
                    mm(pk[0:96, :n], iext[:, :], krT[0:32, t0:t0 + n], False, True, [Bw, Bkr], [Bpk])
                    rope = (perm[:], rcos[:, t0:t0 + n], rsin[:, t0:t0 + n], [Bw]) if ti < 4 else None
                    for _ in e.headnorm_gen(kh[s_][:, t0:t0 + n], [Bkh], pk[0:96, :n], Bpk, 96, n, e.ones96, 1.0 / 96,
                                            e.V("mla_gk", 0, 0, 96), hs, rope, fine=True):
                        yield
                    yield
                for b0 in range(0, 18, 8):
                    nb_ = min(8, 18 - b0)
                    pv, Bpv = psP()
                    for bi in range(nb_):
                        blk = b0 + bi
                        mm(pv[:, bi * 64:(bi + 1) * 64], ckvT[:, blk * 128:(blk + 1) * 128], wuv[:, h * 64:(h + 1) * 64], True, True,
                           [Bw, Bckv], [Bpv])
                    yield
                    cp("act", vA[s_][:, b0:b0 + nb_, 0:64], pv[:, 0:nb_ * 64].rearrange("p (g d) -> p g d", d=64), [Bpv], [Bvh])
                    yield

            def exhaust(g):
                if g is not None:
                    for _ in g:
                        pass

            exhaust(proj_gen(0))
            for h in range(16):
                s_ = h % 2
                Bkh, Bvh = B("mla_kh", s_), B("mla_vh", s_)
                nxt = proj_gen(h + 1) if h + 1 < 16 else None

                hcnt = [0]

                def hook(nxt=nxt, hcnt=hcnt):
                    if nxt is not None:
                        hcnt[0] += 1
                        try:
                            next(nxt)
                            if hcnt[0] % 2 == 0:
                                next(nxt)
                        except StopIteration:
                            pass
                hh = h % 2
                c = h // 2
                for qt in range(4):
                    t0, nq = TL[qt]
                    units = [(kc, 0, nq, None) for kc in range(18)]
                    e.attend(lambda a0, a1, t0=t0, s_=s_: qh[s_][0:96, t0 + a0:t0 + a1],
                             lambda kc, s_=s_: kh[s_][0:96, kc * 128:(kc + 1) * 128], 96,
                             lambda kc, s_=s_: vA[s_][:, kc, :], units, scale, aT[hh * 64:hh * 64 + 64, c, t0:t0 + nq],
                             [B("mla_qh", s_, qt)], [Bkh], [Bvh], Bo(c, hh, qt), ascr, nq, hook=hook)
                exhaust(nxt)
            e.attn_flush()
            e.set_nA(4)
            S.barrier()
    e.dbg_dump("o", aT[:], "pool")
    with ExitStack() as s3:
        e.out_proj(i, b, aT, lambda k, ti: [Bo(k, 0, ti), Bo(k, 1, ti)], e.mla_wo, s3, [0, 1, 2, 3], "mla")
        S.barrier()


_CACHE = {}


def run_layers(inputs, layers, ncores=NCORES, last_layer_idx=3):
    inp = {k: np.asarray(v) for k, v in inputs.items()}
    shared = host_shared(inp)
    key = (tuple(layers), last_layer_idx)
    if key not in _CACHE:
        _CACHE[key] = build(layers=layers, last_layer_idx=last_layer_idx)
    nc = _CACHE[key]
    in_maps = []
    for c in range(ncores):
        m = dict(shared)
        m.update(host_core(inp, c))
        in_maps.append(m)
    res = run_bass_kernel_spmd(nc, in_maps, core_ids=list(range(ncores)))
    run_layers.dbg = np.asarray(res.results[0]["dbgT"]) if "dbgT" in res.results[0] else None
    outs = [np.asarray(r["outT"]).transpose(0, 2, 1) for r in res.results]
    return np.ascontiguousarray(np.concatenate(outs, 0)).astype(np.float32)


def kernel(**inputs):
    return run_layers(inputs, (0, 1, 2, 3))
```

```python
import numpy as np
from contextlib import ExitStack
import concourse.bass as bass
import concourse.mybir as mybir
from concourse.bass_utils import run_bass_kernel_spmd

F32 = mybir.dt.float32
BF16 = mybir.dt.bfloat16
AF = mybir.ActivationFunctionType
ALU = mybir.AluOpType

D = 1024
L = 2048
LC = 256
NT = L + LC
DFF = 2816
NFC = DFF // 128
EPS = 1e-6
TL = [(0, 512), (512, 512), (1024, 512), (1536, 512), (2048, 256)]
NCORES = 8
NBPC = 2


class Buf:
    __slots__ = ("w", "r", "name")

    def __init__(self, name):
        self.name = name
        self.w = None
        self.r = {}


class Sched:
    ENGS = ("pe", "act", "dve", "pool", "sp")

    def __init__(self, nc, es):
        self.nc = nc
        self.es = es
        self.prog = {e: [] for e in self.ENGS}
        self.sem = {e: es.enter_context(nc.semaphore("s_" + e)) for e in self.ENGS}
        self.cnt = {e: 0 for e in self.ENGS}
        self.seen = {e: {} for e in self.ENGS}
        self.dsem = {}
        self.dcnt = {}
        self.bufs = {}
        self.ninst = 0

    def B(self, *key):
        b = self.bufs.get(key)
        if b is None:
            b = self.bufs[key] = Buf(key)
        return b

    def _semof(self, key):
        return self.sem[key] if key in self.sem else self.dsem[key]

    def _wait(self, eng, key, count, same_ok=False):
        if key == eng and same_ok:
            return
        if self.seen[eng].get(key, 0) >= count:
            return
        self.seen[eng][key] = count
        s = self._semof(key)
        self.prog[eng].append(lambda e, s=s, c=count: e.wait_ge(s, c))

    def op(self, eng, fn, reads=(), writes=(), inc=True):
        for b in reads:
            if b.w is not None:
                self._wait(eng, b.w[0], b.w[1], same_ok=(eng == "pe"))
        for b in writes:
            if b.w is not None:
                self._wait(eng, b.w[0], b.w[1], same_ok=(eng == "pe"))
            for k, c in b.r.items():
                self._wait(eng, k, c, same_ok=True)
        c = self.cnt[eng] + 1
        if inc:
            self.cnt[eng] = c
            s = self.sem[eng]
            self.prog[eng].append(lambda e, fn=fn, s=s: fn(e).then_inc(s, 1))
        else:
            self.prog[eng].append(lambda e, fn=fn: fn(e))
        for b in reads:
            if b.r.get(eng, 0) < c:
                b.r[eng] = c
        for b in writes:
            b.w = (eng, c)
            b.r = {}
        self.ninst += 1

    def dma(self, q, out, in_, reads=(), writes=(), key=None):
        if key not in self.dsem:
            self.dsem[key] = self.es.enter_context(self.nc.semaphore("d_%d" % len(self.dsem)))
            self.dcnt[key] = 0
        for b in reads:
            if b.w is not None:
                self._wait(q, b.w[0], b.w[1])
        for b in writes:
            if b.w is not None:
                self._wait(q, b.w[0], b.w[1])
            for k, c in b.r.items():
                self._wait(q, k, c)
        self.dcnt[key] += 16
        c = self.dcnt[key]
        s = self.dsem[key]
        self.prog[q].append(lambda e, s=s, out=out, in_=in_: e.dma_start(out=out, in_=in_).then_inc(s, 16))
        for b in reads:
            if b.r.get(key, 0) < c:
                b.r[key] = c
        for b in writes:
            b.w = (key, c)
            b.r = {}
        self.ninst += 1

    def barrier(self):
        snap = dict(self.cnt)
        dsnap = dict(self.dcnt)
        for e in self.ENGS:
            for k, c in snap.items():
                if k != e and c > 0:
                    self._wait(e, k, c)
            for k, c in dsnap.items():
                if c > 0:
                    self._wait(e, k, c)
        for b in self.bufs.values():
            b.w = None
            b.r = {}

    def finish(self):
        self.barrier()
        nc = self.nc
        with nc.Block() as block:
            @block.tensor
            def _(e):
                for f in self.prog["pe"]:
                    f(e)

            @block.scalar
            def _(e):
                for f in self.prog["act"]:
                    f(e)

            @block.vector
            def _(e):
                for f in self.prog["dve"]:
                    f(e)

            @block.gpsimd
            def _(e):
                for f in self.prog["pool"]:
                    f(e)

            @block.sync
            def _(e):
                for f in self.prog["sp"]:
                    f(e)


def _rope_tables(npart, blocks):
    t = np.arange(L)
    cos = np.ones((npart, L), np.float64)
    ssin = np.zeros((npart, L), np.float64)
    perm = np.zeros((npart, npart), np.float32)
    for (s0, nd, kind) in blocks:
        half = nd // 2
        pos = (t // 64) if kind == "row" else (t % 64)
        freqs = 10000.0 ** (-np.arange(half, dtype=np.float32).astype(np.float64) / half)
        ang = pos[None, :].astype(np.float64) * freqs[:, None]
        ang = (pos[None, :].astype(np.float32) * freqs.astype(np.float32)[:, None]).astype(np.float64)
        for i in range(half):
            d1, d2 = s0 + i, s0 + half + i
            cos[d1] = np.cos(ang[i]); cos[d2] = np.cos(ang[i])
            ssin[d1] = -np.sin(ang[i]); ssin[d2] = np.sin(ang[i])
            perm[d2, d1] = 1.0
            perm[d1, d2] = 1.0
    return cos.astype(np.float32), ssin.astype(np.float32), perm


def _vec_layout():
    off = {}
    n = 0
    def add(name, w):
        nonlocal n
        off[name] = n
        n += w
    add("b_ada", 4 * 48 * 3)
    add("g_mix", 32)
    add("g_ffn", 32)
    add("pool_scale", 8)
    add("pool_b", 8)
    add("swa_gq", 1); add("swa_gk", 1); add("swa_sink", 16)
    add("na_gq", 1); add("na_gk", 1)
    add("mla_gcq", 2); add("mla_gckv", 1); add("mla_gq", 1); add("mla_gk", 1)
    add("eps", 1); add("zero", 1)
    add("cT", 8 * 3)
    return off, n


VOFF, NV = _vec_layout()
SWA_QHEADS = [(8 * (c // 4) + (c % 4), 8 * (c // 4) + 4 + (c % 4)) for c in range(8)]


def _fm(v):
    return np.ascontiguousarray(v.reshape(-1, 128).T)


def host_shared(inp):
    f = np.float32
    sh = {}
    vecs = np.zeros((128, NV), f)
    def put(name, arr):
        arr = np.asarray(arr, f)
        vecs[:, VOFF[name]:VOFF[name] + arr.shape[1]] = arr
    put("b_ada", np.repeat(np.concatenate([_fm(inp["b_ada"][i]) for i in range(4)], 1), 3, axis=1))
    put("g_mix", np.concatenate([_fm(inp["g_mix"][i]) for i in range(4)], 1))
    put("g_ffn", np.concatenate([_fm(inp["g_ffn"][i]) for i in range(4)], 1))
    put("pool_scale", _fm(inp["pool_scale"][0]))
    put("pool_b", _fm(inp["pool_b"][0].reshape(-1)))
    put("swa_gq", np.tile(inp["swa_g_q"][0], 2)[:, None])
    put("swa_gk", np.tile(inp["swa_g_k"][0], 2)[:, None])
    put("swa_sink", np.broadcast_to(inp["swa_sink"][0][None, :], (128, 16)))
    put("na_gq", np.tile(inp["na_g_q"][0], 2)[:, None])
    put("na_gk", np.tile(inp["na_g_k"][0], 2)[:, None])
    put("mla_gcq", _fm(inp["mla_g_cq"][0]))
    put("mla_gckv", inp["mla_g_ckv"][0][:, None])
    gq = np.zeros((128, 1), f); gq[:96, 0] = inp["mla_g_q"][0]
    gk = np.zeros((128, 1), f); gk[:96, 0] = inp["mla_g_k"][0]
    put("mla_gq", gq); put("mla_gk", gk)
    vecs[:, VOFF["eps"]] = EPS
    sh["vecs_shared"] = vecs
    for k in ("w_ada", "w_gate_up", "w_down"):
        sh[k] = np.ascontiguousarray(inp[k], f)
    wqkv = inp["swa_w_qkv"][0]
    qcols = np.concatenate([np.arange(h * 64, h * 64 + 64) for pr in SWA_QHEADS for h in pr])
    sh["swa_wq"] = np.ascontiguousarray(wqkv[:, qcols])
    sh["swa_wk"] = np.ascontiguousarray(wqkv[:, 1024:1280])
    sh["swa_wv"] = np.ascontiguousarray(wqkv[:, 1280:1536])
    sh["swa_wo"] = np.ascontiguousarray(inp["swa_w_o"][0][qcols, :])
    c64, s64, p64 = _rope_tables(64, [(0, 32, "row"), (32, 32, "col")])
    sh["rope64_cos"] = np.concatenate([c64, c64], 0)
    sh["rope64_sin"] = np.concatenate([s64, s64], 0)
    pp = np.zeros((128, 128), f); pp[:64, :64] = p64; pp[64:, 64:] = p64
    sh["rope64_perm"] = pp
    sh["na_wqkv"] = np.ascontiguousarray(inp["na_w_qkv"][0])
    sh["na_wo"] = np.ascontiguousarray(inp["na_w_o"][0])
    sh["na_bias"] = _na_bias_tables(inp["na_rpb"][0])
    sh["pool_w"] = np.ascontiguousarray(inp["pool_w"][0])
    sh["mla_wa"] = np.ascontiguousarray(inp["mla_w_a"][0])
    sh["mla_wuq"] = np.ascontiguousarray(inp["mla_w_uq"][0])
    wukv = inp["mla_w_ukv"][0].reshape(128, 16, 128)
    wk = np.zeros((128, 16, 96), f); wk[:, :, :64] = wukv[:, :, :64]
    sh["mla_wuk"] = wk.reshape(128, 16 * 96)
    sh["mla_wuv"] = np.ascontiguousarray(wukv[:, :, 64:]).reshape(128, 1024)
    sh["mla_wo"] = np.ascontiguousarray(inp["mla_w_o"][0])
    c96, s96, p96 = _rope_tables(96, [(64, 16, "row"), (80, 16, "col")])
    sh["rope96_cos"] = c96; sh["rope96_sin"] = s96; sh["rope96_perm"] = p96
    iext = np.zeros((32, 96), f); iext[np.arange(32), 64 + np.arange(32)] = 1.0
    sh["mla_iext"] = iext
    consts = np.zeros((128, 512), f)
    consts[:, 0:128] = 1.0
    consts[:64, 128:192] = 1.0; consts[64:, 192:256] = 1.0
    consts[:96, 256:352] = 1.0
    sh["consts"] = consts
    kk = np.arange(128)[:, None]; qq = np.arange(128)[None, :]
    m_prev = ((kk + 128 - qq) <= 128).astype(f)
    m_same = np.ones((128, 128), f)
    m_next = ((qq + 128 - kk) <= 128).astype(f)
    sh["swa_mask"] = np.concatenate([m_prev, m_same, m_next], 1)
    pe = np.zeros((128, 4, 2, 2, 16), f)
    for gi, w in enumerate((2, 4, 8, 16)):
        for li, Lx in enumerate((L, LC)):
            t = np.arange(Lx)
            lo = np.clip(t - w // 2, 0, Lx); hi = np.clip(t - w // 2 + w, 0, Lx)
            inv = (1.0 / (hi - lo).astype(f)).astype(f)
            pe[:, gi, li, 0, :w // 2] = inv[:w // 2]
            if w // 2 - 1 > 0:
                pe[:, gi, li, 1, :w // 2 - 1] = inv[Lx - (w // 2 - 1):]
    sh["pool_edge"] = pe.reshape(128, -1)
    return sh


NA_NEG = -30000.0
NA_EI = 24
NA_EF = 16
NA_ET = NA_EI + NA_EF


def _na_bias_tables(rpb):
    H = 16
    tab = np.full((128, H, NA_ET, 64), NA_NEG, np.float32)
    kc = np.arange(64)[:, None]
    qc = np.arange(64)[None, :]
    cs = np.clip(qc - 8, 0, 48)
    colvalid = (kc >= cs) & (kc < cs + 16)
    dcidx = np.clip(kc - qc, -15, 15) + 15
    for half in range(2):
        for sec, (n_e, e0, lo, hi) in enumerate(((NA_EI, -4, -4, 3), (NA_EF, 0, -7, 7))):
            base = 0 if sec == 0 else NA_EI
            for idx in range(n_e):
                e = idx + e0 - half
                dr = 7 - e
                if dr < lo or dr > hi:
                    continue
                vals = rpb[:, dr + 7, :][:, dcidx]
                vals = np.where(colvalid[None], vals, NA_NEG)
                tab[half * 64:(half + 1) * 64, :, base + idx, :] = vals.transpose(1, 0, 2)
    return tab.reshape(128, H * NA_ET * 64)


def host_core(inp, core):
    b0 = core * NBPC
    d = {}
    d["xT"] = np.ascontiguousarray(inp["x"][b0:b0 + NBPC].transpose(0, 2, 1))
    d["ctxT"] = np.ascontiguousarray(inp["ctx"][b0:b0 + NBPC].transpose(0, 2, 1))
    cc = np.stack([inp["c"][b0], inp["c"][b0 + 1], inp["c_ctx"]], 1)
    d["cT"] = np.ascontiguousarray(cc.reshape(8, 128, 3).transpose(1, 0, 2)).reshape(128, 24).astype(np.float32)
    return d


def build(layers=(0, 1, 2, 3), nb=NBPC, last_layer_idx=3, parts="ABCD", dbg=None):
    nc = bass.Bass("TRN2", target_bir_lowering=False)
    dr = {}
    def din(name, shape):
        dr[name] = nc.dram_tensor(name, list(shape), F32, kind="ExternalInput").ap()
        return dr[name]
    xT = din("xT", [NBPC, D, L]); ctxT = din("ctxT", [NBPC, D, LC]); cT_d = din("cT", [128, 24])
    vecs_d = din("vecs_shared", [128, NV])
    w_ada = din("w_ada", [4, D, 6 * D]); w_gu = din("w_gate_up", [4, D, 2 * DFF]); w_dn = din("w_down", [4, DFF, D])
    swa_wq = din("swa_wq", [D, 1024]); swa_wk = din("swa_wk", [D, 256]); swa_wv = din("swa_wv", [D, 256]); swa_wo = din("swa_wo", [D, D])
    r64c = din("rope64_cos", [128, L]); r64s = din("rope64_sin", [128, L]); r64p = din("rope64_perm", [128, 128])
    na_wqkv = din("na_wqkv", [D, 3072]); na_wo = din("na_wo", [D, D]); na_bias = din("na_bias", [128, 16 * NA_ET * 64])
    pool_w = din("pool_w", [4, 256, 256])
    mla_wa = din("mla_wa", [D, 416]); mla_wuq = din("mla_wuq", [256, 1536]); mla_wuk = din("mla_wuk", [128, 16 * 96])
    mla_wuv = din("mla_wuv", [128, 1024]); mla_wo = din("mla_wo", [D, D])
    r96c = din("rope96_cos", [96, L]); r96s = din("rope96_sin", [96, L]); r96p = din("rope96_perm", [96, 96])
    iext_d = din("mla_iext", [32, 96]); consts_d = din("consts", [128, 512]); swa_mask_d = din("swa_mask", [128, 384])
    pool_edge_d = din("pool_edge", [128, 4 * 2 * 2 * 16])
    outT = nc.dram_tensor("outT", [NBPC, D, L], F32, kind="ExternalOutput").ap()
    dbgT = nc.dram_tensor("dbgT", [128, 8, NT], F32, kind="ExternalOutput").ap() if dbg else None

    es = ExitStack()
    with es:
        S = Sched(nc, es)
        B = S.B

        uid = [0]

        def sb(name, shape, dt, stack=es):
            uid[0] += 1
            return stack.enter_context(nc.sbuf_tensor("%s_%d" % (name, uid[0]), list(shape), dt))

        hT = sb("hT", [128, 8, NT], F32)
        vec = sb("vec", [128, NV], F32)
        cst = sb("cst", [128, 512], BF16)
        modT = sb("modT", [128, 4 * 48 * 3], F32)
        gsT = sb("gsT", [128, 4 * 2 * 8 * 3], F32)
        sinkE = sb("sinkE", [128, 16], F32)
        psb = [es.enter_context(nc.psum_tensor("ps%d" % i, [128, 512], F32)) for i in range(8)]
        PSB = [B("ps", i) for i in range(8)]
        rr = {"A": 0, "C": 0, "Bk": 0}

        rr["nA"] = 4

        def psA():
            i = rr["A"] % rr["nA"]; rr["A"] = (i + 1) % rr["nA"]
            return psb[i], PSB[i]

        def psP():
            return psb[3], PSB[3]

        def set_nA(n_):
            rr["nA"] = n_; rr["A"] = 0

        def psAcc():
            i = 4 + rr["Bk"]; rr["Bk"] = (rr["Bk"] + 1) % 2
            return psb[i], PSB[i]

        def psC():
            i = 6 + rr["C"]; rr["C"] = (rr["C"] + 1) % 2
            return psb[i], PSB[i]

        ones128 = cst[:, 0:128]
        bd64 = cst[:, 128:256]
        ones96 = cst[0:96, 256:352]

        def V(name, j=0, p0=0, p1=128):
            o = VOFF[name] + j
            return vec[p0:p1, o:o + 1]

        def modcol(i, ch, col):
            o = (i * 48 + ch) * 3 + col
            return modT[:, o:o + 1]

        def gscol(i, which, k, col):
            o = ((i * 2 + which) * 8 + k) * 3 + col
            return gsT[:, o:o + 1]

        def act(out, in_, func, reads, writes, bias=None, scale=None):
            kw = {}
            if bias is not None:
                kw["bias"] = bias
            if scale is not None:
                kw["scale"] = scale
            S.op("act", lambda e: e.activation(out=out, in_=in_, func=func, **kw), reads, writes)

        def mm(out, lhsT, rhs, start, stop, reads, writes):
            S.op("pe", lambda e: e.matmul(out, lhsT, rhs, start=start, stop=stop), reads, writes, inc=True)

        def tt(eng, out, in0, in1, op, reads, writes):
            S.op(eng, lambda e: e.tensor_tensor(out=out, in0=in0, in1=in1, op=op), reads, writes)

        def ts(eng, out, in0, s1, s2, op0, op1, reads, writes):
            if op1 is None:
                S.op(eng, lambda e: e.tensor_scalar(out=out, in0=in0, scalar1=s1, scalar2=None, op0=op0), reads, writes)
            else:
                S.op(eng, lambda e: e.tensor_scalar(out=out, in0=in0, scalar1=s1, scalar2=s2, op0=op0, op1=op1), reads, writes)

        def stt(eng, out, in0, scalar, in1, op0, op1, reads, writes):
            S.op(eng, lambda e: e.scalar_tensor_tensor(out=out, in0=in0, scalar=scalar, in1=in1, op0=op0, op1=op1), reads, writes)

        def cp(eng, out, in_, reads, writes):
            if eng == "act":
                S.op(eng, lambda e: e.activation(out=out, in_=in_, func=AF.Identity), reads, writes)
            else:
                S.op(eng, lambda e: e.tensor_copy(out=out, in_=in_), reads, writes)

        def wload(dst, src, buf, key):
            S.dma("pool", dst, src, writes=[buf], key=key)

        Bvec = B("vec"); Bcst = B("cst")
        Bmod = lambda i_: B("mod", i_)
        Bgs = lambda i_: B("gs", i_)
        S.dma("sp", vec[:, 0:VOFF["cT"]], vecs_d[:, 0:VOFF["cT"]], writes=[Bvec], key="vec")
        S.dma("sp", vec[:, VOFF["cT"]:NV], cT_d, writes=[Bvec], key="vec")
        wload(cst[:], consts_d, Bcst, "cst")
        condT = sb("condT", [128, 24], BF16)
        Bcond = B("cond")
        act(condT[:], vec[:, VOFF["cT"]:VOFF["cT"] + 24], AF.Silu, [Bvec], [Bcond])
        cond3 = condT[:].rearrange("p (k c) -> p k c", c=3)
        act(sinkE[:], vec[:, VOFF["swa_sink"]:VOFF["swa_sink"] + 16], AF.Exp, [Bvec], [B("sinkE")])

        def mod_gen(i, wab):
            for cb in range(12):
                w = wab[cb % 2]; Bw = B("wab", cb % 2)
                wload(w[:], w_ada[i, :, cb * 512:(cb + 1) * 512].rearrange("(k p) n -> p k n", p=128), Bw, "wab%d" % (cb % 2))
                yield
                pt, Bp = psA()
                for oc in range(4):
                    for k in range(8):
                        mm(pt[:, oc * 3:oc * 3 + 3], w[:, k, oc * 128:(oc + 1) * 128], cond3[:, k, :], k == 0, k == 7,
                           [Bw, Bcond], [Bp])
                o = (i * 48 + cb * 4) * 3
                bb = vec[:, VOFF["b_ada"] + o: VOFF["b_ada"] + o + 12]
                tt("dve", modT[:, o:o + 12], pt[:, 0:12], bb, ALU.add, [Bp, Bvec], [Bmod(i)])
            for which, (gname, scj) in enumerate((("g_mix", 1), ("g_ffn", 4))):
                for col in range(3):
                    o_in = (i * 48 + scj * 8) * 3 + col
                    sc_view = modT[:, o_in:o_in + 22:3]
                    o_out = ((i * 2 + which) * 8) * 3 + col
                    gs_view = gsT[:, o_out:o_out + 22:3]
                    g_view = vec[:, VOFF[gname] + i * 8: VOFF[gname] + i * 8 + 8]
                    stt("dve", gs_view, sc_view, 1.0, g_view, ALU.add, ALU.mult, [Bmod(i), Bvec], [Bgs(i)])
            yield

        with ExitStack() as ps_:
            wab0 = [sb("wab%d" % j, [128, 8, 512], BF16, ps_) for j in range(2)]
            if len(layers) > 0:
                for _ in mod_gen(layers[0], wab0):
                    pass
            S.barrier()

        Bh = lambda k, t: B("h", k, t)

        def modulate(i, which, bcol, aT, Ba, scr, tiles):
            shj = 0 if which == 0 else 3
            for ti in tiles:
                t0, n = TL[ti]
                col = 2 if ti == 4 else bcol
                pt, Bp = psC()
                for k in range(8):
                    sq = scr["sq"][k % 2]; Bsq = B("sq", k % 2)
                    tt("pool", sq[:, :n], hT[:, k, t0:t0 + n], hT[:, k, t0:t0 + n], ALU.mult, [Bh(k, ti)], [Bsq])
                    mm(pt[:, :n], ones128, sq[:, :n], k == 0, k == 7, [Bsq, Bcst], [Bp])
                rstd = scr["rstd"]; Br = B("rstd")
                act(rstd[:, :n], pt[:, :n], AF.Ln, [Bp, Bvec], [Br], bias=V("eps"), scale=1.0 / D)
                act(rstd[:, :n], rstd[:, :n], AF.Exp, [Br], [Br], scale=-0.5)
                for k in range(8):
                    tmp = scr["tmp"][k % 2]; Bt = B("mtmp", k % 2)
                    tt("dve", tmp[:, :n], hT[:, k, t0:t0 + n], rstd[:, :n], ALU.mult, [Bh(k, ti), Br], [Bt])
                    act(aT[:, k, t0:t0 + n], tmp[:, :n], AF.Identity, [Bt, Bgs(i), Bmod(i)], [Ba(k, ti)],
                        bias=modcol(i, shj * 8 + k, col), scale=gscol(i, which, k, col))

        def ffn(i, bcol, aT, Ba, st, tiles, hook=None):
            G = 2
            wg = [sb("ffn_wg%d" % j, [128, 8, 2 * G * 128], BF16, st) for j in range(2)]
            wd = [sb("ffn_wd%d" % j, [128, G, D], BF16, st) for j in range(2)]
            actb = [sb("ffn_act%d" % j, [128, G, 512], BF16, st) for j in range(2)]
            sgb = [sb("ffn_sg%d" % j, [128, 512], BF16, st) for j in range(2)]
            ngroups = NFC // G
            na = 0
            for g in range(ngroups):
                s = g % 2
                Bwg = B("ffn_wg", s); Bwd = B("ffn_wd", s)
                f0 = g * G * 128
                wload(wg[s][:, :, 0:G * 128], w_gu[i, :, f0:f0 + G * 128].rearrange("(k p) n -> p k n", p=128), Bwg, "ffn_wg%d" % s)
                wload(wg[s][:, :, G * 128:2 * G * 128], w_gu[i, :, DFF + f0:DFF + f0 + G * 128].rearrange("(k p) n -> p k n", p=128), Bwg, "ffn_wg%d" % s)
                wload(wd[s][:], w_dn[i, f0:f0 + G * 128, :].rearrange("(g p) n -> p g n", p=128), Bwd, "ffn_wd%d" % s)
                if hook is not None:
                    hook()
                    if g % 4 == 3:
                        hook()
                for ti in tiles:
                    t0, n = TL[ti]
                    col = 2 if ti == 4 else bcol
                    ab = actb[na % 2]; Bab = B("ffn_act", na % 2); na += 1
                    for fi in range(G):
                        pg, Bpg = psA()
                        for k in range(8):
                            mm(pg[:, :n], wg[s][:, k, fi * 128:(fi + 1) * 128], aT[:, k, t0:t0 + n], k == 0, k == 7, [Bwg, Ba(k, ti)], [Bpg])
                        pu, Bpu = psA()
                        for k in range(8):
                            mm(pu[:, :n], wg[s][:, k, (G + fi) * 128:(G + fi + 1) * 128], aT[:, k, t0:t0 + n], k == 0, k == 7, [Bwg, Ba(k, ti)], [Bpu])
                        sg = sgb[fi % 2]; Bsg = B("ffn_sg", fi % 2)
                        act(sg[:, :n], pg[:, :n], AF.Silu, [Bpg], [Bsg])
                        tt("dve", ab[:, fi, :n], pu[:, :n], sg[:, :n], ALU.mult, [Bpu, Bsg], [Bab])
                    for o in range(8):
                        po, Bpo = psAcc() if o % 2 == 0 else psC()
                        for fi in range(G):
                            mm(po[:, :n], wd[s][:, fi, o * 128:(o + 1) * 128], ab[:, fi, :n], fi == 0, fi == G - 1, [Bwd, Bab], [Bpo])
                        stt("dve", hT[:, o, t0:t0 + n], po[:, :n], modcol(i, 5 * 8 + o, col), hT[:, o, t0:t0 + n], ALU.mult, ALU.add,
                            [Bpo, Bmod(i), Bh(o, ti)], [Bh(o, ti)])

        def out_proj(i, bcol, oT, Bo, w_dram, st, tiles, name):
            wo = sb(name + "_wo", [128, 8, D], BF16, st)
            Bwo = B(name + "_wo")
            for half in range(2):
                wload(wo[:, half * 4:(half + 1) * 4, :], w_dram[half * 512:(half + 1) * 512, :].rearrange("(k p) n -> p k n", p=128), Bwo, name + "_wo")
            for ti in tiles:
                t0, n = TL[ti]
                col = 2 if ti == 4 else bcol
                for o in range(8):
                    po, Bpo = psA()
                    for k in range(8):
                        mm(po[:, :n], wo[:, k, o * 128:(o + 1) * 128], oT[:, k, t0:t0 + n], k == 0, k == 7, [Bwo] + Bo(k, ti), [Bpo])
                    stt("dve", hT[:, o, t0:t0 + n], po[:, :n], modcol(i, 2 * 8 + o, col), hT[:, o, t0:t0 + n], ALU.mult, ALU.add,
                        [Bpo, Bmod(i), Bh(o, ti)], [Bh(o, ti)])

        def headnorm_gen(out_ap, Bout, praw, Bpraw, M, n, ones_lhsT, invD, gain, scr, rope=None):
            sq = scr["hsq"]; Bsq = B("hsq")
            act(sq[0:M, :n], praw, AF.Square, [Bpraw], [Bsq])
            yield
            ps2, Bps2 = psC()
            mm(ps2[0:M, :n], ones_lhsT, sq[0:M, :n], True, True, [Bsq, Bcst], [Bps2])
            rs = scr["hrs"]; Brs = B("hrs")
            act(rs[0:M, :n], ps2[0:M, :n], AF.Ln, [Bps2, Bvec], [Brs], bias=V("eps", 0, 0, M), scale=invD)
            act(rs[0:M, :n], rs[0:M, :n], AF.Exp, [Brs], [Brs], scale=-0.5)
            if rope is None:
                stt("dve", out_ap, praw, gain, rs[0:M, :n], ALU.mult, ALU.mult, [Bpraw, Bvec, Brs], Bout)
                return
            perm, cos_ap, sin_ap, Brope = rope
            qn = scr["hqn"]; Bqn = B("hqn")
            stt("dve", qn[0:M, :n], praw, gain, rs[0:M, :n], ALU.mult, ALU.mult, [Bpraw, Bvec, Brs], [Bqn])
            yield
            ps3, Bps3 = psC()
            mm(ps3[0:M, :n], perm, qn[0:M, :n], True, True, [Bqn] + Brope, [Bps3])
            t1 = scr["ht1"]; Bt1 = B("ht1")
            tt("pool", t1[0:M, :n], qn[0:M, :n], cos_ap, ALU.mult, [Bqn] + Brope, [Bt1])
            t2 = scr["ht2"]; Bt2 = B("ht2")
            tt("dve", t2[0:M, :n], ps3[0:M, :n], sin_ap, ALU.mult, [Bps3] + Brope, [Bt2])
            tt("pool", out_ap, t1[0:M, :n], t2[0:M, :n], ALU.add, [Bt1, Bt2], Bout)

        def headnorm(*a_, **k_):
            for _ in headnorm_gen(*a_, **k_):
                pass

        class HNPipe:
            def __init__(self):
                self.g = []

            def push(self, gen):
                self.step()
                next(gen)
                self.g.append(gen)

            def step(self):
                for g in list(self.g):
                    try:
                        next(g)
                    except StopIteration:
                        self.g.remove(g)

            def drain(self):
                while self.g:
                    self.step()

        def head_scratch(st, rope=True):
            d = {"hsq": sb("hsq", [128, 512], BF16, st), "hrs": sb("hrs", [128, 512], F32, st)}
            if rope:
                d.update({"hqn": sb("hqn", [128, 512], BF16, st), "ht1": sb("ht1", [128, 512], F32, st),
                          "ht2": sb("ht2", [128, 512], F32, st)})
            return d

        pend = []
        LA = 2

        def attn_flush(keep=0):
            while len(pend) > keep:
                pend.pop(0)()

        def attend(qT_ap_fn, kT_ap_fn, Kdim, vaug_fn, units, scale, out_ap, reads_q, reads_k, reads_v, Bout, scr,
                   nq, sink_ap=None, hook=None, la=2):
            po, Bpo = psAcc()
            nu = len(units)
            for ui, (kc, c0, c1, mspec) in enumerate(units):
                pS, BpS = psA()
                mm(pS[:, 0:c1 - c0], kT_ap_fn(kc), qT_ap_fn(c0, c1), True, True, reads_q + reads_k, [BpS])
                pi = scr["pi"]; scr["pi"] = (pi + 1) % len(scr["P"])
                Pt = scr["P"][pi]; BP = B("P", pi)
                act(Pt[:, 0:c1 - c0], pS[:, 0:c1 - c0], AF.Exp, [BpS], [BP], scale=scale)
                if mspec:
                    for (cc0, cc1, m_ap, mb) in mspec:
                        tt("dve", Pt[:, cc0 - c0:cc1 - c0], Pt[:, cc0 - c0:cc1 - c0], m_ap, ALU.mult, [BP] + mb, [BP])
                pend.append(lambda kc=kc, c0=c0, c1=c1, Pt=Pt, BP=BP, ui=ui:
                            mm(po[:, c0:c1], vaug_fn(kc), Pt[:, 0:c1 - c0], ui == 0, ui == nu - 1, [BP] + reads_v, [Bpo]))
                attn_flush(la)
                if hook is not None and ui % 2 == 1:
                    hook()

            def epilogue():
                ri = scr["ri"]; scr["ri"] = (ri + 1) % len(scr["rc"])
                rc = scr["rc"][ri]; Brc = B("rc", ri)
                if sink_ap is not None:
                    act(rc[64:128, :nq], po[64:128, :nq], AF.Ln, [Bpo, B("sinkE")], [Brc], bias=sink_ap)
                else:
                    act(rc[64:128, :nq], po[64:128, :nq], AF.Ln, [Bpo], [Brc])
                act(rc[64:128, :nq], rc[64:128, :nq], AF.Exp, [Brc], [Brc], scale=-1.0)
                tt("dve", out_ap, po[0:64, :nq], rc[64:128, :nq], ALU.mult, [Bpo, Brc], [Bout])
            pend.append(epilogue)

        def attn_scratch(st, nP=4):
            return {"P": [sb("Pt%d" % j, [128, 512], BF16, st) for j in range(nP)], "pi": 0,
                    "rc": [sb("rc%d" % j, [128, 512], F32, st) for j in range(1)], "ri": 0}

        for b in range(nb):
            for k in range(8):
                S.dma("sp", hT[:, k, 0:L], xT[b, k * 128:(k + 1) * 128, :], writes=[Bh(k, t) for t in range(4)], key="hx%d" % k)
                S.dma("sp", hT[:, k, L:NT], ctxT[b, k * 128:(k + 1) * 128, :], writes=[Bh(k, 4)], key="hc%d" % k)
            for i in layers:
                need_ctx = i < last_layer_idx
                tiles = [0, 1, 2, 3, 4] if need_ctx else [0, 1, 2, 3]
                Ba = lambda k, t: B("a", k, t)

                def mk_mscr(stk):
                    return {"sq": [sb("msq%d" % j, [128, 512], BF16, stk) for j in range(2)],
                            "rstd": sb("mrstd", [128, 512], F32, stk),
                            "tmp": [sb("mtmp%d" % j, [128, 512], F32, stk) for j in range(2)]}

                def dbg_dump(tag, src, q):
                    if dbg == tag and b == 0:
                        S.barrier()
                        S.dma(q, dbgT, src, writes=[B("dbg")], key="dbg")
                        S.barrier()

                with ExitStack() as st2:
                    if "B" in parts:
                        ENV = dict(locals())
                        ENV.update(dict(S=S, nc=nc, sb=sb, st=st2))
                        if i == 2:
                            pool_layer(ENV)
                        elif i == 0:
                            swa_layer(ENV)
                        elif i == 1:
                            na_layer(ENV)
                        else:
                            mla_layer(ENV)
                    S.barrier()
                dbg_dump("h1", hT[:], "sp")
                with ExitStack() as st3:
                    aT = sb("aT", [128, 8, NT], BF16, st3)
                    mscr = mk_mscr(st3)
                    if "C" in parts:
                        modulate(i, 1, b, aT, Ba, mscr, tiles)
                    dbg_dump("a2", aT[:], "pool")
                    if "D" in parts:
                        li = list(layers).index(i)
                        mg = None
                        if b == 0 and li + 1 < len(layers):
                            wab1 = [sb("wabf%d" % j, [128, 8, 512], BF16, st3) for j in range(2)]
                            mg = mod_gen(layers[li + 1], wab1)

                        def mhook(mg=mg):
                            if mg is not None:
                                try:
                                    next(mg)
                                except StopIteration:
                                    pass
                        ffn(i, b, aT, Ba, st3, tiles, hook=mhook if mg is not None else None)
                        if mg is not None:
                            for _ in mg:
                                pass
                    S.barrier()
            for k in range(8):
                S.dma("sp", outT[b, k * 128:(k + 1) * 128, :], hT[:, k, 0:L], reads=[Bh(k, t) for t in range(4)],
                      writes=[B("out", b, k)], key="ost%d" % k)
        S.finish()
        build.ninst = S.ninst
    return nc


class NS:
    def __init__(self, d):
        self.__dict__.update(d)


def pool_layer(E):
    S, nc, sb, st = E["S"], E["nc"], E["sb"], E["st"]
    B = S.B
    i, b, Ba, hT = E["i"], E["b"], E["Ba"], E["hT"]
    vec, V, Bvec, Bmod, modcol = E["vec"], E["V"], E["Bvec"], E["Bmod"], E["modcol"]
    act, mm, tt, ts, stt, cp, wload, psA = E["act"], E["mm"], E["tt"], E["ts"], E["stt"], E["cp"], E["wload"], E["psA"]
    need_ctx, pool_w, pool_edge_d, Bh = E["need_ctx"], E["pool_w"], E["pool_edge_d"], E["Bh"]
    aT = sb("aT", [128, 8, NT], BF16, st)
    mscr = E["mk_mscr"](st)
    if "A" in E["parts"]:
        E["modulate"](i, 0, b, aT, Ba, mscr, [0, 1, 2, 3, 4])
    E["dbg_dump"]("a1", aT[:], "pool")
    pT = sb("poolP", [128, 8, NT], BF16, st)
    Bp = lambda k, t: B("poolP", k, t)
    wp = sb("poolW", [128, 4, 2, 256], BF16, st)
    Bwp = B("poolW")
    wload(wp[:].rearrange("p g k n -> p (g k) n"), pool_w.rearrange("g (k p) n -> p (g k) n", p=128), Bwp, "poolW")
    edge = sb("poolE", [128, 4 * 2 * 2 * 16], F32, st)
    Bedge = B("poolE")
    S.dma("sp", edge[:], pool_edge_d, writes=[Bedge], key="poolE")
    PADL = 8
    WLEN = PADL + L + 16
    bufA = sb("poolA", [128, WLEN], F32, st)
    bufB = sb("poolB", [128, WLEN], F32, st)
    BA, BB = B("poolA"), B("poolB")
    sT = sb("poolS", [128, 8 * 3], F32, st)
    Bs = B("poolS")
    for col in range(3):
        o_in = (i * 48 + 2 * 8) * 3 + col
        tt("dve", sT[:, col:col + 22:3], E["modT"][:, o_in:o_in + 22:3], vec[:, VOFF["pool_scale"]:VOFF["pool_scale"] + 8], ALU.mult,
           [Bmod(i), Bvec], [Bs])
    segs = [(0, L, 0, [0, 1, 2, 3])] + ([(L, LC, 1, [4])] if True else [])
    for k in range(8):
        gi = k // 2
        w = (2, 4, 8, 16)[gi]
        for (s0, Ln, li, tls) in segs:
            allB = [Ba(k, t) for t in tls]
            S.op("dve", lambda e: e.memset(bufA[:, 0:PADL], 0.0), [], [BA])
            S.op("dve", lambda e, Ln=Ln: e.memset(bufA[:, PADL + Ln:PADL + Ln + 16], 0.0), [], [BA])
            cp("dve", bufA[:, PADL:PADL + Ln], aT[:, k, s0:s0 + Ln], allB, [BA])
            src, Bsrc, dst, Bdst = bufA, BA, bufB, BB
            step = 1
            n_valid = PADL + Ln + 16
            while step < w:
                n_valid -= step
                tt("dve", dst[:, 0:n_valid], src[:, 0:n_valid], src[:, step:step + n_valid], ALU.add, [Bsrc], [Bdst])
                src, Bsrc, dst, Bdst = dst, Bdst, src, Bsrc
                step *= 2
            o0 = PADL - w // 2
            stt("dve", pT[:, k, s0:s0 + Ln], src[:, o0:o0 + Ln], 1.0 / w, aT[:, k, s0:s0 + Ln], ALU.mult, ALU.subtract, [Bsrc] + allB,
                [Bp(k, t) for t in tls])
            eo = ((gi * 2 + li) * 2) * 16
            ne = w // 2
            tt("dve", dst[:, 0:ne], src[:, o0:o0 + ne], edge[:, eo:eo + ne], ALU.mult, [Bsrc, Bedge], [Bdst])
            tt("dve", pT[:, k, s0:s0 + ne], dst[:, 0:ne], aT[:, k, s0:s0 + ne], ALU.subtract, [Bdst, Ba(k, tls[0])], [Bp(k, tls[0])])
            ne2 = w // 2 - 1
            if ne2 > 0:
                tt("dve", dst[:, 16:16 + ne2], src[:, o0 + Ln - ne2:o0 + Ln], edge[:, eo + 16:eo + 16 + ne2], ALU.mult, [Bsrc, Bedge], [Bdst])
                tt("dve", pT[:, k, s0 + Ln - ne2:s0 + Ln], dst[:, 16:16 + ne2], aT[:, k, s0 + Ln - ne2:s0 + Ln], ALU.subtract,
                   [Bdst, Ba(k, tls[-1])], [Bp(k, tls[-1])])
    tmpb = [sb("poolT%d" % j, [128, 512], F32, st) for j in range(2)]
    nn = 0
    for ti in ([0, 1, 2, 3, 4] if need_ctx else [0, 1, 2, 3]):
        t0, n = TL[ti]
        col = 2 if ti == 4 else b
        for o in range(8):
            g = o // 2
            po, Bpo = psA()
            for kk in range(2):
                mm(po[:, :n], wp[:, g, kk, (o % 2) * 128:(o % 2 + 1) * 128], pT[:, 2 * g + kk, t0:t0 + n], kk == 0, kk == 1,
                   [Bwp, Bp(2 * g + kk, ti)], [Bpo])
            tb = tmpb[nn % 2]; Btb = B("poolT", nn % 2); nn += 1
            act(tb[:, :n], po[:, :n], AF.Identity, [Bpo, Bvec], [Btb], bias=V("pool_b", o))
            stt("dve", hT[:, o, t0:t0 + n], tb[:, :n], sT[:, o * 3 + col:o * 3 + col + 1], hT[:, o, t0:t0 + n], ALU.mult, ALU.add,
                [Btb, Bs, Bh(o, ti)], [Bh(o, ti)])


def swa_layer(E):
    e = NS(E)
    S, nc, sb, st = e.S, e.nc, e.sb, e.st
    B = S.B
    i, b = e.i, e.b
    act, mm, tt, stt, cp, wload, psA = e.act, e.mm, e.tt, e.stt, e.cp, e.wload, e.psA
    qT = sb("swa_q", [128, 8, NT], BF16, st)
    kT = sb("swa_k", [128, 2, NT], BF16, st)
    vA = sb("swa_v", [128, 18, 4, 128], BF16, st)
    Bq = lambda c, hh, t: B("swa_q", c, hh, t)
    Bk = lambda m: B("swa_k", m)
    Bv = B("swa_v")
    Ba = e.Ba
    with ExitStack() as s1:
        aT = sb("aT", [128, 8, NT], BF16, s1)
        with ExitStack() as s0:
            mscr = e.mk_mscr(s0)
            e.modulate(i, 0, b, aT, Ba, mscr, [0, 1, 2, 3, 4])
            S.barrier()
        e.dbg_dump("a1", aT[:], "pool")
        hs = e.head_scratch(s1)
        wst = [sb("swa_ws%d" % j, [128, 8, 128], BF16, s1) for j in range(2)]
        ropeT = sb("swa_rope", [128, 2, L], BF16, s1)
        perm = sb("swa_perm", [128, 128], BF16, s1)
        Bperm = B("swa_perm")
        wload(perm[:], e.r64p, Bperm, "swa_perm")
        wload(ropeT[:, 0, :], e.r64c, Bperm, "swa_perm")
        wload(ropeT[:, 1, :], e.r64s, Bperm, "swa_perm")
        S.op("dve", lambda en: en.memset(vA[:, :, :, 64:128], 1.0), [], [Bv])
        nw = 0
        pipe = e.HNPipe()
        plist = [("k", 0), ("k", 1)] + [("q", c) for c in range(8)]
        for (kind, c) in plist:
            w = wst[nw % 2]; Bw = B("swa_ws", nw % 2)
            src = e.swa_wk if kind == "k" else e.swa_wq
            wload(w[:], src[:, c * 128:(c + 1) * 128].rearrange("(k p) n -> p k n", p=128), Bw, "swa_ws%d" % (nw % 2))
            nw += 1
            for ti in range(5):
                t0, n = TL[ti]
                pr, Bpr = psA()
                for k in range(8):
                    mm(pr[:, :n], w[:, k, :], aT[:, k, t0:t0 + n], k == 0, k == 7, [Bw, Ba(k, ti)], [Bpr])
                rope = None
                if ti < 4:
                    rope = (perm[:], ropeT[:, 0, t0:t0 + n], ropeT[:, 1, t0:t0 + n], [Bperm])
                if kind == "k":
                    out_ap = kT[:, c, t0:t0 + n]; Bo = [Bk(c)]; gain = e.V("swa_gk")
                else:
                    out_ap = qT[:, c, t0:t0 + n]; Bo = [Bq(c, 0, ti), Bq(c, 1, ti)]; gain = e.V("swa_gq")
                pipe.push(e.headnorm_gen(out_ap, Bo, pr[:, :n], Bpr, 128, n, e.bd64, 1.0 / 64, gain, hs, rope))
        pipe.drain()
        for half in range(2):
            w = wst[nw % 2]; Bw = B("swa_ws", nw % 2)
            wload(w[:], e.swa_wv[:, half * 128:(half + 1) * 128].rearrange("(k p) n -> p k n", p=128), Bw, "swa_ws%d" % (nw % 2))
            nw += 1
            for blk in range(18):
                ti = min(blk // 4, 4)
                pv, Bpv = psA()
                for k in range(8):
                    mm(pv[:, 0:128], aT[:, k, blk * 128:(blk + 1) * 128], w[:, k, :], k == 0, k == 7, [Bw, Ba(k, ti)], [Bpv])
                cp("act" if blk % 2 else "dve", vA[:, blk, 2 * half:2 * half + 2, 0:64], pv[:, 0:128].rearrange("p (g d) -> p g d", d=64),
                   [Bpv], [Bv])
        S.barrier()
    with ExitStack() as s2:
        ascr = e.attn_scratch(s2, 4)
        mask = sb("swa_mask", [128, 384], BF16, s2)
        Bmask = B("swa_mask")
        wload(mask[:], e.swa_mask_d, Bmask, "swa_mask")
        for c in range(8):
            m = c // 4
            for hh in range(2):
                head = SWA_QHEADS[c][hh]
                g = 2 * m + hh
                p0, p1 = hh * 64, hh * 64 + 64
                for qt in range(5):
                    t0, nq = TL[qt]
                    units = [(16, 0, nq, None), (17, 0, nq, None)]
                    if qt < 4:
                        for rel in range(-1, 5):
                            kc = qt * 4 + rel
                            if kc < 0 or kc > 15:
                                continue
                            qlo, qhi = max(rel - 1, 0), min(rel + 1, 3)
                            c0, c1 = qlo * 128, (qhi + 1) * 128
                            mc0 = (qlo - (rel - 1)) * 128
                            units.append((kc, c0, c1, [(c0, c1, mask[:, mc0:mc0 + (c1 - c0)], [Bmask])]))
                    e.attend(lambda a0, a1, t0=t0, p0=p0, p1=p1, c=c: qT[p0:p1, c, t0 + a0:t0 + a1],
                             lambda kc, p0=p0, p1=p1, m=m: kT[p0:p1, m, kc * 128:(kc + 1) * 128], 64,
                             lambda kc, g=g: vA[:, kc, g, :], units, 0.125, qT[p0:p1, c, t0:t0 + nq],
                             [Bq(c, hh, qt)], [Bk(m)], [Bv], Bq(c, hh, qt), ascr, nq,
                             sink_ap=e.sinkE[64:128, head:head + 1], la=3)
        e.attn_flush()
        S.barrier()
    e.dbg_dump("o", qT[:], "pool")
    with ExitStack() as s3:
        e.out_proj(i, b, qT, lambda k, ti: [Bq(k, 0, ti), Bq(k, 1, ti)], e.swa_wo, s3, [0, 1, 2, 3, 4], "swa")
        S.barrier()


def _na_units(m, ET, hh, BET):
    if m == 0:
        segs = [(0, 4, "F", range(0, 4)), (4, 8, "I", range(0, 6))]
    elif m == 3:
        segs = [(24, 29, "I", range(10, 16)), (29, 32, "F", range(12, 16))]
    else:
        segs = [(8 * m, 8 * m + 8, "I", range(4 * m - 2, 4 * m + 6))]
    units = []
    allk = sorted(set(k for sg in segs for k in sg[3]))
    for kc in allk:
        kr0 = 2 * kc
        ms = []
        for (ra, rb, kind, chunks) in segs:
            if kc not in chunks:
                continue
            if kind == "I":
                rlo, rhi = max(ra, kr0 - 3), min(rb - 1, kr0 + 5)
                if rlo > rhi:
                    continue
                idx0 = (rlo - kr0 + 7) + 4
                assert 0 <= idx0 and idx0 + (rhi - rlo + 1) <= NA_EI
            else:
                rlo, rhi = ra, rb - 1
                idx0 = NA_EI + (rlo - kr0 + 7)
                assert NA_EI + 1 <= idx0 and idx0 + (rhi - rlo + 1) <= NA_ET
            nr = rhi - rlo + 1
            cc0, cc1 = (rlo - 8 * m) * 64, (rhi + 1 - 8 * m) * 64
            tab = ET[:, hh, idx0:idx0 + nr, :].rearrange("p e c -> p (e c)")
            ms.append((cc0, cc1, tab, [BET]))
        if not ms:
            continue
        ms.sort(key=lambda x: x[0])
        for a_, b_ in zip(ms[:-1], ms[1:]):
            assert a_[1] == b_[0], (m, kc, ms)
        units.append((kc, ms[0][0], ms[-1][1], ms))
    return units


def na_layer(E):
    e = NS(E)
    S, nc, sb, st = e.S, e.nc, e.sb, e.st
    B = S.B
    i, b = e.i, e.b
    act, mm, tt, stt, cp, wload, psA = e.act, e.mm, e.tt, e.stt, e.cp, e.wload, e.psA
    Ba = e.Ba
    oT = sb("na_o", [128, 8, NT], BF16, st)
    Bo = lambda c, hh, t: B("na_o", c, hh, t)
    with ExitStack() as s1:
        aT = sb("aT", [128, 8, NT], BF16, s1)
        with ExitStack() as s0:
            mscr = e.mk_mscr(s0)
            e.modulate(i, 0, b, aT, Ba, mscr, [0, 1, 2, 3, 4])
            S.barrier()
        hs = e.head_scratch(s1, rope=False)
        wst = [sb("na_ws%d" % j, [128, 8, 128], BF16, s1) for j in range(3)]
        qp = sb("na_q", [128, NT], BF16, s1)
        kp = sb("na_k", [128, NT], BF16, s1)
        vA = sb("na_v", [128, 18, 2, 128], BF16, s1)
        ET = sb("na_et", [128, 2, NA_ET, 64], BF16, s1)
        ascr = e.attn_scratch(s1, 4)
        Bqp = lambda t: B("na_q", t)
        Bkp, Bv, BET = B("na_k"), B("na_v"), B("na_et")
        S.op("dve", lambda en: en.memset(vA[:, :, :, 64:128], 1.0), [], [Bv])
        for c in range(8):
            Bw = [B("na_ws", j) for j in range(3)]
            for j in range(3):
                wload(wst[j][:], e.na_wqkv[:, j * 1024 + c * 128:j * 1024 + (c + 1) * 128].rearrange("(k p) n -> p k n", p=128),
                      Bw[j], "na_ws%d" % j)
            wload(ET[:].rearrange("p a e c -> p (a e c)"), e.na_bias[:, (2 * c) * NA_ET * 64:(2 * c + 2) * NA_ET * 64], BET, "na_et")
            ETf = ET[:].rearrange("p a e c -> p (a e c)")
            for hx in range(2):
                act(ETf[:, hx * NA_ET * 64:(hx + 1) * NA_ET * 64], ETf[:, hx * NA_ET * 64:(hx + 1) * NA_ET * 64], AF.Exp, [BET], [BET])
            pipe = e.HNPipe()
            for ti in range(5):
                t0, n = TL[ti]
                for j, (dst, Bd, gname) in enumerate(((qp, [Bqp(ti)], "na_gq"), (kp, [Bkp], "na_gk"))):
                    pr, Bpr = psA()
                    for k in range(8):
                        mm(pr[:, :n], wst[j][:, k, :], aT[:, k, t0:t0 + n], k == 0, k == 7, [Bw[j], Ba(k, ti)], [Bpr])
                    pipe.push(e.headnorm_gen(dst[:, t0:t0 + n], Bd, pr[:, :n], Bpr, 128, n, e.bd64, 1.0 / 64, e.V(gname), hs, None))
            pipe.drain()
            for blk in range(18):
                ti = min(blk // 4, 4)
                pv, Bpv = psA()
                for k in range(8):
                    mm(pv[:, 0:128], aT[:, k, blk * 128:(blk + 1) * 128], wst[2][:, k, :], k == 0, k == 7, [Bw[2], Ba(k, ti)], [Bpv])
                cp("act" if blk % 2 else "dve", vA[:, blk, :, 0:64], pv[:, 0:128].rearrange("p (g d) -> p g d", d=64), [Bpv], [Bv])
            for hh in range(2):
                p0, p1 = hh * 64, hh * 64 + 64
                for qt in range(5):
                    t0, nq = TL[qt]
                    units = [(16, 0, nq, None), (17, 0, nq, None)]
                    if qt < 4:
                        units += _na_units(qt, ET, hh, BET)
                    e.attend(lambda a0, a1, t0=t0, p0=p0, p1=p1: qp[p0:p1, t0 + a0:t0 + a1],
                             lambda kc, p0=p0, p1=p1: kp[p0:p1, kc * 128:(kc + 1) * 128], 64,
                             lambda kc, hh=hh: vA[:, kc, hh, :], units, 0.125, oT[p0:p1, c, t0:t0 + nq],
                             [Bqp(qt)], [Bkp], [Bv], Bo(c, hh, qt), ascr, nq, la=3)
            e.attn_flush()
        S.barrier()
    e.dbg_dump("o", oT[:], "pool")
    with ExitStack() as s3:
        e.out_proj(i, b, oT, lambda k, ti: [Bo(k, 0, ti), Bo(k, 1, ti)], e.na_wo, s3, [0, 1, 2, 3, 4], "na")
        S.barrier()


def mla_layer(E):
    e = NS(E)
    S, nc, sb, st = e.S, e.nc, e.sb, e.st
    B = S.B
    i, b = e.i, e.b
    act, mm, tt, stt, cp, wload, psA, psC = e.act, e.mm, e.tt, e.stt, e.cp, e.wload, e.psA, e.psC
    Ba = e.Ba
    aT = sb("aT", [128, 8, NT], BF16, st)
    with ExitStack() as s0:
        mscr = e.mk_mscr(s0)
        e.modulate(i, 0, b, aT, Ba, mscr, [0, 1, 2, 3, 4])
        S.barrier()
    Bo = lambda c, hh, t: B("mla_o", c, hh, t)
    with ExitStack() as s1:
        cqT = sb("mla_cq", [128, 2, L], BF16, s1)
        ckvT = sb("mla_ckv", [128, NT], BF16, s1)
        krT = sb("mla_kr", [32, NT], BF16, s1)
        hs = e.head_scratch(s1, rope=True)
        Bcq = lambda t: B("mla_cq", t)
        Bckv, Bkr = B("mla_ckv"), B("mla_kr")
        with ExitStack() as sA:
            wa = sb("mla_wa", [128, 8, 416], BF16, sA)
            Bwa = B("mla_wa")
            wload(wa[:], e.mla_wa.rearrange("(k p) n -> p k n", p=128), Bwa, "mla_wa")
            for ti in range(5):
                t0, n = TL[ti]
                rd = [Bwa] + [Ba(k, ti) for k in range(8)]
                if ti < 4:
                    pq = [psA(), psA()]
                    for j in range(2):
                        for k in range(8):
                            mm(pq[j][0][:, :n], wa[:, k, j * 128:(j + 1) * 128], aT[:, k, t0:t0 + n], k == 0, k == 7, rd, [pq[j][1]])
                    ps2, Bps2 = psC()
                    sq = hs["hsq"]; Bsq = B("hsq")
                    for j in range(2):
                        act(sq[:, :n], pq[j][0][:, :n], AF.Square, [pq[j][1]], [Bsq])
                        mm(ps2[:, :n], e.ones128, sq[:, :n], j == 0, j == 1, [Bsq, e.Bcst], [Bps2])
                    rs = hs["hrs"]; Brs = B("hrs")
                    act(rs[:, :n], ps2[:, :n], AF.Ln, [Bps2, e.Bvec], [Brs], bias=e.V("eps"), scale=1.0 / 256)
                    act(rs[:, :n], rs[:, :n], AF.Exp, [Brs], [Brs], scale=-0.5)
                    for j in range(2):
                        stt("dve", cqT[:, j, t0:t0 + n], pq[j][0][:, :n], e.V("mla_gcq", j), rs[:, :n], ALU.mult, ALU.mult,
                            [pq[j][1], e.Bvec, Brs], [Bcq(ti)])
                pk, Bpk = psA()
                for k in range(8):
                    mm(pk[:, :n], wa[:, k, 256:384], aT[:, k, t0:t0 + n], k == 0, k == 7, rd, [Bpk])
                e.headnorm(ckvT[:, t0:t0 + n], [Bckv], pk[:, :n], Bpk, 128, n, e.ones128, 1.0 / 128, e.V("mla_gckv"), hs, None)
                pr, Bpr = psA()
                for k in range(8):
                    mm(pr[0:32, :n], wa[:, k, 384:416], aT[:, k, t0:t0 + n], k == 0, k == 7, rd, [Bpr])
                cp("act", krT[0:32, t0:t0 + n], pr[0:32, :n], [Bpr], [Bkr])
            S.barrier()
        with ExitStack() as sB:
            wuq = sb("mla_wuq", [128, 2, 1536], BF16, sB)
            wuk = sb("mla_wuk", [128, 16 * 96], BF16, sB)
            wuv = sb("mla_wuv", [128, 1024], BF16, sB)
            iext = sb("mla_iext", [32, 96], BF16, sB)
            perm = sb("mla_perm", [96, 96], BF16, sB)
            rcos = sb("mla_cos", [96, L], BF16, sB)
            rsin = sb("mla_sin", [96, L], BF16, sB)
            Bw = B("mla_w")
            wload(wuq[:], e.mla_wuq.rearrange("(k p) n -> p k n", p=128), Bw, "mla_w")
            wload(wuk[:], e.mla_wuk, Bw, "mla_w")
            wload(wuv[:], e.mla_wuv, Bw, "mla_w")
            wload(iext[:], e.iext_d, Bw, "mla_w")
            wload(perm[:], e.r96p, Bw, "mla_w")
            wload(rcos[:], e.r96c, Bw, "mla_w")
            wload(rsin[:], e.r96s, Bw, "mla_w")
            qh = [sb("mla_qh%d" % j, [96, L], BF16, sB) for j in range(2)]
            kh = [sb("mla_kh%d" % j, [96, NT], BF16, sB) for j in range(2)]
            vA = [sb("mla_vh%d" % j, [128, 18, 128], BF16, sB) for j in range(2)]
            ascr = e.attn_scratch(sB, 4)
            for j in range(2):
                S.op("dve", lambda en, j=j: en.memset(vA[j][:, :, 64:128], 1.0), [], [B("mla_vh", j)])
            scale = 96.0 ** -0.5
            e.set_nA(3)
            psP = e.psP

            def proj_gen(h):
                s_ = h % 2
                Bkh, Bvh = B("mla_kh", s_), B("mla_vh", s_)
                for ti in range(4):
                    t0, n = TL[ti]
                    pq, Bpq = psP()
                    for j in range(2):
                        mm(pq[0:96, :n], wuq[:, j, h * 96:(h + 1) * 96], cqT[:, j, t0:t0 + n], j == 0, j == 1, [Bw, Bcq(ti)], [Bpq])
                    for _ in e.headnorm_gen(qh[s_][:, t0:t0 + n], [B("mla_qh", s_, ti)], pq[0:96, :n], Bpq, 96, n, e.ones96, 1.0 / 96,
                                            e.V("mla_gq", 0, 0, 96), hs, (perm[:], rcos[:, t0:t0 + n], rsin[:, t0:t0 + n], [Bw])):
                        yield
                    yield
                for ti in range(5):
                    t0, n = TL[ti]
                    pk, Bpk = psP()
                    mm(pk[0:96, :n], wuk[:, h * 96:(h + 1) * 96], ckvT[:, t0:t0 + n], True, False, [Bw, Bckv], [Bpk])
                    mm(pk[0:96, :n], iext[:, :], krT[0:32, t0:t0 + n], False, True, [Bw, Bkr], [Bpk])
                    rope = (perm[:], rcos[:, t0:t0 + n], rsin[:, t0:t0 + n], [Bw]) if ti < 4 else None
                    for _ in e.headnorm_gen(kh[s_][:, t0:t0 + n], [Bkh], pk[0:96, :n], Bpk, 96, n, e.ones96, 1.0 / 96,
                                            e.V("mla_gk", 0, 0, 96), hs, rope):
                        yield
                    yield
                for b0 in range(0, 18, 8):
                    nb_ = min(8, 18 - b0)
                    pv, Bpv = psP()
                    for bi in range(nb_):
                        blk = b0 + bi
                        mm(pv[:, bi * 64:(bi + 1) * 64], ckvT[:, blk * 128:(blk + 1) * 128], wuv[:, h * 64:(h + 1) * 64], True, True,
                           [Bw, Bckv], [Bpv])
                    cp("act", vA[s_][:, b0:b0 + nb_, 0:64], pv[:, 0:nb_ * 64].rearrange("p (g d) -> p g d", d=64), [Bpv], [Bvh])
                    yield

            def exhaust(g):
                if g is not None:
                    for _ in g:
                        pass

            exhaust(proj_gen(0))
            for h in range(16):
                s_ = h % 2
                Bkh, Bvh = B("mla_kh", s_), B("mla_vh", s_)
                nxt = proj_gen(h + 1) if h + 1 < 16 else None

                def hook(nxt=nxt):
                    if nxt is not None:
                        try:
                            next(nxt)
                        except StopIteration:
                            pass
                hh = h % 2
                c = h // 2
                for qt in range(4):
                    t0, nq = TL[qt]
                    units = [(kc, 0, nq, None) for kc in range(18)]
                    e.attend(lambda a0, a1, t0=t0, s_=s_: qh[s_][0:96, t0 + a0:t0 + a1],
                             lambda kc, s_=s_: kh[s_][0:96, kc * 128:(kc + 1) * 128], 96,
                             lambda kc, s_=s_: vA[s_][:, kc, :], units, scale, aT[hh * 64:hh * 64 + 64, c, t0:t0 + nq],
                             [B("mla_qh", s_, qt)], [Bkh], [Bvh], Bo(c, hh, qt), ascr, nq, hook=hook)
                exhaust(nxt)
            e.attn_flush()
            e.set_nA(4)
            S.barrier()
    e.dbg_dump("o", aT[:], "pool")
    with ExitStack() as s3:
        e.out_proj(i, b, aT, lambda k, ti: [Bo(k, 0, ti), Bo(k, 1, ti)], e.mla_wo, s3, [0, 1, 2, 3], "mla")
        S.barrier()


_CACHE = {}


def run_layers(inputs, layers, ncores=NCORES, last_layer_idx=3):
    inp = {k: np.asarray(v) for k, v in inputs.items()}
    shared = host_shared(inp)
    key = (tuple(layers), last_layer_idx)
    if key not in _CACHE:
        _CACHE[key] = build(layers=layers, last_layer_idx=last_layer_idx)
    nc = _CACHE[key]
    in_maps = []
    for c in range(ncores):
        m = dict(shared)
        m.update(host_core(inp, c))
        in_maps.append(m)
    res = run_bass_kernel_spmd(nc, in_maps, core_ids=list(range(ncores)))
    run_layers.dbg = np.asarray(res.results[0]["dbgT"]) if "dbgT" in res.results[0] else None
    outs = [np.asarray(r["outT"]).transpose(0, 2, 1) for r in res.results]
    return np.ascontiguousarray(np.concatenate(outs, 0)).astype(np.float32)


def kernel(**inputs):
    return run_layers(inputs, (0, 1, 2, 3))
```

```python
import numpy as np
from contextlib import ExitStack
import concourse.bass as bass
import concourse.mybir as mybir
from concourse.bass_utils import run_bass_kernel_spmd

F32 = mybir.dt.float32
BF16 = mybir.dt.bfloat16
AF = mybir.ActivationFunctionType
ALU = mybir.AluOpType

D = 1024
L = 2048
LC = 256
NT = L + LC
DFF = 2816
NFC = DFF // 128
EPS = 1e-6
TL = [(0, 512), (512, 512), (1024, 512), (1536, 512), (2048, 256)]
NCORES = 8
NBPC = 2


class Buf:
    __slots__ = ("w", "r", "name")

    def __init__(self, name):
        self.name = name
        self.w = None
        self.r = {}


class Sched:
    ENGS = ("pe", "act", "dve", "pool", "sp")

    def __init__(self, nc, es):
        self.nc = nc
        self.es = es
        self.prog = {e: [] for e in self.ENGS}
        self.sem = {e: es.enter_context(nc.semaphore("s_" + e)) for e in self.ENGS}
        self.cnt = {e: 0 for e in self.ENGS}
        self.seen = {e: {} for e in self.ENGS}
        self.dsem = {}
        self.dcnt = {}
        self.bufs = {}
        self.ninst = 0

    def B(self, *key):
        b = self.bufs.get(key)
        if b is None:
            b = self.bufs[key] = Buf(key)
        return b

    def _semof(self, key):
        return self.sem[key] if key in self.sem else self.dsem[key]

    def _wait(self, eng, key, count, same_ok=False):
        if key == eng and same_ok:
            return
        if self.seen[eng].get(key, 0) >= count:
            return
        self.seen[eng][key] = count
        s = self._semof(key)
        self.prog[eng].append(lambda e, s=s, c=count: e.wait_ge(s, c))

    def op(self, eng, fn, reads=(), writes=(), inc=True):
        for b in reads:
            if b.w is not None:
                self._wait(eng, b.w[0], b.w[1], same_ok=(eng == "pe"))
        for b in writes:
            if b.w is not None:
                self._wait(eng, b.w[0], b.w[1], same_ok=(eng == "pe"))
            for k, c in b.r.items():
                self._wait(eng, k, c, same_ok=True)
        c = self.cnt[eng] + 1
        if inc:
            self.cnt[eng] = c
            s = self.sem[eng]
            self.prog[eng].append(lambda e, fn=fn, s=s: fn(e).then_inc(s, 1))
        else:
            self.prog[eng].append(lambda e, fn=fn: fn(e))
        for b in reads:
            if b.r.get(eng, 0) < c:
                b.r[eng] = c
        for b in writes:
            b.w = (eng, c)
            b.r = {}
        self.ninst += 1

    def dma(self, q, out, in_, reads=(), writes=(), key=None):
        if key not in self.dsem:
            self.dsem[key] = self.es.enter_context(self.nc.semaphore("d_%d" % len(self.dsem)))
            self.dcnt[key] = 0
        for b in reads:
            if b.w is not None:
                self._wait(q, b.w[0], b.w[1])
        for b in writes:
            if b.w is not None:
                self._wait(q, b.w[0], b.w[1])
            for k, c in b.r.items():
                self._wait(q, k, c)
        self.dcnt[key] += 16
        c = self.dcnt[key]
        s = self.dsem[key]
        self.prog[q].append(lambda e, s=s, out=out, in_=in_: e.dma_start(out=out, in_=in_).then_inc(s, 16))
        for b in reads:
            if b.r.get(key, 0) < c:
                b.r[key] = c
        for b in writes:
            b.w = (key, c)
            b.r = {}
        self.ninst += 1

    def barrier(self):
        snap = dict(self.cnt)
        dsnap = dict(self.dcnt)
        for e in self.ENGS:
            for k, c in snap.items():
                if k != e and c > 0:
                    self._wait(e, k, c)
            for k, c in dsnap.items():
                if c > 0:
                    self._wait(e, k, c)
        for b in self.bufs.values():
            b.w = None
            b.r = {}

    def finish(self):
        self.barrier()
        nc = self.nc
        with nc.Block() as block:
            @block.tensor
            def _(e):
                for f in self.prog["pe"]:
                    f(e)

            @block.scalar
            def _(e):
                for f in self.prog["act"]:
                    f(e)

            @block.vector
            def _(e):
                for f in self.prog["dve"]:
                    f(e)

            @block.gpsimd
            def _(e):
                for f in self.prog["pool"]:
                    f(e)

            @block.sync
            def _(e):
                for f in self.prog["sp"]:
                    f(e)


def _rope_tables(npart, blocks):
    t = np.arange(L)
    cos = np.ones((npart, L), np.float64)
    ssin = np.zeros((npart, L), np.float64)
    perm = np.zeros((npart, npart), np.float32)
    for (s0, nd, kind) in blocks:
        half = nd // 2
        pos = (t // 64) if kind == "row" else (t % 64)
        freqs = 10000.0 ** (-np.arange(half, dtype=np.float32).astype(np.float64) / half)
        ang = pos[None, :].astype(np.float64) * freqs[:, None]
        ang = (pos[None, :].astype(np.float32) * freqs.astype(np.float32)[:, None]).astype(np.float64)
        for i in range(half):
            d1, d2 = s0 + i, s0 + half + i
            cos[d1] = np.cos(ang[i]); cos[d2] = np.cos(ang[i])
            ssin[d1] = -np.sin(ang[i]); ssin[d2] = np.sin(ang[i])
            perm[d2, d1] = 1.0
            perm[d1, d2] = 1.0
    return cos.astype(np.float32), ssin.astype(np.float32), perm


def _vec_layout():
    off = {}
    n = 0
    def add(name, w):
        nonlocal n
        off[name] = n
        n += w
    add("b_ada", 4 * 48 * 3)
    add("g_mix", 32)
    add("g_ffn", 32)
    add("pool_scale", 8)
    add("pool_b", 8)
    add("swa_gq", 1); add("swa_gk", 1); add("swa_sink", 16)
    add("na_gq", 1); add("na_gk", 1)
    add("mla_gcq", 2); add("mla_gckv", 1); add("mla_gq", 1); add("mla_gk", 1)
    add("eps", 1); add("zero", 1)
    add("cT", 8 * 3)
    return off, n


VOFF, NV = _vec_layout()
SWA_QHEADS = [(8 * (c // 4) + (c % 4), 8 * (c // 4) + 4 + (c % 4)) for c in range(8)]


def _fm(v):
    return np.ascontiguousarray(v.reshape(-1, 128).T)


def host_shared(inp):
    f = np.float32
    sh = {}
    vecs = np.zeros((128, NV), f)
    def put(name, arr):
        arr = np.asarray(arr, f)
        vecs[:, VOFF[name]:VOFF[name] + arr.shape[1]] = arr
    put("b_ada", np.repeat(np.concatenate([_fm(inp["b_ada"][i]) for i in range(4)], 1), 3, axis=1))
    put("g_mix", np.concatenate([_fm(inp["g_mix"][i]) for i in range(4)], 1))
    put("g_ffn", np.concatenate([_fm(inp["g_ffn"][i]) for i in range(4)], 1))
    put("pool_scale", _fm(inp["pool_scale"][0]))
    put("pool_b", _fm(inp["pool_b"][0].reshape(-1)))
    put("swa_gq", np.tile(inp["swa_g_q"][0], 2)[:, None])
    put("swa_gk", np.tile(inp["swa_g_k"][0], 2)[:, None])
    put("swa_sink", np.broadcast_to(inp["swa_sink"][0][None, :], (128, 16)))
    put("na_gq", np.tile(inp["na_g_q"][0], 2)[:, None])
    put("na_gk", np.tile(inp["na_g_k"][0], 2)[:, None])
    put("mla_gcq", _fm(inp["mla_g_cq"][0]))
    put("mla_gckv", inp["mla_g_ckv"][0][:, None])
    gq = np.zeros((128, 1), f); gq[:96, 0] = inp["mla_g_q"][0]
    gk = np.zeros((128, 1), f); gk[:96, 0] = inp["mla_g_k"][0]
    put("mla_gq", gq); put("mla_gk", gk)
    vecs[:, VOFF["eps"]] = EPS
    sh["vecs_shared"] = vecs
    for k in ("w_ada", "w_gate_up", "w_down"):
        sh[k] = np.ascontiguousarray(inp[k], f)
    wqkv = inp["swa_w_qkv"][0]
    qcols = np.concatenate([np.arange(h * 64, h * 64 + 64) for pr in SWA_QHEADS for h in pr])
    sh["swa_wq"] = np.ascontiguousarray(wqkv[:, qcols])
    sh["swa_wk"] = np.ascontiguousarray(wqkv[:, 1024:1280])
    sh["swa_wv"] = np.ascontiguousarray(wqkv[:, 1280:1536])
    sh["swa_wo"] = np.ascontiguousarray(inp["swa_w_o"][0][qcols, :])
    c64, s64, p64 = _rope_tables(64, [(0, 32, "row"), (32, 32, "col")])
    sh["rope64_cos"] = np.concatenate([c64, c64], 0)
    sh["rope64_sin"] = np.concatenate([s64, s64], 0)
    pp = np.zeros((128, 128), f); pp[:64, :64] = p64; pp[64:, 64:] = p64
    sh["rope64_perm"] = pp
    sh["na_wqkv"] = np.ascontiguousarray(inp["na_w_qkv"][0])
    sh["na_wo"] = np.ascontiguousarray(inp["na_w_o"][0])
    sh["na_bias"] = _na_bias_tables(inp["na_rpb"][0])
    sh["pool_w"] = np.ascontiguousarray(inp["pool_w"][0])
    sh["mla_wa"] = np.ascontiguousarray(inp["mla_w_a"][0])
    sh["mla_wuq"] = np.ascontiguousarray(inp["mla_w_uq"][0])
    wukv = inp["mla_w_ukv"][0].reshape(128, 16, 128)
    wk = np.zeros((128, 16, 96), f); wk[:, :, :64] = wukv[:, :, :64]
    sh["mla_wuk"] = wk.reshape(128, 16 * 96)
    sh["mla_wuv"] = np.ascontiguousarray(wukv[:, :, 64:]).reshape(128, 1024)
    sh["mla_wo"] = np.ascontiguousarray(inp["mla_w_o"][0])
    c96, s96, p96 = _rope_tables(96, [(64, 16, "row"), (80, 16, "col")])
    sh["rope96_cos"] = c96; sh["rope96_sin"] = s96; sh["rope96_perm"] = p96
    iext = np.zeros((32, 96), f); iext[np.arange(32), 64 + np.arange(32)] = 1.0
    sh["mla_iext"] = iext
    consts = np.zeros((128, 512), f)
    consts[:, 0:128] = 1.0
    consts[:64, 128:192] = 1.0; consts[64:, 192:256] = 1.0
    consts[:96, 256:352] = 1.0
    sh["consts"] = consts
    kk = np.arange(128)[:, None]; qq = np.arange(128)[None, :]
    m_prev = ((kk + 128 - qq) <= 128).astype(f)
    m_same = np.ones((128, 128), f)
    m_next = ((qq + 128 - kk) <= 128).astype(f)
    sh["swa_mask"] = np.concatenate([m_prev, m_same, m_next], 1)
    pe = np.zeros((128, 4, 2, 2, 16), f)
    for gi, w in enumerate((2, 4, 8, 16)):
        for li, Lx in enumerate((L, LC)):
            t = np.arange(Lx)
            lo = np.clip(t - w // 2, 0, Lx); hi = np.clip(t - w // 2 + w, 0, Lx)
            inv = (1.0 / (hi - lo).astype(f)).astype(f)
            pe[:, gi, li, 0, :w // 2] = inv[:w // 2]
            if w // 2 - 1 > 0:
                pe[:, gi, li, 1, :w // 2 - 1] = inv[Lx - (w // 2 - 1):]
    sh["pool_edge"] = pe.reshape(128, -1)
    return sh


NA_NEG = -30000.0
NA_EI = 24
NA_EF = 16
NA_ET = NA_EI + NA_EF


def _na_bias_tables(rpb):
    H = 16
    tab = np.full((128, H, NA_ET, 64), NA_NEG, np.float32)
    kc = np.arange(64)[:, None]
    qc = np.arange(64)[None, :]
    cs = np.clip(qc - 8, 0, 48)
    colvalid = (kc >= cs) & (kc < cs + 16)
    dcidx = np.clip(kc - qc, -15, 15) + 15
    for half in range(2):
        for sec, (n_e, e0, lo, hi) in enumerate(((NA_EI, -4, -4, 3), (NA_EF, 0, -7, 7))):
            base = 0 if sec == 0 else NA_EI
            for idx in range(n_e):
                e = idx + e0 - half
                dr = 7 - e
                if dr < lo or dr > hi:
                    continue
                vals = rpb[:, dr + 7, :][:, dcidx]
                vals = np.where(colvalid[None], vals, NA_NEG)
                tab[half * 64:(half + 1) * 64, :, base + idx, :] = vals.transpose(1, 0, 2)
    return tab.reshape(128, H * NA_ET * 64)


def host_core(inp, core):
    b0 = core * NBPC
    d = {}
    d["xT"] = np.ascontiguousarray(inp["x"][b0:b0 + NBPC].transpose(0, 2, 1))
    d["ctxT"] = np.ascontiguousarray(inp["ctx"][b0:b0 + NBPC].transpose(0, 2, 1))
    cc = np.stack([inp["c"][b0], inp["c"][b0 + 1], inp["c_ctx"]], 1)
    d["cT"] = np.ascontiguousarray(cc.reshape(8, 128, 3).transpose(1, 0, 2)).reshape(128, 24).astype(np.float32)
    return d


def build(layers=(0, 1, 2, 3), nb=NBPC, last_layer_idx=3, parts="ABCD", dbg=None):
    nc = bass.Bass("TRN2", target_bir_lowering=False)
    dr = {}
    def din(name, shape):
        dr[name] = nc.dram_tensor(name, list(shape), F32, kind="ExternalInput").ap()
        return dr[name]
    xT = din("xT", [NBPC, D, L]); ctxT = din("ctxT", [NBPC, D, LC]); cT_d = din("cT", [128, 24])
    vecs_d = din("vecs_shared", [128, NV])
    w_ada = din("w_ada", [4, D, 6 * D]); w_gu = din("w_gate_up", [4, D, 2 * DFF]); w_dn = din("w_down", [4, DFF, D])
    swa_wq = din("swa_wq", [D, 1024]); swa_wk = din("swa_wk", [D, 256]); swa_wv = din("swa_wv", [D, 256]); swa_wo = din("swa_wo", [D, D])
    r64c = din("rope64_cos", [128, L]); r64s = din("rope64_sin", [128, L]); r64p = din("rope64_perm", [128, 128])
    na_wqkv = din("na_wqkv", [D, 3072]); na_wo = din("na_wo", [D, D]); na_bias = din("na_bias", [128, 16 * NA_ET * 64])
    pool_w = din("pool_w", [4, 256, 256])
    mla_wa = din("mla_wa", [D, 416]); mla_wuq = din("mla_wuq", [256, 1536]); mla_wuk = din("mla_wuk", [128, 16 * 96])
    mla_wuv = din("mla_wuv", [128, 1024]); mla_wo = din("mla_wo", [D, D])
    r96c = din("rope96_cos", [96, L]); r96s = din("rope96_sin", [96, L]); r96p = din("rope96_perm", [96, 96])
    iext_d = din("mla_iext", [32, 96]); consts_d = din("consts", [128, 512]); swa_mask_d = din("swa_mask", [128, 384])
    pool_edge_d = din("pool_edge", [128, 4 * 2 * 2 * 16])
    outT = nc.dram_tensor("outT", [NBPC, D, L], F32, kind="ExternalOutput").ap()
    dbgT = nc.dram_tensor("dbgT", [128, 8, NT], F32, kind="ExternalOutput").ap() if dbg else None

    es = ExitStack()
    with es:
        S = Sched(nc, es)
        B = S.B

        uid = [0]

        def sb(name, shape, dt, stack=es):
            uid[0] += 1
            return stack.enter_context(nc.sbuf_tensor("%s_%d" % (name, uid[0]), list(shape), dt))

        hT = sb("hT", [128, 8, NT], F32)
        vec = sb("vec", [128, NV], F32)
        cst = sb("cst", [128, 512], BF16)
        modT = sb("modT", [128, 4 * 48 * 3], F32)
        gsT = sb("gsT", [128, 4 * 2 * 8 * 3], F32)
        sinkE = sb("sinkE", [128, 16], F32)
        psb = [es.enter_context(nc.psum_tensor("ps%d" % i, [128, 512], F32)) for i in range(8)]
        PSB = [B("ps", i) for i in range(8)]
        rr = {"A": 0, "C": 0, "Bk": 0}

        rr["nA"] = 4

        def psA():
            i = rr["A"] % rr["nA"]; rr["A"] = (i + 1) % rr["nA"]
            return psb[i], PSB[i]

        def psP():
            return psb[3], PSB[3]

        def set_nA(n_):
            rr["nA"] = n_; rr["A"] = 0

        def psAcc():
            i = 4 + rr["Bk"]; rr["Bk"] = (rr["Bk"] + 1) % 2
            return psb[i], PSB[i]

        def psC():
            i = 6 + rr["C"]; rr["C"] = (rr["C"] + 1) % 2
            return psb[i], PSB[i]

        ones128 = cst[:, 0:128]
        bd64 = cst[:, 128:256]
        ones96 = cst[0:96, 256:352]

        def V(name, j=0, p0=0, p1=128):
            o = VOFF[name] + j
            return vec[p0:p1, o:o + 1]

        def modcol(i, ch, col):
            o = (i * 48 + ch) * 3 + col
            return modT[:, o:o + 1]

        def gscol(i, which, k, col):
            o = ((i * 2 + which) * 8 + k) * 3 + col
            return gsT[:, o:o + 1]

        def act(out, in_, func, reads, writes, bias=None, scale=None):
            kw = {}
            if bias is not None:
                kw["bias"] = bias
            if scale is not None:
                kw["scale"] = scale
            S.op("act", lambda e: e.activation(out=out, in_=in_, func=func, **kw), reads, writes)

        def mm(out, lhsT, rhs, start, stop, reads, writes):
            S.op("pe", lambda e: e.matmul(out, lhsT, rhs, start=start, stop=stop), reads, writes, inc=True)

        def tt(eng, out, in0, in1, op, reads, writes):
            S.op(eng, lambda e: e.tensor_tensor(out=out, in0=in0, in1=in1, op=op), reads, writes)

        def ts(eng, out, in0, s1, s2, op0, op1, reads, writes):
            if op1 is None:
                S.op(eng, lambda e: e.tensor_scalar(out=out, in0=in0, scalar1=s1, scalar2=None, op0=op0), reads, writes)
            else:
                S.op(eng, lambda e: e.tensor_scalar(out=out, in0=in0, scalar1=s1, scalar2=s2, op0=op0, op1=op1), reads, writes)

        def stt(eng, out, in0, scalar, in1, op0, op1, reads, writes):
            S.op(eng, lambda e: e.scalar_tensor_tensor(out=out, in0=in0, scalar=scalar, in1=in1, op0=op0, op1=op1), reads, writes)

        def cp(eng, out, in_, reads, writes):
            if eng == "act":
                S.op(eng, lambda e: e.activation(out=out, in_=in_, func=AF.Identity), reads, writes)
            else:
                S.op(eng, lambda e: e.tensor_copy(out=out, in_=in_), reads, writes)

        def wload(dst, src, buf, key):
            S.dma("pool", dst, src, writes=[buf], key=key)

        Bvec = B("vec"); Bcst = B("cst")
        Bmod = lambda i_: B("mod", i_)
        Bgs = lambda i_: B("gs", i_)
        S.dma("sp", vec[:, 0:VOFF["cT"]], vecs_d[:, 0:VOFF["cT"]], writes=[Bvec], key="vec")
        S.dma("sp", vec[:, VOFF["cT"]:NV], cT_d, writes=[Bvec], key="vec")
        wload(cst[:], consts_d, Bcst, "cst")
        condT = sb("condT", [128, 24], BF16)
        Bcond = B("cond")
        act(condT[:], vec[:, VOFF["cT"]:VOFF["cT"] + 24], AF.Silu, [Bvec], [Bcond])
        cond3 = condT[:].rearrange("p (k c) -> p k c", c=3)
        act(sinkE[:], vec[:, VOFF["swa_sink"]:VOFF["swa_sink"] + 16], AF.Exp, [Bvec], [B("sinkE")])

        def mod_gen(i, wab):
            for cb in range(12):
                w = wab[cb % 2]; Bw = B("wab", cb % 2)
                wload(w[:], w_ada[i, :, cb * 512:(cb + 1) * 512].rearrange("(k p) n -> p k n", p=128), Bw, "wab%d" % (cb % 2))
                yield
                pt, Bp = psA()
                for oc in range(4):
                    for k in range(8):
                        mm(pt[:, oc * 3:oc * 3 + 3], w[:, k, oc * 128:(oc + 1) * 128], cond3[:, k, :], k == 0, k == 7,
                           [Bw, Bcond], [Bp])
                o = (i * 48 + cb * 4) * 3
                bb = vec[:, VOFF["b_ada"] + o: VOFF["b_ada"] + o + 12]
                tt("dve", modT[:, o:o + 12], pt[:, 0:12], bb, ALU.add, [Bp, Bvec], [Bmod(i)])
            for which, (gname, scj) in enumerate((("g_mix", 1), ("g_ffn", 4))):
                for col in range(3):
                    o_in = (i * 48 + scj * 8) * 3 + col
                    sc_view = modT[:, o_in:o_in + 22:3]
                    o_out = ((i * 2 + which) * 8) * 3 + col
                    gs_view = gsT[:, o_out:o_out + 22:3]
                    g_view = vec[:, VOFF[gname] + i * 8: VOFF[gname] + i * 8 + 8]
                    stt("dve", gs_view, sc_view, 1.0, g_view, ALU.add, ALU.mult, [Bmod(i), Bvec], [Bgs(i)])
            yield

        with ExitStack() as ps_:
            wab0 = [sb("wab%d" % j, [128, 8, 512], BF16, ps_) for j in range(2)]
            if len(layers) > 0:
                for _ in mod_gen(layers[0], wab0):
                    pass
            S.barrier()

        Bh = lambda k, t: B("h", k, t)

        def modulate(i, which, bcol, aT, Ba, scr, tiles):
            shj = 0 if which == 0 else 3
            for ti in tiles:
                t0, n = TL[ti]
                col = 2 if ti == 4 else bcol
                pt, Bp = psC()
                for k in range(8):
                    sq = scr["sq"][k % 2]; Bsq = B("sq", k % 2)
                    tt("pool", sq[:, :n], hT[:, k, t0:t0 + n], hT[:, k, t0:t0 + n], ALU.mult, [Bh(k, ti)], [Bsq])
                    mm(pt[:, :n], ones128, sq[:, :n], k == 0, k == 7, [Bsq, Bcst], [Bp])
                rstd = scr["rstd"]; Br = B("rstd")
                act(rstd[:, :n], pt[:, :n], AF.Ln, [Bp, Bvec], [Br], bias=V("eps"), scale=1.0 / D)
                act(rstd[:, :n], rstd[:, :n], AF.Exp, [Br], [Br], scale=-0.5)
                for k in range(8):
                    tmp = scr["tmp"][k % 2]; Bt = B("mtmp", k % 2)
                    tt("dve", tmp[:, :n], hT[:, k, t0:t0 + n], rstd[:, :n], ALU.mult, [Bh(k, ti), Br], [Bt])
                    act(aT[:, k, t0:t0 + n], tmp[:, :n], AF.Identity, [Bt, Bgs(i), Bmod(i)], [Ba(k, ti)],
                        bias=modcol(i, shj * 8 + k, col), scale=gscol(i, which, k, col))

        def ffn(i, bcol, aT, Ba, st, tiles, hook=None):
            G = 2
            wg = [sb("ffn_wg%d" % j, [128, 8, 2 * G * 128], BF16, st) for j in range(2)]
            wd = [sb("ffn_wd%d" % j, [128, G, D], BF16, st) for j in range(2)]
            actb = [sb("ffn_act%d" % j, [128, G, 512], BF16, st) for j in range(2)]
            sgb = [sb("ffn_sg%d" % j, [128, 512], BF16, st) for j in range(2)]
            ngroups = NFC // G
            na = 0
            for g in range(ngroups):
                s = g % 2
                Bwg = B("ffn_wg", s); Bwd = B("ffn_wd", s)
                f0 = g * G * 128
                wload(wg[s][:, :, 0:G * 128], w_gu[i, :, f0:f0 + G * 128].rearrange("(k p) n -> p k n", p=128), Bwg, "ffn_wg%d" % s)
                wload(wg[s][:, :, G * 128:2 * G * 128], w_gu[i, :, DFF + f0:DFF + f0 + G * 128].rearrange("(k p) n -> p k n", p=128), Bwg, "ffn_wg%d" % s)
                wload(wd[s][:], w_dn[i, f0:f0 + G * 128, :].rearrange("(g p) n -> p g n", p=128), Bwd, "ffn_wd%d" % s)
                if hook is not None:
                    hook()
                    if g % 4 == 3:
                        hook()
                for ti in tiles:
                    t0, n = TL[ti]
                    col = 2 if ti == 4 else bcol
                    ab = actb[na % 2]; Bab = B("ffn_act", na % 2); na += 1
                    for fi in range(G):
                        pg, Bpg = psA()
                        for k in range(8):
                            mm(pg[:, :n], wg[s][:, k, fi * 128:(fi + 1) * 128], aT[:, k, t0:t0 + n], k == 0, k == 7, [Bwg, Ba(k, ti)], [Bpg])
                        pu, Bpu = psA()
                        for k in range(8):
                            mm(pu[:, :n], wg[s][:, k, (G + fi) * 128:(G + fi + 1) * 128], aT[:, k, t0:t0 + n], k == 0, k == 7, [Bwg, Ba(k, ti)], [Bpu])
                        sg = sgb[fi % 2]; Bsg = B("ffn_sg", fi % 2)
                        act(sg[:, :n], pg[:, :n], AF.Silu, [Bpg], [Bsg])
                        tt("dve", ab[:, fi, :n], pu[:, :n], sg[:, :n], ALU.mult, [Bpu, Bsg], [Bab])
                    for o in range(8):
                        po, Bpo = psAcc() if o % 2 == 0 else psC()
                        for fi in range(G):
                            mm(po[:, :n], wd[s][:, fi, o * 128:(o + 1) * 128], ab[:, fi, :n], fi == 0, fi == G - 1, [Bwd, Bab], [Bpo])
                        stt("dve", hT[:, o, t0:t0 + n], po[:, :n], modcol(i, 5 * 8 + o, col), hT[:, o, t0:t0 + n], ALU.mult, ALU.add,
                            [Bpo, Bmod(i), Bh(o, ti)], [Bh(o, ti)])

        def out_proj(i, bcol, oT, Bo, w_dram, st, tiles, name):
            wo = sb(name + "_wo", [128, 8, D], BF16, st)
            Bwo = B(name + "_wo")
            for half in range(2):
                wload(wo[:, half * 4:(half + 1) * 4, :], w_dram[half * 512:(half + 1) * 512, :].rearrange("(k p) n -> p k n", p=128), Bwo, name + "_wo")
            for ti in tiles:
                t0, n = TL[ti]
                col = 2 if ti == 4 else bcol
                for o in range(8):
                    po, Bpo = psA()
                    for k in range(8):
                        mm(po[:, :n], wo[:, k, o * 128:(o + 1) * 128], oT[:, k, t0:t0 + n], k == 0, k == 7, [Bwo] + Bo(k, ti), [Bpo])
                    stt("dve", hT[:, o, t0:t0 + n], po[:, :n], modcol(i, 2 * 8 + o, col), hT[:, o, t0:t0 + n], ALU.mult, ALU.add,
                        [Bpo, Bmod(i), Bh(o, ti)], [Bh(o, ti)])

        def headnorm_gen(out_ap, Bout, praw, Bpraw, M, n, ones_lhsT, invD, gain, scr, rope=None):
            sq = scr["hsq"]; Bsq = B("hsq")
            act(sq[0:M, :n], praw, AF.Square, [Bpraw], [Bsq])
            yield
            ps2, Bps2 = psC()
            mm(ps2[0:M, :n], ones_lhsT, sq[0:M, :n], True, True, [Bsq, Bcst], [Bps2])
            rs = scr["hrs"]; Brs = B("hrs")
            act(rs[0:M, :n], ps2[0:M, :n], AF.Ln, [Bps2, Bvec], [Brs], bias=V("eps", 0, 0, M), scale=invD)
            act(rs[0:M, :n], rs[0:M, :n], AF.Exp, [Brs], [Brs], scale=-0.5)
            if rope is None:
                stt("dve", out_ap, praw, gain, rs[0:M, :n], ALU.mult, ALU.mult, [Bpraw, Bvec, Brs], Bout)
                return
            perm, cos_ap, sin_ap, Brope = rope
            qn = scr["hqn"]; Bqn = B("hqn")
            stt("dve", qn[0:M, :n], praw, gain, rs[0:M, :n], ALU.mult, ALU.mult, [Bpraw, Bvec, Brs], [Bqn])
            yield
            ps3, Bps3 = psC()
            mm(ps3[0:M, :n], perm, qn[0:M, :n], True, True, [Bqn] + Brope, [Bps3])
            t1 = scr["ht1"]; Bt1 = B("ht1")
            tt("pool", t1[0:M, :n], qn[0:M, :n], cos_ap, ALU.mult, [Bqn] + Brope, [Bt1])
            t2 = scr["ht2"]; Bt2 = B("ht2")
            tt("dve", t2[0:M, :n], ps3[0:M, :n], sin_ap, ALU.mult, [Bps3] + Brope, [Bt2])
            tt("pool", out_ap, t1[0:M, :n], t2[0:M, :n], ALU.add, [Bt1, Bt2], Bout)

        def headnorm(*a_, **k_):
            for _ in headnorm_gen(*a_, **k_):
                pass

        class HNPipe:
            def __init__(self):
                self.g = []

            def push(self, gen):
                self.step()
                next(gen)
                self.g.append(gen)

            def step(self):
                for g in list(self.g):
                    try:
                        next(g)
                    except StopIteration:
                        self.g.remove(g)

            def drain(self):
                while self.g:
                    self.step()

        def head_scratch(st, rope=True):
            d = {"hsq": sb("hsq", [128, 512], BF16, st), "hrs": sb("hrs", [128, 512], F32, st)}
            if rope:
                d.update({"hqn": sb("hqn", [128, 512], BF16, st), "ht1": sb("ht1", [128, 512], F32, st),
                          "ht2": sb("ht2", [128, 512], F32, st)})
            return d

        pend = []
        LA = 2

        def attn_flush(keep=0):
            while len(pend) > keep:
                pend.pop(0)()

        def attend(qT_ap_fn, kT_ap_fn, Kdim, vaug_fn, units, scale, out_ap, reads_q, reads_k, reads_v, Bout, scr,
                   nq, sink_ap=None, hook=None, la=2):
            po, Bpo = psAcc()
            nu = len(units)
            for ui, (kc, c0, c1, mspec) in enumerate(units):
                pS, BpS = psA()
                mm(pS[:, 0:c1 - c0], kT_ap_fn(kc), qT_ap_fn(c0, c1), True, True, reads_q + reads_k, [BpS])
                pi = scr["pi"]; scr["pi"] = (pi + 1) % len(scr["P"])
                Pt = scr["P"][pi]; BP = B("P", pi)
                act(Pt[:, 0:c1 - c0], pS[:, 0:c1 - c0], AF.Exp, [BpS], [BP], scale=scale)
                if mspec:
                    for (cc0, cc1, m_ap, mb) in mspec:
                        tt("dve", Pt[:, cc0 - c0:cc1 - c0], Pt[:, cc0 - c0:cc1 - c0], m_ap, ALU.mult, [BP] + mb, [BP])
                pend.append(lambda kc=kc, c0=c0, c1=c1, Pt=Pt, BP=BP, ui=ui:
                            mm(po[:, c0:c1], vaug_fn(kc), Pt[:, 0:c1 - c0], ui == 0, ui == nu - 1, [BP] + reads_v, [Bpo]))
                attn_flush(la)
                if hook is not None and ui % 2 == 1:
                    hook()

            def epilogue():
                ri = scr["ri"]; scr["ri"] = (ri + 1) % len(scr["rc"])
                rc = scr["rc"][ri]; Brc = B("rc", ri)
                if sink_ap is not None:
                    act(rc[64:128, :nq], po[64:128, :nq], AF.Ln, [Bpo, B("sinkE")], [Brc], bias=sink_ap)
                else:
                    act(rc[64:128, :nq], po[64:128, :nq], AF.Ln, [Bpo], [Brc])
                act(rc[64:128, :nq], rc[64:128, :nq], AF.Exp, [Brc], [Brc], scale=-1.0)
                tt("dve", out_ap, po[0:64, :nq], rc[64:128, :nq], ALU.mult, [Bpo, Brc], [Bout])
            pend.append(epilogue)

        def attn_scratch(st, nP=4):
            return {"P": [sb("Pt%d" % j, [128, 512], BF16, st) for j in range(nP)], "pi": 0,
                    "rc": [sb("rc%d" % j, [128, 512], F32, st) for j in range(1)], "ri": 0}

        for b in range(nb):
            for k in range(8):
                S.dma("sp", hT[:, k, 0:L], xT[b, k * 128:(k + 1) * 128, :], writes=[Bh(k, t) for t in range(4)], key="hx%d" % k)
                S.dma("sp", hT[:, k, L:NT], ctxT[b, k * 128:(k + 1) * 128, :], writes=[Bh(k, 4)], key="hc%d" % k)
            for i in layers:
                need_ctx = i < last_layer_idx
                tiles = [0, 1, 2, 3, 4] if need_ctx else [0, 1, 2, 3]
                Ba = lambda k, t: B("a", k, t)

                def mk_mscr(stk):
                    return {"sq": [sb("msq%d" % j, [128, 512], BF16, stk) for j in range(2)],
                            "rstd": sb("mrstd", [128, 512], F32, stk),
                            "tmp": [sb("mtmp%d" % j, [128, 512], F32, stk) for j in range(2)]}

                def dbg_dump(tag, src, q):
                    if dbg == tag and b == 0:
                        S.barrier()
                        S.dma(q, dbgT, src, writes=[B("dbg")], key="dbg")
                        S.barrier()

                with ExitStack() as st2:
                    if "B" in parts:
                        ENV = dict(locals())
                        ENV.update(dict(S=S, nc=nc, sb=sb, st=st2))
                        if i == 2:
                            pool_layer(ENV)
                        elif i == 0:
                            swa_layer(ENV)
                        elif i == 1:
                            na_layer(ENV)
                        else:
                            mla_layer(ENV)
                    S.barrier()
                dbg_dump("h1", hT[:], "sp")
                with ExitStack() as st3:
                    aT = sb("aT", [128, 8, NT], BF16, st3)
                    mscr = mk_mscr(st3)
                    if "C" in parts:
                        modulate(i, 1, b, aT, Ba, mscr, tiles)
                    dbg_dump("a2", aT[:], "pool")
                    if "D" in parts:
                        li = list(layers).index(i)
                        mg = None
                        if b == 0 and li + 1 < len(layers):
                            wab1 = [sb("wabf%d" % j, [128, 8, 512], BF16, st3) for j in range(2)]
                            mg = mod_gen(layers[li + 1], wab1)

                        def mhook(mg=mg):
                            if mg is not None:
                                try:
                                    next(mg)
                                except StopIteration:
                                    pass
                        ffn(i, b, aT, Ba, st3, tiles, hook=mhook if mg is not None else None)
                        if mg is not None:
                            for _ in mg:
                                pass
                    S.barrier()
            for k in range(8):
                S.dma("sp", outT[b, k * 128:(k + 1) * 128, :], hT[:, k, 0:L], reads=[Bh(k, t) for t in range(4)],
                      writes=[B("out", b, k)], key="ost%d" % k)
        S.finish()
        build.ninst = S.ninst
    return nc


class NS:
    def __init__(self, d):
        self.__dict__.update(d)


def pool_layer(E):
    S, nc, sb, st = E["S"], E["nc"], E["sb"], E["st"]
    B = S.B
    i, b, Ba, hT = E["i"], E["b"], E["Ba"], E["hT"]
    vec, V, Bvec, Bmod, modcol = E["vec"], E["V"], E["Bvec"], E["Bmod"], E["modcol"]
    act, mm, tt, ts, stt, cp, wload, psA = E["act"], E["mm"], E["tt"], E["ts"], E["stt"], E["cp"], E["wload"], E["psA"]
    need_ctx, pool_w, pool_edge_d, Bh = E["need_ctx"], E["pool_w"], E["pool_edge_d"], E["Bh"]
    aT = sb("aT", [128, 8, NT], BF16, st)
    mscr = E["mk_mscr"](st)
    if "A" in E["parts"]:
        E["modulate"](i, 0, b, aT, Ba, mscr, [0, 1, 2, 3, 4])
    E["dbg_dump"]("a1", aT[:], "pool")
    pT = sb("poolP", [128, 8, NT], BF16, st)
    Bp = lambda k, t: B("poolP", k, t)
    wp = sb("poolW", [128, 4, 2, 256], BF16, st)
    Bwp = B("poolW")
    wload(wp[:].rearrange("p g k n -> p (g k) n"), pool_w.rearrange("g (k p) n -> p (g k) n", p=128), Bwp, "poolW")
    edge = sb("poolE", [128, 4 * 2 * 2 * 16], F32, st)
    Bedge = B("poolE")
    S.dma("sp", edge[:], pool_edge_d, writes=[Bedge], key="poolE")
    PADL = 8
    WLEN = PADL + L + 16
    pbufs = {}
    for en_ in ("dve", "pool"):
        pbufs[en_] = (sb("poolA_" + en_, [128, WLEN], F32, st), sb("poolB_" + en_, [128, WLEN], F32, st),
                      B("poolA", en_), B("poolB", en_))
    sT = sb("poolS", [128, 8 * 3], F32, st)
    Bs = B("poolS")
    for col in range(3):
        o_in = (i * 48 + 2 * 8) * 3 + col
        tt("dve", sT[:, col:col + 22:3], E["modT"][:, o_in:o_in + 22:3], vec[:, VOFF["pool_scale"]:VOFF["pool_scale"] + 8], ALU.mult,
           [Bmod(i), Bvec], [Bs])
    segs = [(0, L, 0, [0, 1, 2, 3])] + ([(L, LC, 1, [4])] if True else [])
    for k in range(8):
        gi = k // 2
        w = (2, 4, 8, 16)[gi]
        EN = "pool" if k in (1, 3, 5) else "dve"
        bufA, bufB, BA, BB = pbufs[EN]
        for (s0, Ln, li, tls) in segs:
            allB = [Ba(k, t) for t in tls]
            S.op(EN, lambda e, bufA=bufA: e.memset(bufA[:, 0:PADL], 0.0), [], [BA])
            S.op(EN, lambda e, Ln=Ln, bufA=bufA: e.memset(bufA[:, PADL + Ln:PADL + Ln + 16], 0.0), [], [BA])
            cp(EN, bufA[:, PADL:PADL + Ln], aT[:, k, s0:s0 + Ln], allB, [BA])
            src, Bsrc, dst, Bdst = bufA, BA, bufB, BB
            step = 1
            n_valid = PADL + Ln + 16
            while step < w:
                n_valid -= step
                tt(EN, dst[:, 0:n_valid], src[:, 0:n_valid], src[:, step:step + n_valid], ALU.add, [Bsrc], [Bdst])
                src, Bsrc, dst, Bdst = dst, Bdst, src, Bsrc
                step *= 2
            o0 = PADL - w // 2
            stt("dve", pT[:, k, s0:s0 + Ln], src[:, o0:o0 + Ln], 1.0 / w, aT[:, k, s0:s0 + Ln], ALU.mult, ALU.subtract, [Bsrc] + allB,
                [Bp(k, t) for t in tls])
            eo = ((gi * 2 + li) * 2) * 16
            ne = w // 2
            tt(EN, dst[:, 0:ne], src[:, o0:o0 + ne], edge[:, eo:eo + ne], ALU.mult, [Bsrc, Bedge], [Bdst])
            tt(EN, pT[:, k, s0:s0 + ne], dst[:, 0:ne], aT[:, k, s0:s0 + ne], ALU.subtract, [Bdst, Ba(k, tls[0])], [Bp(k, tls[0])])
            ne2 = w // 2 - 1
            if ne2 > 0:
                tt(EN, dst[:, 16:16 + ne2], src[:, o0 + Ln - ne2:o0 + Ln], edge[:, eo + 16:eo + 16 + ne2], ALU.mult, [Bsrc, Bedge], [Bdst])
                tt(EN, pT[:, k, s0 + Ln - ne2:s0 + Ln], dst[:, 16:16 + ne2], aT[:, k, s0 + Ln - ne2:s0 + Ln], ALU.subtract,
                   [Bdst, Ba(k, tls[-1])], [Bp(k, tls[-1])])
    tmpb = [sb("poolT%d" % j, [128, 512], F32, st) for j in range(2)]
    nn = 0
    for ti in ([0, 1, 2, 3, 4] if need_ctx else [0, 1, 2, 3]):
        t0, n = TL[ti]
        col = 2 if ti == 4 else b
        for o in range(8):
            g = o // 2
            po, Bpo = psA()
            for kk in range(2):
                mm(po[:, :n], wp[:, g, kk, (o % 2) * 128:(o % 2 + 1) * 128], pT[:, 2 * g + kk, t0:t0 + n], kk == 0, kk == 1,
                   [Bwp, Bp(2 * g + kk, ti)], [Bpo])
            tb = tmpb[nn % 2]; Btb = B("poolT", nn % 2); nn += 1
            act(tb[:, :n], po[:, :n], AF.Identity, [Bpo, Bvec], [Btb], bias=V("pool_b", o))
            stt("dve", hT[:, o, t0:t0 + n], tb[:, :n], sT[:, o * 3 + col:o * 3 + col + 1], hT[:, o, t0:t0 + n], ALU.mult, ALU.add,
                [Btb, Bs, Bh(o, ti)], [Bh(o, ti)])


def swa_layer(E):
    e = NS(E)
    S, nc, sb, st = e.S, e.nc, e.sb, e.st
    B = S.B
    i, b = e.i, e.b
    act, mm, tt, stt, cp, wload, psA = e.act, e.mm, e.tt, e.stt, e.cp, e.wload, e.psA
    qT = sb("swa_q", [128, 8, NT], BF16, st)
    kT = sb("swa_k", [128, 2, NT], BF16, st)
    vA = sb("swa_v", [128, 18, 4, 128], BF16, st)
    Bq = lambda c, hh, t: B("swa_q", c, hh, t)
    Bk = lambda m: B("swa_k", m)
    Bv = B("swa_v")
    Ba = e.Ba
    with ExitStack() as s1:
        aT = sb("aT", [128, 8, NT], BF16, s1)
        with ExitStack() as s0:
            mscr = e.mk_mscr(s0)
            e.modulate(i, 0, b, aT, Ba, mscr, [0, 1, 2, 3, 4])
            S.barrier()
        e.dbg_dump("a1", aT[:], "pool")
        hs = e.head_scratch(s1)
        wst = [sb("swa_ws%d" % j, [128, 8, 128], BF16, s1) for j in range(2)]
        ropeT = sb("swa_rope", [128, 2, L], BF16, s1)
        perm = sb("swa_perm", [128, 128], BF16, s1)
        Bperm = B("swa_perm")
        wload(perm[:], e.r64p, Bperm, "swa_perm")
        wload(ropeT[:, 0, :], e.r64c, Bperm, "swa_perm")
        wload(ropeT[:, 1, :], e.r64s, Bperm, "swa_perm")
        S.op("dve", lambda en: en.memset(vA[:, :, :, 64:128], 1.0), [], [Bv])
        nw = 0
        pipe = e.HNPipe()
        plist = [("k", 0), ("k", 1)] + [("q", c) for c in range(8)]
        for (kind, c) in plist:
            w = wst[nw % 2]; Bw = B("swa_ws", nw % 2)
            src = e.swa_wk if kind == "k" else e.swa_wq
            wload(w[:], src[:, c * 128:(c + 1) * 128].rearrange("(k p) n -> p k n", p=128), Bw, "swa_ws%d" % (nw % 2))
            nw += 1
            for ti in range(5):
                t0, n = TL[ti]
                pr, Bpr = psA()
                for k in range(8):
                    mm(pr[:, :n], w[:, k, :], aT[:, k, t0:t0 + n], k == 0, k == 7, [Bw, Ba(k, ti)], [Bpr])
                rope = None
                if ti < 4:
                    rope = (perm[:], ropeT[:, 0, t0:t0 + n], ropeT[:, 1, t0:t0 + n], [Bperm])
                if kind == "k":
                    out_ap = kT[:, c, t0:t0 + n]; Bo = [Bk(c)]; gain = e.V("swa_gk")
                else:
                    out_ap = qT[:, c, t0:t0 + n]; Bo = [Bq(c, 0, ti), Bq(c, 1, ti)]; gain = e.V("swa_gq")
                pipe.push(e.headnorm_gen(out_ap, Bo, pr[:, :n], Bpr, 128, n, e.bd64, 1.0 / 64, gain, hs, rope))
        pipe.drain()
        for half in range(2):
            w = wst[nw % 2]; Bw = B("swa_ws", nw % 2)
            wload(w[:], e.swa_wv[:, half * 128:(half + 1) * 128].rearrange("(k p) n -> p k n", p=128), Bw, "swa_ws%d" % (nw % 2))
            nw += 1
            for blk in range(18):
                ti = min(blk // 4, 4)
                pv, Bpv = psA()
                for k in range(8):
                    mm(pv[:, 0:128], aT[:, k, blk * 128:(blk + 1) * 128], w[:, k, :], k == 0, k == 7, [Bw, Ba(k, ti)], [Bpv])
                cp("act" if blk % 2 else "dve", vA[:, blk, 2 * half:2 * half + 2, 0:64], pv[:, 0:128].rearrange("p (g d) -> p g d", d=64),
                   [Bpv], [Bv])
        S.barrier()
    with ExitStack() as s2:
        ascr = e.attn_scratch(s2, 4)
        mask = sb("swa_mask", [128, 384], BF16, s2)
        Bmask = B("swa_mask")
        wload(mask[:], e.swa_mask_d, Bmask, "swa_mask")
        for c in range(8):
            m = c // 4
            for hh in range(2):
                head = SWA_QHEADS[c][hh]
                g = 2 * m + hh
                p0, p1 = hh * 64, hh * 64 + 64
                for qt in range(5):
                    t0, nq = TL[qt]
                    units = [(16, 0, nq, None), (17, 0, nq, None)]
                    if qt < 4:
                        for rel in range(-1, 5):
                            kc = qt * 4 + rel
                            if kc < 0 or kc > 15:
                                continue
                            qlo, qhi = max(rel - 1, 0), min(rel + 1, 3)
                            c0, c1 = qlo * 128, (qhi + 1) * 128
                            mc0 = (qlo - (rel - 1)) * 128
                            units.append((kc, c0, c1, [(c0, c1, mask[:, mc0:mc0 + (c1 - c0)], [Bmask])]))
                    e.attend(lambda a0, a1, t0=t0, p0=p0, p1=p1, c=c: qT[p0:p1, c, t0 + a0:t0 + a1],
                             lambda kc, p0=p0, p1=p1, m=m: kT[p0:p1, m, kc * 128:(kc + 1) * 128], 64,
                             lambda kc, g=g: vA[:, kc, g, :], units, 0.125, qT[p0:p1, c, t0:t0 + nq],
                             [Bq(c, hh, qt)], [Bk(m)], [Bv], Bq(c, hh, qt), ascr, nq,
                             sink_ap=e.sinkE[64:128, head:head + 1], la=3)
        e.attn_flush()
        S.barrier()
    e.dbg_dump("o", qT[:], "pool")
    with ExitStack() as s3:
        e.out_proj(i, b, qT, lambda k, ti: [Bq(k, 0, ti), Bq(k, 1, ti)], e.swa_wo, s3, [0, 1, 2, 3, 4], "swa")
        S.barrier()


def _na_units(m, ET, hh, BET):
    if m == 0:
        segs = [(0, 4, "F", range(0, 4)), (4, 8, "I", range(0, 6))]
    elif m == 3:
        segs = [(24, 29, "I", range(10, 16)), (29, 32, "F", range(12, 16))]
    else:
        segs = [(8 * m, 8 * m + 8, "I", range(4 * m - 2, 4 * m + 6))]
    units = []
    allk = sorted(set(k for sg in segs for k in sg[3]))
    for kc in allk:
        kr0 = 2 * kc
        ms = []
        for (ra, rb, kind, chunks) in segs:
            if kc not in chunks:
                continue
            if kind == "I":
                rlo, rhi = max(ra, kr0 - 3), min(rb - 1, kr0 + 5)
                if rlo > rhi:
                    continue
                idx0 = (rlo - kr0 + 7) + 4
                assert 0 <= idx0 and idx0 + (rhi - rlo + 1) <= NA_EI
            else:
                rlo, rhi = ra, rb - 1
                idx0 = NA_EI + (rlo - kr0 + 7)
                assert NA_EI + 1 <= idx0 and idx0 + (rhi - rlo + 1) <= NA_ET
            nr = rhi - rlo + 1
            cc0, cc1 = (rlo - 8 * m) * 64, (rhi + 1 - 8 * m) * 64
            tab = ET[:, hh, idx0:idx0 + nr, :].rearrange("p e c -> p (e c)")
            ms.append((cc0, cc1, tab, [BET]))
        if not ms:
            continue
        ms.sort(key=lambda x: x[0])
        for a_, b_ in zip(ms[:-1], ms[1:]):
            assert a_[1] == b_[0], (m, kc, ms)
        units.append((kc, ms[0][0], ms[-1][1], ms))
    return units


def na_layer(E):
    e = NS(E)
    S, nc, sb, st = e.S, e.nc, e.sb, e.st
    B = S.B
    i, b = e.i, e.b
    act, mm, tt, stt, cp, wload, psA = e.act, e.mm, e.tt, e.stt, e.cp, e.wload, e.psA
    Ba = e.Ba
    oT = sb("na_o", [128, 8, NT], BF16, st)
    Bo = lambda c, hh, t: B("na_o", c, hh, t)
    with ExitStack() as s1:
        aT = sb("aT", [128, 8, NT], BF16, s1)
        with ExitStack() as s0:
            mscr = e.mk_mscr(s0)
            e.modulate(i, 0, b, aT, Ba, mscr, [0, 1, 2, 3, 4])
            S.barrier()
        hs = e.head_scratch(s1, rope=False)
        wst = [sb("na_ws%d" % j, [128, 8, 128], BF16, s1) for j in range(3)]
        qp = sb("na_q", [128, NT], BF16, s1)
        kp = sb("na_k", [128, NT], BF16, s1)
        vA = sb("na_v", [128, 18, 2, 128], BF16, s1)
        ET = sb("na_et", [128, 2, NA_ET, 64], BF16, s1)
        ascr = e.attn_scratch(s1, 4)
        Bqp = lambda t: B("na_q", t)
        Bkp, Bv, BET = B("na_k"), B("na_v"), B("na_et")
        S.op("dve", lambda en: en.memset(vA[:, :, :, 64:128], 1.0), [], [Bv])
        for c in range(8):
            Bw = [B("na_ws", j) for j in range(3)]
            for j in range(3):
                wload(wst[j][:], e.na_wqkv[:, j * 1024 + c * 128:j * 1024 + (c + 1) * 128].rearrange("(k p) n -> p k n", p=128),
                      Bw[j], "na_ws%d" % j)
            wload(ET[:].rearrange("p a e c -> p (a e c)"), e.na_bias[:, (2 * c) * NA_ET * 64:(2 * c + 2) * NA_ET * 64], BET, "na_et")
            ETf = ET[:].rearrange("p a e c -> p (a e c)")
            for hx in range(2):
                act(ETf[:, hx * NA_ET * 64:(hx + 1) * NA_ET * 64], ETf[:, hx * NA_ET * 64:(hx + 1) * NA_ET * 64], AF.Exp, [BET], [BET])
            pipe = e.HNPipe()
            for ti in range(5):
                t0, n = TL[ti]
                for j, (dst, Bd, gname) in enumerate(((qp, [Bqp(ti)], "na_gq"), (kp, [Bkp], "na_gk"))):
                    pr, Bpr = psA()
                    for k in range(8):
                        mm(pr[:, :n], wst[j][:, k, :], aT[:, k, t0:t0 + n], k == 0, k == 7, [Bw[j], Ba(k, ti)], [Bpr])
                    pipe.push(e.headnorm_gen(dst[:, t0:t0 + n], Bd, pr[:, :n], Bpr, 128, n, e.bd64, 1.0 / 64, e.V(gname), hs, None))
            pipe.drain()
            for blk in range(18):
                ti = min(blk // 4, 4)
                pv, Bpv = psA()
                for k in range(8):
                    mm(pv[:, 0:128], aT[:, k, blk * 128:(blk + 1) * 128], wst[2][:, k, :], k == 0, k == 7, [Bw[2], Ba(k, ti)], [Bpv])
                cp("act" if blk % 2 else "dve", vA[:, blk, :, 0:64], pv[:, 0:128].rearrange("p (g d) -> p g d", d=64), [Bpv], [Bv])
            for hh in range(2):
                p0, p1 = hh * 64, hh * 64 + 64
                for qt in range(5):
                    t0, nq = TL[qt]
                    units = [(16, 0, nq, None), (17, 0, nq, None)]
                    if qt < 4:
                        units += _na_units(qt, ET, hh, BET)
                    e.attend(lambda a0, a1, t0=t0, p0=p0, p1=p1: qp[p0:p1, t0 + a0:t0 + a1],
                             lambda kc, p0=p0, p1=p1: kp[p0:p1, kc * 128:(kc + 1) * 128], 64,
                             lambda kc, hh=hh: vA[:, kc, hh, :], units, 0.125, oT[p0:p1, c, t0:t0 + nq],
                             [Bqp(qt)], [Bkp], [Bv], Bo(c, hh, qt), ascr, nq, la=3)
            e.attn_flush()
        S.barrier()
    e.dbg_dump("o", oT[:], "pool")
    with ExitStack() as s3:
        e.out_proj(i, b, oT, lambda k, ti: [Bo(k, 0, ti), Bo(k, 1, ti)], e.na_wo, s3, [0, 1, 2, 3, 4], "na")
        S.barrier()


def mla_layer(E):
    e = NS(E)
    S, nc, sb, st = e.S, e.nc, e.sb, e.st
    B = S.B
    i, b = e.i, e.b
    act, mm, tt, stt, cp, wload, psA, psC = e.act, e.mm, e.tt, e.stt, e.cp, e.wload, e.psA, e.psC
    Ba = e.Ba
    aT = sb("aT", [128, 8, NT], BF16, st)
    with ExitStack() as s0:
        mscr = e.mk_mscr(s0)
        e.modulate(i, 0, b, aT, Ba, mscr, [0, 1, 2, 3, 4])
        S.barrier()
    Bo = lambda c, hh, t: B("mla_o", c, hh, t)
    with ExitStack() as s1:
        cqT = sb("mla_cq", [128, 2, L], BF16, s1)
        ckvT = sb("mla_ckv", [128, NT], BF16, s1)
        krT = sb("mla_kr", [32, NT], BF16, s1)
        hs = e.head_scratch(s1, rope=True)
        Bcq = lambda t: B("mla_cq", t)
        Bckv, Bkr = B("mla_ckv"), B("mla_kr")
        with ExitStack() as sA:
            wa = sb("mla_wa", [128, 8, 416], BF16, sA)
            Bwa = B("mla_wa")
            wload(wa[:], e.mla_wa.rearrange("(k p) n -> p k n", p=128), Bwa, "mla_wa")
            for ti in range(5):
                t0, n = TL[ti]
                rd = [Bwa] + [Ba(k, ti) for k in range(8)]
                if ti < 4:
                    pq = [psA(), psA()]
                    for j in range(2):
                        for k in range(8):
                            mm(pq[j][0][:, :n], wa[:, k, j * 128:(j + 1) * 128], aT[:, k, t0:t0 + n], k == 0, k == 7, rd, [pq[j][1]])
                    ps2, Bps2 = psC()
                    sq = hs["hsq"]; Bsq = B("hsq")
                    for j in range(2):
                        act(sq[:, :n], pq[j][0][:, :n], AF.Square, [pq[j][1]], [Bsq])
                        mm(ps2[:, :n], e.ones128, sq[:, :n], j == 0, j == 1, [Bsq, e.Bcst], [Bps2])
                    rs = hs["hrs"]; Brs = B("hrs")
                    act(rs[:, :n], ps2[:, :n], AF.Ln, [Bps2, e.Bvec], [Brs], bias=e.V("eps"), scale=1.0 / 256)
                    act(rs[:, :n], rs[:, :n], AF.Exp, [Brs], [Brs], scale=-0.5)
                    for j in range(2):
                        stt("dve", cqT[:, j, t0:t0 + n], pq[j][0][:, :n], e.V("mla_gcq", j), rs[:, :n], ALU.mult, ALU.mult,
                            [pq[j][1], e.Bvec, Brs], [Bcq(ti)])
                pk, Bpk = psA()
                for k in range(8):
                    mm(pk[:, :n], wa[:, k, 256:384], aT[:, k, t0:t0 + n], k == 0, k == 7, rd, [Bpk])
                e.headnorm(ckvT[:, t0:t0 + n], [Bckv], pk[:, :n], Bpk, 128, n, e.ones128, 1.0 / 128, e.V("mla_gckv"), hs, None)
                pr, Bpr = psA()
                for k in range(8):
                    mm(pr[0:32, :n], wa[:, k, 384:416], aT[:, k, t0:t0 + n], k == 0, k == 7, rd, [Bpr])
                cp("act", krT[0:32, t0:t0 + n], pr[0:32, :n], [Bpr], [Bkr])
            S.barrier()
        with ExitStack() as sB:
            wuq = sb("mla_wuq", [128, 2, 1536], BF16, sB)
            wuk = sb("mla_wuk", [128, 16 * 96], BF16, sB)
            wuv = sb("mla_wuv", [128, 1024], BF16, sB)
            iext = sb("mla_iext", [32, 96], BF16, sB)
            perm = sb("mla_perm", [96, 96], BF16, sB)
            rcos = sb("mla_cos", [96, L], BF16, sB)
            rsin = sb("mla_sin", [96, L], BF16, sB)
            Bw = B("mla_w")
            wload(wuq[:], e.mla_wuq.rearrange("(k p) n -> p k n", p=128), Bw, "mla_w")
            wload(wuk[:], e.mla_wuk, Bw, "mla_w")
            wload(wuv[:], e.mla_wuv, Bw, "mla_w")
            wload(iext[:], e.iext_d, Bw, "mla_w")
            wload(perm[:], e.r96p, Bw, "mla_w")
            wload(rcos[:], e.r96c, Bw, "mla_w")
            wload(rsin[:], e.r96s, Bw, "mla_w")
            qh = [sb("mla_qh%d" % j, [96, L], BF16, sB) for j in range(2)]
            kh = [sb("mla_kh%d" % j, [96, NT], BF16, sB) for j in range(2)]
            vA = [sb("mla_vh%d" % j, [128, 18, 128], BF16, sB) for j in range(2)]
            ascr = e.attn_scratch(sB, 4)
            for j in range(2):
                S.op("dve", lambda en, j=j: en.memset(vA[j][:, :, 64:128], 1.0), [], [B("mla_vh", j)])
            scale = 96.0 ** -0.5
            e.set_nA(3)
            psP = e.psP

            def proj_gen(h):
                s_ = h % 2
                Bkh, Bvh = B("mla_kh", s_), B("mla_vh", s_)
                for ti in range(4):
                    t0, n = TL[ti]
                    pq, Bpq = psP()
                    for j in range(2):
                        mm(pq[0:96, :n], wuq[:, j, h * 96:(h + 1) * 96], cqT[:, j, t0:t0 + n], j == 0, j == 1, [Bw, Bcq(ti)], [Bpq])
                    for _ in e.headnorm_gen(qh[s_][:, t0:t0 + n], [B("mla_qh", s_, ti)], pq[0:96, :n], Bpq, 96, n, e.ones96, 1.0 / 96,
                                            e.V("mla_gq", 0, 0, 96), hs, (perm[:], rcos[:, t0:t0 + n], rsin[:, t0:t0 + n], [Bw])):
                        yield
                    yield
                for ti in range(5):
                    t0, n = TL[ti]
                    pk, Bpk = psP()
                    mm(pk[0:96, :n], wuk[:, h * 96:(h + 1) * 96], ckvT[:, t0:t0 + n], True, False, [Bw, Bckv], [Bpk])
                    mm(pk[0:96, :n], iext[:, :], krT[0:32, t0:t0 + n], False, True, [Bw, Bkr], [Bpk])
                    rope = (perm[:], rcos[:, t0:t0 + n], rsin[:, t0:t0 + n], [Bw]) if ti < 4 else None
                    for _ in e.headnorm_gen(kh[s_][:, t0:t0 + n], [Bkh], pk[0:96, :n], Bpk, 96, n, e.ones96, 1.0 / 96,
                                            e.V("mla_gk", 0, 0, 96), hs, rope):
                        yield
                    yield
                for b0 in range(0, 18, 8):
                    nb_ = min(8, 18 - b0)
                    pv, Bpv = psP()
                    for bi in range(nb_):
                        blk = b0 + bi
                        mm(pv[:, bi * 64:(bi + 1) * 64], ckvT[:, blk * 128:(blk + 1) * 128], wuv[:, h * 64:(h + 1) * 64], True, True,
                           [Bw, Bckv], [Bpv])
                    cp("act", vA[s_][:, b0:b0 + nb_, 0:64], pv[:, 0:nb_ * 64].rearrange("p (g d) -> p g d", d=64), [Bpv], [Bvh])
                    yield

            def exhaust(g):
                if g is not None:
                    for _ in g:
                        pass

            exhaust(proj_gen(0))
            for h in range(16):
                s_ = h % 2
                Bkh, Bvh = B("mla_kh", s_), B("mla_vh", s_)
                nxt = proj_gen(h + 1) if h + 1 < 16 else None

                def hook(nxt=nxt):
                    if nxt is not None:
                        try:
                            next(nxt)
                        except StopIteration:
                            pass
                hh = h % 2
                c = h // 2
                for qt in range(4):
                    t0, nq = TL[qt]
                    units = [(kc, 0, nq, None) for kc in range(18)]
                    e.attend(lambda a0, a1, t0=t0, s_=s_: qh[s_][0:96, t0 + a0:t0 + a1],
                             lambda kc, s_=s_: kh[s_][0:96, kc * 128:(kc + 1) * 128], 96,
                             lambda kc, s_=s_: vA[s_][:, kc, :], units, scale, aT[hh * 64:hh * 64 + 64, c, t0:t0 + nq],
                             [B("mla_qh", s_, qt)], [Bkh], [Bvh], Bo(c, hh, qt), ascr, nq, hook=hook)
                exhaust(nxt)
            e.attn_flush()
            e.set_nA(4)
            S.barrier()
    e.dbg_dump("o", aT[:], "pool")
    with ExitStack() as s3:
        e.out_proj(i, b, aT, lambda k, ti: [Bo(k, 0, ti), Bo(k, 1, ti)], e.mla_wo, s3, [0, 1, 2, 3], "mla")
        S.barrier()


_CACHE = {}


def run_layers(inputs, layers, ncores=NCORES, last_layer_idx=3):
    inp = {k: np.asarray(v) for k, v in inputs.items()}
    shared = host_shared(inp)
    key = (tuple(layers), last_layer_idx)
    if key not in _CACHE:
        _CACHE[key] = build(layers=layers, last_layer_idx=last_layer_idx)
    nc = _CACHE[key]
    in_maps = []
    for c in range(ncores):
        m = dict(shared)
        m.update(host_core(inp, c))
        in_maps.append(m)
    res = run_bass_kernel_spmd(nc, in_maps, core_ids=list(range(ncores)))
    run_layers.dbg = np.asarray(res.results[0]["dbgT"]) if "dbgT" in res.results[0] else None
    outs = [np.asarray(r["outT"]).transpose(0, 2, 1) for r in res.results]
    return np.ascontiguousarray(np.concatenate(outs, 0)).astype(np.float32)


def kernel(**inputs):
    return run_layers(inputs, (0, 1, 2, 3))
```
